# Optimizing a Trainium2 kernel written in Bass

```python
import math
import jax
import jax.numpy as jnp
from jax import lax
import numpy as np

D_MODEL = 2048
BATCH = 8
SEQ = 4096
DEPTH = 1
DEC_BATCH = 8
DEC_SEQ = 16
PAST_LEN = 4096

CHUNK = 64
N_META = 16
MIX_WIDTH = D_MODEL
DN_WIDTH = MIX_WIDTH // 2
DN_HEAD_DIM = 128
DN_HEADS = DN_WIDTH // DN_HEAD_DIM
QKV_DIM = 3 * DN_WIDTH
CONV_W = 4
S5_WIDTH = MIX_WIDTH - DN_WIDTH
S5_GROUP_CH = 16
S5_GROUPS = S5_WIDTH // S5_GROUP_CH
S5_STATE = 64
IN_PROJ_DIM = QKV_DIM + DN_WIDTH + 2 * DN_HEADS + S5_WIDTH
FFN_HIDDEN = ((8 * D_MODEL // 3 + 255) // 256) * 256
DEEPNORM_ALPHA = (2.0 * DEPTH) ** 0.25
DEEPNORM_BETA = (8.0 * DEPTH) ** -0.25
LN_EPS = 1e-5
RMS_EPS = 1e-6

kernel_name = 'hybrid_gdn_s5_streaming_encoder'


def _f32(t):
    return t.astype(jnp.float32)


def layer_norm(x, g, b):
    x = _f32(x)
    mu = jnp.mean(x, axis=-1, keepdims=True)
    xc = x - mu
    var = jnp.mean(xc * xc, axis=-1, keepdims=True)
    return xc * lax.rsqrt(var + LN_EPS) * _f32(g) + _f32(b)


def l2_normalize(x):
    return x * lax.rsqrt(jnp.sum(x * x, axis=-1, keepdims=True) + RMS_EPS)


def pad_front(t, n):
    return jnp.pad(t, ((0, 0), (n, 0)) + ((0, 0),) * (t.ndim - 2))


def gated_delta_rule(q, k, v, g, beta, s0, block):
    bsz, seqlen, nh, _ = q.shape
    dv = v.shape[-1]
    nb = seqlen // block

    def to_blocks(t):
        t = jnp.moveaxis(t, 2, 1)
        return t.reshape((bsz, nh, nb, block) + t.shape[3:])

    q, k, v, g, beta = (to_blocks(t) for t in (q, k, v, g, beta))
    gcum = jnp.cumsum(g, axis=-1)
    pos = jnp.arange(block)
    incl = pos[:, None] >= pos[None, :]
    strict = pos[:, None] > pos[None, :]
    diff = gcum[..., :, None] - gcum[..., None, :]
    decay = jnp.where(incl, jnp.exp(jnp.where(incl, diff, 0.0)), 0.0)
    kb = k * beta[..., None]
    a_mat = jnp.where(strict, jnp.einsum('bhnid,bhnjd->bhnij', kb, k) * decay, 0.0)
    rhs = jnp.concatenate([v * beta[..., None], kb * jnp.exp(gcum)[..., None]], axis=-1)
    sol = lax.linalg.triangular_solve(a_mat, rhs, left_side=True, lower=True, unit_diagonal=True)
    w_val, k_cd = sol[..., :dv], sol[..., dv:]
    qk = jnp.einsum('bhnid,bhnjd->bhnij', q, k) * decay
    q_dec = q * jnp.exp(gcum)[..., None]
    k_dec = k * jnp.exp(gcum[..., -1:] - gcum)[..., None]
    g_tot = jnp.exp(gcum[..., -1])

    def step(s, xs):
        w_val_c, k_cd_c, qk_c, q_dec_c, k_dec_c, g_tot_c = xs
        w = w_val_c - jnp.einsum('bhcd,bhde->bhce', k_cd_c, s)
        o = jnp.einsum('bhcd,bhde->bhce', q_dec_c, s) + jnp.einsum('bhij,bhje->bhie', qk_c, w)
        s = s * g_tot_c[..., None, None] + jnp.einsum('bhcd,bhce->bhde', k_dec_c, w)
        return s, o

    xs = tuple(jnp.moveaxis(t, 2, 0) for t in (w_val, k_cd, qk, q_dec, k_dec, g_tot))
    s_final, o = lax.scan(step, s0, xs)
    o = jnp.moveaxis(o, 0, 2).reshape(bsz, nh, seqlen, dv)
    return jnp.moveaxis(o, 1, 2), s_final


def _complex_affine_combine(e1, e2):
    a1r, a1i, b1r, b1i = e1
    a2r, a2i, b2r, b2i = e2
    return (a2r * a1r - a2i * a1i,
            a2r * a1i + a2i * a1r,
            a2r * b1r - a2i * b1i + b2r,
            a2r * b1i + a2i * b1r + b2i)


def s5_ssm(u, x0_re, x0_im, a_re, a_im, log_dt, b_re, b_im, c_re, c_im, d):
    bsz, seqlen, _ = u.shape
    ug = u.reshape(bsz, seqlen, S5_GROUPS, S5_GROUP_CH)
    dt = jnp.exp(log_dt)[:, None]
    mag = jnp.exp(a_re * dt)
    lam_re = mag * jnp.cos(a_im * dt)
    lam_im = mag * jnp.sin(a_im * dt)
    den = a_re * a_re + a_im * a_im
    nr = lam_re - 1.0
    f_re = (nr * a_re + lam_im * a_im) / den
    f_im = (lam_im * a_re - nr * a_im) / den
    bb_re = f_re[..., None] * b_re - f_im[..., None] * b_im
    bb_im = f_re[..., None] * b_im + f_im[..., None] * b_re
    in_re = jnp.einsum('blgh,gph->blgp', ug, bb_re)
    in_im = jnp.einsum('blgh,gph->blgp', ug, bb_im)
    in_re = in_re.at[:, 0].add(lam_re * x0_re - lam_im * x0_im)
    in_im = in_im.at[:, 0].add(lam_re * x0_im + lam_im * x0_re)
    ar = jnp.broadcast_to(lam_re[None, None], (1, seqlen, S5_GROUPS, S5_STATE))
    ai = jnp.broadcast_to(lam_im[None, None], (1, seqlen, S5_GROUPS, S5_STATE))
    _, _, xr, xi = lax.associative_scan(_complex_affine_combine, (ar, ai, in_re, in_im), axis=1)
    y = jnp.einsum('blgp,ghp->blgh', xr, c_re) - jnp.einsum('blgp,ghp->blgh', xi, c_im)
    y = y.reshape(bsz, seqlen, S5_WIDTH) + d * u
    return y, xr[:, -1], xi[:, -1]


def hybrid_layer(h, conv_prev, s_delta0, s5_re0, s5_im0, n_pad, block,
                 w_in, conv_w, dn_a_log, dn_dt_bias, dn_norm_w,
                 s5_a_re, s5_a_im, s5_log_dt, s5_b_re, s5_b_im, s5_c_re, s5_c_im, s5_d,
                 s5_w_glu, s5_b_glu, w_out, ln1_g, ln1_b, ffn_w_gate, ffn_w_up, ffn_w_down,
                 ln2_g, ln2_b):
    h = _f32(h)
    bsz, seqlen, _ = h.shape
    proj = h @ _f32(w_in)
    qkv_pre, z, a_logit, b_logit, u = jnp.split(
        proj, [QKV_DIM, QKV_DIM + DN_WIDTH, QKV_DIM + DN_WIDTH + DN_HEADS,
               QKV_DIM + DN_WIDTH + 2 * DN_HEADS], axis=-1)

    cat = jnp.concatenate([_f32(conv_prev), qkv_pre], axis=1)
    cw = _f32(conv_w)
    qkv = sum(cat[:, i:i + seqlen] * cw[i] for i in range(CONV_W))
    qkv = jax.nn.silu(qkv)
    new_conv = cat[:, seqlen:]
    q, k, v = jnp.split(qkv, 3, axis=-1)
    q = l2_normalize(q.reshape(bsz, seqlen, DN_HEADS, DN_HEAD_DIM)) * (DN_HEAD_DIM ** -0.5)
    k = l2_normalize(k.reshape(bsz, seqlen, DN_HEADS, DN_HEAD_DIM))
    v = v.reshape(bsz, seqlen, DN_HEADS, DN_HEAD_DIM)
    beta = jax.nn.sigmoid(b_logit)
    g = -jnp.exp(_f32(dn_a_log)) * jax.nn.softplus(a_logit + _f32(dn_dt_bias))
    o, s_delta = gated_delta_rule(pad_front(q, n_pad), pad_front(k, n_pad), pad_front(v, n_pad),
                                  pad_front(g, n_pad), pad_front(beta, n_pad), _f32(s_delta0), block)
    o = o[:, n_pad:]
    o = o * lax.rsqrt(jnp.mean(o * o, axis=-1, keepdims=True) + RMS_EPS) * _f32(dn_norm_w)
    o = o * jax.nn.silu(z.reshape(bsz, seqlen, DN_HEADS, DN_HEAD_DIM))
    o = o.reshape(bsz, seqlen, DN_WIDTH)

    y5, s5_re, s5_im = s5_ssm(u, _f32(s5_re0), _f32(s5_im0), _f32(s5_a_re), _f32(s5_a_im),
                              _f32(s5_log_dt), _f32(s5_b_re), _f32(s5_b_im), _f32(s5_c_re),
                              _f32(s5_c_im), _f32(s5_d))
    y5 = jax.nn.gelu(y5)
    y5 = y5 * jax.nn.sigmoid(y5 @ _f32(s5_w_glu) + _f32(s5_b_glu))

    mix = jnp.concatenate([o, y5], axis=-1) @ _f32(w_out)
    x1 = layer_norm(DEEPNORM_ALPHA * h + mix, ln1_g, ln1_b)
    ffn = (jax.nn.silu(x1 @ _f32(ffn_w_gate)) * (x1 @ _f32(ffn_w_up))) @ _f32(ffn_w_down)
    x2 = layer_norm(DEEPNORM_ALPHA * x1 + ffn, ln2_g, ln2_b)
    return x2, new_conv, s_delta, s5_re, s5_im


def setup_inputs(seed: int = 0) -> dict:
    key = jax.random.key(seed)
    ks = jax.random.split(key, 32)

    def nrm(k, shape, scale):
        return scale * jax.random.normal(k, shape, jnp.float32)

    dt_dn = jnp.exp(jax.random.uniform(ks[12], (DEPTH, DN_HEADS), jnp.float32,
                                       minval=math.log(1e-3), maxval=math.log(1e-1)))
    n_idx = jnp.arange(S5_STATE, dtype=jnp.float32)
    return {
        'x_prompt': nrm(ks[0], (BATCH, SEQ, D_MODEL), 1.0),
        'x_sample': nrm(ks[1], (DEC_BATCH, DEC_SEQ, D_MODEL), 1.0),
        'state_conv_qkv': nrm(ks[2], (DEPTH, DEC_BATCH, CONV_W - 1, QKV_DIM), 1.0),
        'state_delta': nrm(ks[3], (DEPTH, DEC_BATCH, DN_HEADS, DN_HEAD_DIM, DN_HEAD_DIM), DN_HEAD_DIM ** -0.5),
        'state_s5_re': nrm(ks[4], (DEPTH, DEC_BATCH, S5_GROUPS, S5_STATE), 0.1),
        'state_s5_im': nrm(ks[5], (DEPTH, DEC_BATCH, S5_GROUPS, S5_STATE), 0.1),
        'meta_tokens': nrm(ks[6], (N_META, D_MODEL), 1.0),
        'ln_in_g': 1.0 + nrm(ks[7], (D_MODEL,), 0.02),
        'ln_in_b': nrm(ks[8], (D_MODEL,), 0.02),
        'w_in': nrm(ks[9], (DEPTH, D_MODEL, IN_PROJ_DIM), D_MODEL ** -0.5),
        'conv_w': nrm(ks[10], (DEPTH, CONV_W, QKV_DIM), 0.5),
        'dn_a_log': jnp.log(jax.random.uniform(ks[11], (DEPTH, DN_HEADS), jnp.float32, minval=1.0, maxval=16.0)),
        'dn_dt_bias': dt_dn + jnp.log(-jnp.expm1(-dt_dn)),
        'dn_norm_w': 1.0 + nrm(ks[13], (DEPTH, DN_HEAD_DIM), 0.02),
        's5_a_re': -0.5 + nrm(ks[14], (DEPTH, S5_GROUPS, S5_STATE), 0.01),
        's5_a_im': math.pi * n_idx + nrm(ks[15], (DEPTH, S5_GROUPS, S5_STATE), 0.01),
        's5_log_dt': jax.random.uniform(ks[16], (DEPTH, S5_GROUPS), jnp.float32,
                                        minval=math.log(1e-3), maxval=math.log(1e-1)),
        's5_b_re': nrm(ks[17], (DEPTH, S5_GROUPS, S5_STATE, S5_GROUP_CH), (2 * S5_GROUP_CH) ** -0.5),
        's5_b_im': nrm(ks[18], (DEPTH, S5_GROUPS, S5_STATE, S5_GROUP_CH), (2 * S5_GROUP_CH) ** -0.5),
        's5_c_re': nrm(ks[19], (DEPTH, S5_GROUPS, S5_GROUP_CH, S5_STATE), S5_STATE ** -0.5),
        's5_c_im': nrm(ks[20], (DEPTH, S5_GROUPS, S5_GROUP_CH, S5_STATE), S5_STATE ** -0.5),
        's5_d': nrm(ks[21], (DEPTH, S5_WIDTH), 1.0),
        's5_w_glu': nrm(ks[22], (DEPTH, S5_WIDTH, S5_WIDTH), S5_WIDTH ** -0.5),
        's5_b_glu': nrm(ks[23], (DEPTH, S5_WIDTH), 0.01),
        'w_out': nrm(ks[24], (DEPTH, MIX_WIDTH, D_MODEL), DEEPNORM_BETA * MIX_WIDTH ** -0.5),
        'ln1_g': 1.0 + nrm(ks[25], (DEPTH, D_MODEL), 0.02),
        'ln1_b': nrm(ks[26], (DEPTH, D_MODEL), 0.02),
        'ffn_w_gate': nrm(ks[27], (DEPTH, D_MODEL, FFN_HIDDEN), D_MODEL ** -0.5),
        'ffn_w_up': nrm(ks[28], (DEPTH, D_MODEL, FFN_HIDDEN), D_MODEL ** -0.5),
        'ffn_w_down': nrm(ks[29], (DEPTH, FFN_HIDDEN, D_MODEL), DEEPNORM_BETA * FFN_HIDDEN ** -0.5),
        'ln2_g': 1.0 + nrm(ks[30], (DEPTH, D_MODEL), 0.02),
        'ln2_b': nrm(ks[31], (DEPTH, D_MODEL), 0.02),
    }


def reference(x_prompt, x_sample, state_conv_qkv, state_delta, state_s5_re, state_s5_im,
              meta_tokens, ln_in_g, ln_in_b, w_in, conv_w, dn_a_log, dn_dt_bias, dn_norm_w,
              s5_a_re, s5_a_im, s5_log_dt, s5_b_re, s5_b_im, s5_c_re, s5_c_im, s5_d,
              s5_w_glu, s5_b_glu, w_out, ln1_g, ln1_b, ffn_w_gate, ffn_w_up, ffn_w_down,
              ln2_g, ln2_b):
    bp = x_prompt.shape[0]
    meta = jnp.broadcast_to(meta_tokens[None].astype(x_prompt.dtype), (bp, N_META, D_MODEL))
    hp = layer_norm(jnp.concatenate([meta, x_prompt], axis=1), ln_in_g, ln_in_b)
    hs = layer_norm(x_sample, ln_in_g, ln_in_b)
    n_pad = (-N_META) % CHUNK
    dec_block = x_sample.shape[1]
    conv_p, delta_p, re_p, im_p = [], [], [], []
    conv_s, delta_s, re_s, im_s = [], [], [], []
    for l in range(DEPTH):
        lw = (w_in[l], conv_w[l], dn_a_log[l], dn_dt_bias[l], dn_norm_w[l],
              s5_a_re[l], s5_a_im[l], s5_log_dt[l], s5_b_re[l], s5_b_im[l], s5_c_re[l], s5_c_im[l],
              s5_d[l], s5_w_glu[l], s5_b_glu[l], w_out[l], ln1_g[l], ln1_b[l],
              ffn_w_gate[l], ffn_w_up[l], ffn_w_down[l], ln2_g[l], ln2_b[l])
        hp, c_p, d_p, r_p, i_p = hybrid_layer(
            hp,
            jnp.zeros((bp, CONV_W - 1, QKV_DIM), jnp.float32),
            jnp.zeros((bp, DN_HEADS, DN_HEAD_DIM, DN_HEAD_DIM), jnp.float32),
            jnp.zeros((bp, S5_GROUPS, S5_STATE), jnp.float32),
            jnp.zeros((bp, S5_GROUPS, S5_STATE), jnp.float32),
            n_pad, CHUNK, *lw)
        hs, c_s, d_s, r_s, i_s = hybrid_layer(
            hs, state_conv_qkv[l], state_delta[l], state_s5_re[l], state_s5_im[l],
            0, dec_block, *lw)
        conv_p.append(c_p); delta_p.append(d_p); re_p.append(r_p); im_p.append(i_p)
        conv_s.append(c_s); delta_s.append(d_s); re_s.append(r_s); im_s.append(i_s)
    y_prompt = hp[:, N_META:].astype(x_prompt.dtype)
    y_sample = hs.astype(x_sample.dtype)
    new_conv_prompt = jnp.stack(conv_p, axis=0).astype(state_conv_qkv.dtype)
    new_delta_prompt = jnp.stack(delta_p, axis=0).astype(state_delta.dtype)
    new_s5_re_prompt = jnp.stack(re_p, axis=0).astype(state_s5_re.dtype)
    new_s5_im_prompt = jnp.stack(im_p, axis=0).astype(state_s5_im.dtype)
    new_conv_sample = jnp.stack(conv_s, axis=0).astype(state_conv_qkv.dtype)
    new_delta_sample = jnp.stack(delta_s, axis=0).astype(state_delta.dtype)
    new_s5_re_sample = jnp.stack(re_s, axis=0).astype(state_s5_re.dtype)
    new_s5_im_sample = jnp.stack(im_s, axis=0).astype(state_s5_im.dtype)
    return (y_prompt, y_sample, new_conv_prompt, new_delta_prompt, new_s5_re_prompt, new_s5_im_prompt,
            new_conv_sample, new_delta_sample, new_s5_re_sample, new_s5_im_sample)
```

```python
import math
from contextlib import ExitStack

import numpy as np

import concourse.bass as bass
import concourse.mybir as mybir
from concourse.bass_utils import run_bass_kernel_spmd

F32 = mybir.dt.float32
BF16 = mybir.dt.bfloat16
ALU = mybir.AluOpType
AF = mybir.ActivationFunctionType
AX = mybir.AxisListType

D = 2048
KC = 16
H = 8
QKV = 3072
FF = 5632
FKC = 44
NMETA = 16
ALPHA = 2.0 ** 0.25
LN_EPS = 1e-5
RMS_EPS = 1e-6
PAD = 258
NEG = -30000.0
MAGIC = 12582912.0
TWO_PI = 2.0 * math.pi


class Buf:
    __slots__ = ("name", "w", "r", "excl")

    def __init__(self, name):
        self.name = name
        self.w = None
        self.r = []
        self.excl = False


class Op:
    __slots__ = ("eng", "fn", "lane", "seq", "waits", "signal", "val")

    def __init__(self, eng, fn, lane):
        self.eng = eng
        self.fn = fn
        self.lane = lane
        self.seq = 0
        self.waits = []
        self.signal = False
        self.val = 0


EPOCH = 20000


class Sched:
    ENGS = ("pe", "act", "dve", "pool", "sp")

    def __init__(self):
        self.ops = {e: [] for e in self.ENGS}
        self.cnt = {e: 0 for e in self.ENGS}
        self.waited = {e: {} for e in self.ENGS}
        self.lane_cnt = {}
        self.lane_last = {}

    def add(self, eng, fn, reads=(), writes=(), lane=None):
        op = Op(eng, fn, lane)
        deps = []
        for b in reads:
            if b.w is not None:
                deps.append(b.w)
            if b.excl:
                deps.extend(r_ for r_ in b.r if r_.eng != eng)
        for b in writes:
            if b.w is not None:
                deps.append(b.w)
            deps.extend(b.r)
        if lane is not None:
            if lane in self.lane_last:
                deps.append(self.lane_last[lane])
            self.lane_cnt[lane] = self.lane_cnt.get(lane, 0) + 1
            op.seq = self.lane_cnt[lane]
            self.lane_last[lane] = op
        else:
            self.cnt[eng] += 1
            op.seq = self.cnt[eng]
        need = {}
        for d in deps:
            if d is op:
                continue
            key = ("L", d.lane) if d.lane is not None else ("E", d.eng)
            if d.lane is None and d.eng == eng and eng == "pe":
                continue
            if key not in need or d.seq > need[key].seq:
                need[key] = d
        for key, d in need.items():
            if self.waited[eng].get(key, 0) >= d.seq:
                continue
            self.waited[eng][key] = d.seq
            d.signal = True
            op.waits.append(d)
        for b in reads:
            b.r.append(op)
        for b in writes:
            b.w = op
            b.r = []
        self.ops[eng].append(op)
        return op

    def emit(self, nc, es):
        nsem = {}
        for e in self.ENGS:
            c = 0
            for op in self.ops[e]:
                if op.lane is None and op.signal:
                    c += 1
                    op.val = c
            nsem[e] = max(1, (c + EPOCH - 1) // EPOCH)
        esem = {e: [es.enter_context(nc.semaphore("s_%s_%d" % (e, i))) for i in range(nsem[e])]
                for e in self.ENGS}
        lsem = {l: es.enter_context(nc.semaphore("l_%s" % l)) for l in self.lane_cnt}

        def semval(d):
            if d.lane is not None:
                return lsem[d.lane], 16 * d.seq
            i = (d.val - 1) // EPOCH
            return esem[d.eng][i], d.val - i * EPOCH

        def run(engname, eobj):
            for op in self.ops[engname]:
                for d in op.waits:
                    s, v = semval(d)
                    eobj.wait_ge(s, v)
                ins = op.fn(eobj)
                if op.lane is not None:
                    ins.then_inc(lsem[op.lane], 16)
                elif op.signal:
                    s, _ = semval(op)
                    ins.then_inc(s, 1)

        with nc.Block() as block:
            @block.tensor
            def _(e):
                run("pe", e)

            @block.scalar
            def _(e):
                run("act", e)

            @block.vector
            def _(e):
                run("dve", e)

            @block.gpsimd
            def _(e):
                run("pool", e)

            @block.sync
            def _(e):
                run("sp", e)


class T:
    def __init__(self, t, n=1, name="t"):
        self.t = t
        self.bs = [Buf("%s%d" % (name, i)) for i in range(n)]

    @property
    def b(self):
        return self.bs[0]


class Rot:
    def __init__(self, items):
        self.items = items
        self.i = 0

    def get(self):
        x = self.items[self.i % len(self.items)]
        self.i += 1
        return x


def build(nblk):
    seq = 128 * nblk
    nc = bass.Bass("TRN2", target_bir_lowering=False)
    S = Sched()

    def din(name, shape, dt=F32):
        return nc.dram_tensor(name, list(shape), dt, kind="ExternalInput").ap()

    def dout(name, shape):
        return nc.dram_tensor(name, list(shape), F32, kind="ExternalOutput").ap()

    x_p = din("x_p", [seq, D])
    x_s = din("x_s", [16, D])
    st_conv = din("st_conv", [128, 24, 3])
    st_delta = din("st_delta", [H, 128, 128])
    st_re = din("st_re", [128, 32])
    st_im = din("st_im", [128, 32])
    meta = din("meta", [NMETA, D])
    lnp = din("lnp", [6, D])
    w_in_s = din("w_in_s", [16, 128, 4096])
    w_in_z = din("w_in_z", [4, 128, 4096])
    w_in_ab = din("w_in_ab", [128, KC * 16])
    cw_d = din("cw", [128, 24 * 4])
    dnp = din("dnp", [3, 128])
    s5a = din("s5a", [3, 64, 64])
    bpadT = din("bpadT", [2, 32, 128, 128])
    cpad = din("cpad", [2, 32, 128, 128])
    s5d = din("s5d", [128, 8])
    bglu = din("bglu", [128, 8])
    w_glu = din("w_glu", [2, 128, 4096])
    w_out = din("w_out", [8, 128, 4096])
    w_gate = din("w_gate", [22, 128, 4096])
    w_up = din("w_up", [22, 128, 4096])
    w_down = din("w_down", [8, 4, 128, 11 * 256])
    cmat = din("cmat", [128, 5 * 128])

    y_p = dout("y_p", [seq, D])
    y_s = dout("y_s", [16, D])
    o_conv_p = dout("o_conv_p", [128, 24, 3])
    o_delta_p = dout("o_delta_p", [H, 128, 128])
    o_re_p = dout("o_re_p", [128, 32])
    o_im_p = dout("o_im_p", [128, 32])
    o_conv_s = dout("o_conv_s", [128, 24, 3])
    o_delta_s = dout("o_delta_s", [H, 128, 128])
    o_re_s = dout("o_re_s", [128, 32])
    o_im_s = dout("o_im_s", [128, 32])
    s5scr = nc.dram_tensor("s5scr", [4, 64, 64], F32, kind="Internal").ap()
    s5w2 = nc.dram_tensor("s5w2", [32, 128, 4096], BF16, kind="Internal").ap()
    kscr = nc.dram_tensor("kscr", [8, 128, 1024], BF16, kind="Internal").ap()
    def dscr(name, shape):
        return nc.dram_tensor(name, list(shape), BF16, kind="Internal").ap()
    c_in_s = dscr("c_in_s", [16, 128, 4096])
    c_in_z = dscr("c_in_z", [4, 128, 4096])
    c_glu = dscr("c_glu", [2, 128, 4096])
    c_out = dscr("c_out", [8, 128, 4096])
    c_gate = dscr("c_gate", [22, 128, 4096])
    c_up = dscr("c_up", [22, 128, 4096])
    c_down = dscr("c_down", [8, 4, 128, 11 * 256])
    wsbuf = Buf("wscratch")
    outbuf = Buf("dram_out")
    scrbuf = Buf("s5scr")
    swbuf = Buf("s5w")

    with ExitStack() as es:
        def sb(name, shape, dt=F32, n=1):
            return T(es.enter_context(nc.sbuf_tensor("sb_" + name, list(shape), dt)), n, name)

        def psum(name, shape, dt=F32, n=1):
            t_ = T(es.enter_context(nc.psum_tensor("ps_" + name, list(shape), dt)), n, name)
            for b_ in t_.bs:
                b_.excl = True
            return t_

        cm = sb("cm", [128, 5, 128])
        identb = sb("identb", [128, 128], BF16)
        ident = cm.t[:, 0, :]
        ones = cm.t[:, 1, :]
        negmT = cm.t[:, 2, :]
        strict = cm.t[:, 3, :]
        uincl = cm.t[:, 4, :]
        S.add("sp", lambda e: e.dma_start(out=cm.t[:].rearrange("p a b -> p (a b)"), in_=cmat[:, :]),
              writes=[cm.b], lane="c0")
        S.add("act", lambda e: e.activation(out=identb.t[:], in_=ident, func=AF.Copy),
              reads=[cm.b], writes=[identb.b])
        dn = sb("dn", [128, 3, 128])
        S.add("sp", lambda e: e.dma_start(out=dn.t[:], in_=dnp.partition_broadcast(128)),
              writes=[dn.b], lane="c0")
        negA = sb("negA", [128, 8])
        S.add("act", lambda e: e.activation(out=negA.t[:], in_=dn.t[:, 0, 0:8], func=AF.Exp),
              reads=[dn.b], writes=[negA.b])
        S.add("dve", lambda e: e.tensor_scalar_mul(out=negA.t[:], in0=negA.t[:], scalar1=-1.0),
              reads=[negA.b], writes=[negA.b])
        dtb = dn.t[:, 1, 0:8]
        normw = dn.t[:, 2, :]
        cw = sb("cw", [128, 24, 4])
        S.add("sp", lambda e: e.dma_start(out=cw.t[:].rearrange("p a b -> p (a b)"), in_=cw_d[:, :]),
              writes=[cw.b], lane="c0")
        d5 = sb("d5", [128, 8])
        hbg = sb("hbg", [128, 8])
        S.add("sp", lambda e: e.dma_start(out=d5.t[:], in_=s5d[:, :]), writes=[d5.b], lane="c0")
        S.add("sp", lambda e: e.dma_start(out=hbg.t[:], in_=bglu[:, :]), writes=[hbg.b], lane="c0")
        S.add("dve", lambda e: e.tensor_scalar_mul(out=hbg.t[:], in0=hbg.t[:], scalar1=0.5),
              reads=[hbg.b], writes=[hbg.b])
        mhalf = sb("mhalf", [128, 8])
        S.add("pool", lambda e: e.memset(mhalf.t[:], -0.5), writes=[mhalf.b])

        mm = [psum("mm%d" % i, [128, 512]) for i in range(4)]
        tp = [psum("tp%d" % i, [128, 8, 128], BF16) for i in range(2)]
        gp_t = [psum("gp%d" % i, [128, 4, 128], F32, 4) for i in range(2)]
        mmr = Rot(mm)
        tpr = Rot(tp)
        gpr = Rot([(gp_t[i].t[:, j, :], gp_t[i].bs[0]) for j in range(4) for i in range(2)])

        a3 = sb("a3", [64, 3, 64])
        S.add("sp", lambda e: e.dma_start(out=a3.t[:], in_=s5a.rearrange("a g p -> g a p")),
              writes=[a3.b], lane="c0")
        s5t = sb("s5t", [64, 12, 64])
        tb = s5t.b

        def s5op(eng, fn):
            S.add(eng, fn, reads=[a3.b, tb], writes=[tb])
        st = s5t.t
        s5op("act", lambda e: e.activation(out=st[:, 0, :], in_=a3.t[:, 2, :], func=AF.Exp))
        s5op("dve", lambda e: e.tensor_tensor(out=st[:, 1, :], in0=a3.t[:, 0, :], in1=st[:, 0, :], op=ALU.mult))
        s5op("dve", lambda e: e.tensor_tensor(out=st[:, 2, :], in0=a3.t[:, 1, :], in1=st[:, 0, :], op=ALU.mult))
        s5op("act", lambda e: e.activation(out=st[:, 3, :], in_=st[:, 1, :], func=AF.Exp))
        for (dst, shift) in ((5, 0.0), (6, math.pi / 2)):
            s5op("dve", lambda e, shift=shift: e.tensor_scalar(out=st[:, 11, :], in0=st[:, 2, :], scalar1=shift,
                                                              scalar2=None, op0=ALU.add))
            s5op("dve", lambda e: e.tensor_scalar(out=st[:, 4, :], in0=st[:, 11, :], scalar1=1.0 / TWO_PI,
                                                  scalar2=MAGIC, op0=ALU.mult, op1=ALU.add))
            s5op("dve", lambda e: e.tensor_scalar(out=st[:, 4, :], in0=st[:, 4, :], scalar1=-MAGIC,
                                                  scalar2=-TWO_PI, op0=ALU.add, op1=ALU.mult))
            s5op("dve", lambda e: e.tensor_tensor(out=st[:, 4, :], in0=st[:, 4, :], in1=st[:, 11, :], op=ALU.add))
            s5op("act", lambda e, dst=dst: e.activation(out=st[:, dst, :], in_=st[:, 4, :], func=AF.Sin))
        s5op("dve", lambda e: e.tensor_tensor(out=st[:, 7, :], in0=st[:, 3, :], in1=st[:, 6, :], op=ALU.mult))
        s5op("dve", lambda e: e.tensor_tensor(out=st[:, 8, :], in0=st[:, 3, :], in1=st[:, 5, :], op=ALU.mult))
        s5op("dve", lambda e: e.tensor_tensor(out=st[:, 11, :], in0=a3.t[:, 0, :], in1=a3.t[:, 0, :], op=ALU.mult))
        s5op("dve", lambda e: e.tensor_tensor(out=st[:, 0, :], in0=a3.t[:, 1, :], in1=a3.t[:, 1, :], op=ALU.mult))
        s5op("dve", lambda e: e.tensor_tensor(out=st[:, 11, :], in0=st[:, 11, :], in1=st[:, 0, :], op=ALU.add))
        s5op("dve", lambda e: e.reciprocal(out=st[:, 11, :], in_=st[:, 11, :]))
        s5op("dve", lambda e: e.tensor_scalar(out=st[:, 4, :], in0=st[:, 7, :], scalar1=-1.0, scalar2=None, op0=ALU.add))
        s5op("dve", lambda e: e.tensor_tensor(out=st[:, 9, :], in0=st[:, 4, :], in1=a3.t[:, 0, :], op=ALU.mult))
        s5op("dve", lambda e: e.tensor_tensor(out=st[:, 0, :], in0=st[:, 8, :], in1=a3.t[:, 1, :], op=ALU.mult))
        s5op("dve", lambda e: e.tensor_tensor(out=st[:, 9, :], in0=st[:, 9, :], in1=st[:, 0, :], op=ALU.add))
        s5op("dve", lambda e: e.tensor_tensor(out=st[:, 9, :], in0=st[:, 9, :], in1=st[:, 11, :], op=ALU.mult))
        s5op("dve", lambda e: e.tensor_tensor(out=st[:, 10, :], in0=st[:, 8, :], in1=a3.t[:, 0, :], op=ALU.mult))
        s5op("dve", lambda e: e.tensor_tensor(out=st[:, 0, :], in0=st[:, 4, :], in1=a3.t[:, 1, :], op=ALU.mult))
        s5op("dve", lambda e: e.tensor_tensor(out=st[:, 10, :], in0=st[:, 10, :], in1=st[:, 0, :], op=ALU.subtract))
        s5op("dve", lambda e: e.tensor_tensor(out=st[:, 10, :], in0=st[:, 10, :], in1=st[:, 11, :], op=ALU.mult))
        S.add("sp", lambda e: e.dma_start(out=s5scr.rearrange("a g p -> g a p"), in_=st[:, 7:11, :]),
              reads=[tb], writes=[scrbuf], lane="c0")
        Sst = sb("Sst", [128, H, 128], F32, H)
        Sbf = sb("Sbf", [128, H, 128], BF16, H)
        hist = sb("hist", [128, 24, 3])
        xprev = sb("xprev", [128, 2, 32])

        NT = 256
        hb4 = [sb("h%d" % i, [128, D]) for i in range(4)]
        hb16 = sb("hb16", [128, D], BF16)
        lnbc = sb("lnbc", [128, 2, D])
        actTs = [sb("actT%d" % i, [128, KC, NT], BF16) for i in range(2)]
        mixT = sb("mixT", [128, KC, NT], BF16)
        big = sb("big", [128, FKC, NT], BF16, FKC)
        Wsl = [sb("W%d" % i, [128, 4096], BF16) for i in range(4)]
        wab = sb("wab", [128, KC, 16], BF16)
        wr = Rot(Wsl)
        wlane = Rot(["w0", "w1", "w2", "w3"])
        stats = sb("stats", [128, 4, 6])
        mv = sb("mv", [128, 4])
        cbufs = Rot([sb("cb%d" % i, [128, NT + 3]) for i in range(2)])
        caccs = Rot([sb("ca%d" % i, [128, NT]) for i in range(2)])
        absb = sb("absb", [128, 2, 16], F32, 2)
        gsm = sb("gsm", [128, 12, 8])
        gtmp = Rot([sb("gt%d" % i, [128, 128]) for i in range(24)])
        gtb = Rot([sb("gb%d" % i, [128, 128], BF16) for i in range(8)])
        ktm = sb("ktm", [128, H, 128], BF16, H)
        kdtm = sb("kdtm", [128, H, 128], BF16, H)
        vtm = sb("vtm", [128, H, 128], BF16, H)
        kTn = sb("kTn", [128, H, 128], BF16, H)
        qTn = sb("qTn", [128, H, 128], BF16, H)
        otm = sb("otm", [128, H, 128], BF16, H)
        ontm = sb("ontm", [128, H, 128], BF16, H)
        nrm = sb("nrm", [128, 3, 8], F32, 3)
        gz = [sb("gz%d" % i, [128, 1024], BF16) for i in range(2)]
        xs = [sb("xs%d" % i, [128, 32, 33]) for i in range(2)]
        xq = [sb("xq%d" % i, [128, 4, 32], F32, 4) for i in range(1)]
        xb = sb("xb", [128, 2, 32, 32], BF16)
        s5ws = [sb("s5ws%d" % i, [128, 16, 128], BF16) for i in range(2)]
        s5ks = [sb("s5ks%d" % i, [128, 8, 128], BF16) for i in range(1)]
        sgt = Rot([sb("sg%d" % i, [128, NT]) for i in range(2)])

        pwv = lnbc.t[:].rearrange("p a d -> p (a d)")[:, 0:4 * 17 * 32].rearrange("p (c k j) -> p c k j", c=4, k=17)
        pwb = lnbc.b
        ct = sb("ct", [128, 2, 32])
        l8 = sb("l8", [128, 1, 2, 32])
        with nc.allow_non_contiguous_dma(reason="tiny s5 param reshuffle"):
            for c in range(2):
                S.add("sp", lambda e, c=c: e.dma_start(
                    out=pwv[:, c, 1, :], in_=s5scr[c].rearrange("(j g) p -> (g p) j", g=2)),
                    reads=[scrbuf], writes=[pwb], lane="c0")
                S.add("sp", lambda e, c=c: e.dma_start(
                    out=pwv[:, c, 9, :], in_=s5scr[2 + c].rearrange("(j g) p -> (g p) j", g=2)),
                    reads=[scrbuf], writes=[pwb], lane="c0")
        S.add("dve", lambda e: e.memset(pwv[:, 0, 0, :], 1.0), writes=[pwb])
        S.add("dve", lambda e: e.memset(pwv[:, 1, 0, :], 0.0), writes=[pwb])

        def cmul(dr, di, ar, ai, br, bi, db, sbs):
            rd = sbs + [ct.b]
            S.add("dve", lambda e: e.tensor_tensor(out=ct.t[:, 0, :], in0=ar, in1=br, op=ALU.mult), reads=rd, writes=[ct.b])
            S.add("dve", lambda e: e.tensor_tensor(out=ct.t[:, 1, :], in0=ai, in1=bi, op=ALU.mult), reads=rd, writes=[ct.b])
            S.add("dve", lambda e: e.tensor_tensor(out=dr, in0=ct.t[:, 0, :], in1=ct.t[:, 1, :], op=ALU.subtract), reads=rd + [db], writes=[db])
            S.add("dve", lambda e: e.tensor_tensor(out=ct.t[:, 0, :], in0=ar, in1=bi, op=ALU.mult), reads=rd + [db], writes=[ct.b])
            S.add("dve", lambda e: e.tensor_tensor(out=ct.t[:, 1, :], in0=ai, in1=br, op=ALU.mult), reads=rd + [db], writes=[ct.b])
            S.add("dve", lambda e: e.tensor_tensor(out=di, in0=ct.t[:, 0, :], in1=ct.t[:, 1, :], op=ALU.add), reads=rd + [db], writes=[db])
        for k in range(1, 8):
            cmul(pwv[:, 0, k + 1, :], pwv[:, 1, k + 1, :], pwv[:, 0, k, :], pwv[:, 1, k, :], pwv[:, 0, 1, :], pwv[:, 1, 1, :], pwb, [pwb])
            cmul(pwv[:, 0, 9 + k, :], pwv[:, 1, 9 + k, :], pwv[:, 0, k, :], pwv[:, 1, k, :], pwv[:, 0, 9, :], pwv[:, 1, 9, :], pwb, [pwb])
        S.add("dve", lambda e: e.tensor_scalar_mul(out=pwv[:, 2:4, :, :], in0=pwv[:, 0:2, :, :], scalar1=-1.0), reads=[pwb], writes=[pwb])
        S.add("dve", lambda e: e.tensor_copy(out=l8.t[:, 0, :, :], in_=pwv[:, 0:2, 8, :]), reads=[pwb], writes=[l8.b])
        hv0 = hb4[0].t[:, :]
        hv1 = hb4[1].t[:, :]
        ldt = [T(hv1[:, i * 512:(i + 1) * 512].rearrange("p (a c) -> p a c", a=4), 1, "ld%d" % i) for i in range(2)]
        ncit = [T(hv1[:, 1024 + i * 128:1024 + (i + 1) * 128], 1, "nci%d" % i) for i in range(2)]
        wtt = Rot([T(hv1[:, 1280 + i * 128:1280 + (i + 1) * 128], 1, "wt%d" % i) for i in range(6)] + [T(hv0[:, 1024 + i * 128:1024 + (i + 1) * 128], 1, "wu%d" % i) for i in range(8)])
        kacc = T(hv0[:, 0:1024].rearrange("p (a c) -> p a c", a=8), 1, "kacc")
        kbf = T(hb16.t[:, 0:1024].rearrange("p (a c) -> p a c", a=8), 1, "kbf")
        spr = Rot([(gp_t[0].t[:, 0, :], gp_t[0].bs[0]), (mm[2].t[:, 0:128], mm[2].b), (gp_t[1].t[:, 0, :], gp_t[1].bs[0]), (mm[3].t[:, 0:128], mm[3].b)])
        for j in range(32):
            ld = ldt[j % 2]
            nci = ncit[j % 2]
            wst = s5ws[0]
            vst = s5ws[1]
            S.add("sp", lambda e, ld=ld, j=j: e.dma_start(out=ld.t[:, 0:2, :], in_=bpadT[:, j].rearrange("a p c -> p a c")),
                  writes=[ld.b], lane="b%d" % (j % 2))
            S.add("sp", lambda e, ld=ld, j=j: e.dma_start(out=ld.t[:, 2:4, :], in_=cpad[:, j].rearrange("a p c -> p a c")),
                  writes=[ld.b], lane="b%d" % (j % 2))
            S.add("act", lambda e, ld=ld, nci=nci: e.activation(out=nci.t[:, :], in_=ld.t[:, 3, :], func=AF.Copy, scale=-1.0),
                  reads=[ld.b], writes=[nci.b])
            Br, Bi, Cr, Ci = ld.t[:, 0, :], ld.t[:, 1, :], ld.t[:, 2, :], ld.t[:, 3, :]
            for k in range(8):
                gr, gi, ngi = pwv[:, 0, 9 + k, j:j + 1], pwv[:, 1, 9 + k, j:j + 1], pwv[:, 3, 9 + k, j:j + 1]
                wtr = wtt.get()
                wti = wtt.get()
                S.add("act", lambda e, wtr=wtr, Br=Br, gr=gr: e.activation(out=wtr.t[:, :], in_=Br, func=AF.Copy, scale=gr),
                      reads=[ld.b, pwb], writes=[wtr.b])
                S.add("dve", lambda e, wtr=wtr, Bi=Bi, ngi=ngi: e.scalar_tensor_tensor(out=wtr.t[:, :], in0=Bi, scalar=ngi, in1=wtr.t[:, :], op0=ALU.mult, op1=ALU.add),
                      reads=[ld.b, pwb, wtr.b], writes=[wtr.b])
                S.add("act", lambda e, wti=wti, Br=Br, gi=gi: e.activation(out=wti.t[:, :], in_=Br, func=AF.Copy, scale=gi),
                      reads=[ld.b, pwb], writes=[wti.b])
                S.add("dve", lambda e, wti=wti, Bi=Bi, gr=gr: e.scalar_tensor_tensor(out=wti.t[:, :], in0=Bi, scalar=gr, in1=wti.t[:, :], op0=ALU.mult, op1=ALU.add),
                      reads=[ld.b, pwb, wti.b], writes=[wti.b])
                kp = mm[k // 4]
                S.add("pe", lambda e, kp=kp, k=k, wtr=wtr, Cr=Cr: e.matmul(kp.t[:, (k % 4) * 128:(k % 4 + 1) * 128], lhsT=wtr.t[:, :], rhs=Cr, start=True, stop=False),
                      reads=[wtr.b, ld.b], writes=[kp.b])
                S.add("pe", lambda e, kp=kp, k=k, wti=wti, nci=nci: e.matmul(kp.t[:, (k % 4) * 128:(k % 4 + 1) * 128], lhsT=wti.t[:, :], rhs=nci.t[:, :], start=False, stop=True),
                      reads=[wti.b, nci.b], writes=[kp.b])
                sp_ = 7 - k
                for c, wsrc in ((0, wtr), (1, wti)):
                    pt_, ptb = spr.get()
                    S.add("pe", lambda e, pt_=pt_, wsrc=wsrc: e.transpose(out=pt_[:, :], in_=wsrc.t[:, :], identity=ident),
                          reads=[wsrc.b, cm.b], writes=[ptb])
                    S.add("act", lambda e, pt_=pt_, sp_=sp_, c=c, wst=wst: e.activation(out=wst.t[:, sp_ * 2 + c, :], in_=pt_[:, :], func=AF.Copy),
                          reads=[ptb], writes=[wst.b])
                ar, ai, nar, nai = pwv[:, 0, k + 1, j:j + 1], pwv[:, 1, k + 1, j:j + 1], pwv[:, 2, k + 1, j:j + 1], pwv[:, 3, k + 1, j:j + 1]
                v1 = wtt.get()
                S.add("act", lambda e, v1=v1, Cr=Cr, ar=ar: e.activation(out=v1.t[:, :], in_=Cr, func=AF.Copy, scale=ar),
                      reads=[ld.b, pwb], writes=[v1.b])
                S.add("dve", lambda e, v1=v1, Ci=Ci, nai=nai, k=k, vst=vst: e.scalar_tensor_tensor(out=vst.t[:, k * 2, :], in0=Ci, scalar=nai, in1=v1.t[:, :], op0=ALU.mult, op1=ALU.add),
                      reads=[ld.b, pwb, v1.b], writes=[vst.b])
                v2 = wtt.get()
                S.add("act", lambda e, v2=v2, Cr=Cr, nai=nai: e.activation(out=v2.t[:, :], in_=Cr, func=AF.Copy, scale=nai),
                      reads=[ld.b, pwb], writes=[v2.b])
                S.add("dve", lambda e, v2=v2, Ci=Ci, nar=nar, k=k, vst=vst: e.scalar_tensor_tensor(out=vst.t[:, k * 2 + 1, :], in0=Ci, scalar=nar, in1=v2.t[:, :], op0=ALU.mult, op1=ALU.add),
                      reads=[ld.b, pwb, v2.b], writes=[vst.b])
            for half in range(2):
                kp = mm[half]
                if j % 4 == 0:
                    S.add("act", lambda e, kp=kp, half=half: e.activation(out=kacc.t[:, half * 4:half * 4 + 4, :], in_=kp.t[:, :].rearrange("p (a c) -> p a c", a=4), func=AF.Copy),
                          reads=[kp.b], writes=[kacc.b])
                else:
                    S.add("dve", lambda e, kp=kp, half=half: e.tensor_tensor(out=kacc.t[:, half * 4:half * 4 + 4, :], in0=kp.t[:, :].rearrange("p (a c) -> p a c", a=4),
                                                                          in1=kacc.t[:, half * 4:half * 4 + 4, :], op=ALU.add),
                          reads=[kp.b, kacc.b], writes=[kacc.b])
            if j % 4 == 3:
                S.add("act", lambda e: e.activation(out=kbf.t[:, :, :], in_=kacc.t[:, :, :], func=AF.Copy), reads=[kacc.b], writes=[kbf.b])
                S.add("sp", lambda e, j=j: e.dma_start(out=kscr[j // 4], in_=kbf.t[:, :, :].rearrange("p a c -> p (a c)")),
                      reads=[kbf.b], writes=[swbuf], lane="b%d" % (j % 2))
            S.add("sp", lambda e, j=j, wst=wst: e.dma_start(out=s5w2[j, :, 0:2048], in_=wst.t[:].rearrange("p a c -> p (a c)")),
                  reads=[wst.b], writes=[swbuf], lane="b%d" % (j % 2))
            S.add("sp", lambda e, j=j, vst=vst: e.dma_start(out=s5w2[j, :, 2048:4096], in_=vst.t[:].rearrange("p a c -> p (a c)")),
                  reads=[vst.b], writes=[swbuf], lane="b%d" % (j % 2))
        S.add("pool", lambda e: e.memset(ct.t[:], 0.0),
              reads=[b_.b for b_ in ldt + ncit + wtt.items + [kacc, kbf]] + [pwb],
              writes=[hb4[0].b, hb4[1].b, hb16.b, lnbc.b, ct.b])
        def load_w(src_ap, nel):
            w = wr.get()
            S.add("sp", lambda e, w=w, src_ap=src_ap, nel=nel: e.dma_start(out=w.t[:, 0:nel], in_=src_ap),
                  reads=[wsbuf], writes=[w.b], lane=wlane.get())
            return w

        def convert_w(src_ap, dst_ap, nel):
            w = wr.get()
            S.add("pool", lambda e, w=w: e.dma_start(out=w.t[:, 0:nel], in_=src_ap), writes=[w.b], lane=cwl.get())
            S.add("sp", lambda e, w=w: e.dma_start(out=dst_ap, in_=w.t[:, 0:nel]), reads=[w.b, wsbuf], lane=cvl.get())

        cvl = Rot(["cv0", "cv1", "cv2", "cv3"])
        cwl = Rot(["cw0", "cw1", "cw2", "cw3"])
        S.add("pool", lambda e: e.dma_start(out=wab.t[:].rearrange("p a b -> p (a b)"), in_=w_in_ab[:, :]),
              writes=[wab.b], lane="wab")
        for i in range(16):
            convert_w(w_in_s[i], c_in_s[i], 4096)
        for i in range(4):
            convert_w(w_in_z[i], c_in_z[i], 4096)
        for i in range(2):
            convert_w(w_glu[i], c_glu[i], 4096)
        for i in range(8):
            convert_w(w_out[i], c_out[i], 4096)
        for i in range(22):
            convert_w(w_gate[i], c_gate[i], 4096)
            convert_w(w_up[i], c_up[i], 4096)
        for i in range(8):
            for k in range(4):
                convert_w(w_down[i, k], c_down[i, k], 11 * 256)
        S.add("sp", lambda e: e.nop(), writes=[wsbuf])

        ln_cur = [-1]

        def layer_norm(hb, n, gi):
            if ln_cur[0] != gi:
                ln_cur[0] = gi
                S.add("sp", lambda e: e.dma_start(out=lnbc.t[:].rearrange("p a d -> p (a d)"),
                                                  in_=lnp[gi:gi + 2, :].rearrange("a d -> (a d)").partition_broadcast(128)),
                      writes=[lnbc.b], lane="ln")
            for c in range(4):
                S.add("dve", lambda e, c=c: e.bn_stats(out=stats.t[:n, c, :], in_=hb.t[:n, c * 512:(c + 1) * 512]),
                      reads=[hb.b], writes=[stats.b])
            S.add("dve", lambda e: e.bn_aggr(out=mv.t[:n, 0:2], in_=stats.t[:n].rearrange("p a b -> p (a b)")),
                  reads=[stats.b], writes=[mv.b])
            S.add("dve", lambda e: e.tensor_scalar(out=mv.t[:n, 2:3], in0=mv.t[:n, 1:2], scalar1=LN_EPS, scalar2=None, op0=ALU.add),
                  reads=[mv.b], writes=[mv.b])
            S.add("pool", lambda e: e.tensor_tensor(out=mv.t[:n, 2:3], in0=mv.t[:n, 2:3], in1=mhalf.t[:n, 0:1], op=ALU.pow),
                  reads=[mv.b, mhalf.b], writes=[mv.b])
            S.add("dve", lambda e: e.scalar_tensor_tensor(out=mv.t[:n, 3:4], in0=mv.t[:n, 0:1], scalar=-1.0, in1=mv.t[:n, 2:3],
                                                          op0=ALU.mult, op1=ALU.mult),
                  reads=[mv.b], writes=[mv.b])
            S.add("act", lambda e: e.activation(out=hb.t[:n, :], in_=hb.t[:n, :], func=AF.Identity,
                                                scale=mv.t[:n, 2:3], bias=mv.t[:n, 3:4]),
                  reads=[hb.b, mv.b], writes=[hb.b])
            S.add("dve", lambda e: e.tensor_tensor(out=hb.t[:n, :], in0=hb.t[:n, :], in1=lnbc.t[:n, 0, :], op=ALU.mult),
                  reads=[hb.b, lnbc.b], writes=[hb.b])
            S.add("pool", lambda e: e.tensor_tensor(out=hb.t[:n, :], in0=hb.t[:n, :], in1=lnbc.t[:n, 1, :], op=ALU.add),
                  reads=[hb.b, lnbc.b], writes=[hb.b])

        def to_actT(hb, n, c0, actT):
            S.add("act", lambda e: e.activation(out=hb16.t[:n, :], in_=hb.t[:n, :], func=AF.Copy),
                  reads=[hb.b], writes=[hb16.b])
            for half in range(2):
                p = tpr.get()
                for k in range(8):
                    kc = half * 8 + k
                    S.add("pe", lambda e, p=p, k=k, kc=kc: e.transpose(out=p.t[:, k, 0:n], in_=hb16.t[:n, kc * 128:(kc + 1) * 128],
                                                                       identity=identb.t[:n, :n]),
                          reads=[hb16.b, identb.b], writes=[p.b])
                S.add("act", lambda e, p=p, half=half: e.activation(out=actT.t[:, half * 8:half * 8 + 8, c0:c0 + n],
                                                                    in_=p.t[:, :, 0:n], func=AF.Copy),
                      reads=[p.b], writes=[actT.b])

        def pe_T(dst_ap, dst_b, src_ap, src_b, np_, nf, evac="act", scale=None, scale_b=None):
            p = tpr.get()
            S.add("pe", lambda e: e.transpose(out=p.t[:nf, 0, 0:np_], in_=src_ap, identity=identb.t[:np_, :np_]),
                  reads=[src_b, identb.b], writes=[p.b])
            S.add("act", lambda e: e.activation(out=dst_ap, in_=p.t[:nf, 0, 0:np_], func=AF.Copy),
                  reads=[p.b], writes=[dst_b])

        def rsq(dst, src, n, b, mul, add):
            S.add("dve", lambda e: e.tensor_scalar(out=dst, in0=src, scalar1=mul, scalar2=add, op0=ALU.mult, op1=ALU.add),
                  reads=[b], writes=[b])
            S.add("pool", lambda e: e.tensor_tensor(out=dst, in0=dst, in1=mhalf.t[:n, :], op=ALU.pow),
                  reads=[b, mhalf.b], writes=[b])

        def phase_A(k, blocks):
            hbuf = [hb4[(2 * k) % 4], hb4[(2 * k + 1) % 4]]
            actT = actTs[k % 2]
            c = 0
            for bi, (n, rows, orow) in enumerate(blocks):
                hb = hbuf[bi]
                for (r0, nr, src) in rows:
                    S.add("sp", lambda e, hb=hb, r0=r0, nr=nr, src=src: e.dma_start(out=hb.t[r0:r0 + nr, :], in_=src),
                          writes=[hb.b], lane="x%d" % ((2 * k + bi) % 4))
                yield
                layer_norm(hb, n, 0)
                for _ in range(6):
                    yield
                to_actT(hb, n, c, actT)
                yield
                c += n

        def run_tile(k, blocks, prefetch):
            hbuf = [hb4[(2 * k) % 4], hb4[(2 * k + 1) % 4]]
            actT = actTs[k % 2]
            N = sum(b_[0] for b_ in blocks)
            c0s = []
            c = 0
            for b_ in blocks:
                c0s.append(c)
                c += b_[0]
            nb = len(blocks)
            abps = []
            for bi, (n, rows, orow) in enumerate(blocks):
                pa, pb = gpr.get()
                for kc in range(KC):
                    S.add("pe", lambda e, pa=pa, kc=kc, n=n, c0=c0s[bi]: e.matmul(
                        pa[:n, 0:16], lhsT=actT.t[:, kc, c0:c0 + n], rhs=wab.t[:, kc, :], start=(kc == 0), stop=(kc == KC - 1)),
                        reads=[actT.b, wab.b], writes=[pb])
                S.add("act", lambda e, pa=pa, n=n, bi=bi: e.activation(out=absb.t[:n, bi, :], in_=pa[:n, 0:16], func=AF.Copy),
                      reads=[pb], writes=[absb.bs[bi]])
                abps.append((absb.t[:, bi, :], absb.bs[bi]))
            for bt_ in range(16):
                w = load_w(c_in_s[bt_], 4096)
                wv = w.t[:].rearrange("p (m k c) -> p m k c", m=2, k=KC)
                for mi in range(2):
                    mt = bt_ * 2 + mi
                    p = mmr.get()
                    for kc in range(KC):
                        S.add("pe", lambda e, p=p, wv=wv, mi=mi, kc=kc: e.matmul(
                            p.t[:, 0:N], lhsT=wv[:, mi, kc, :], rhs=actT.t[:, kc, 0:N], start=(kc == 0), stop=(kc == KC - 1)),
                            reads=[w.b, actT.b], writes=[p.b])
                    if mt < 24:
                        cb = cbufs.get()
                        ca = caccs.get()
                        hsrc = hist
                        S.add("act", lambda e, cb=cb, p=p: e.activation(out=cb.t[:, 3:3 + N], in_=p.t[:, 0:N], func=AF.Copy),
                              reads=[p.b], writes=[cb.b])
                        S.add("act", lambda e, cb=cb, mt=mt: e.activation(out=cb.t[:, 0:3], in_=hist.t[:, mt, :], func=AF.Copy),
                              reads=[hist.b, cb.b], writes=[cb.b])
                        S.add("act", lambda e, cb=cb, mt=mt: e.activation(out=hist.t[:, mt, :], in_=cb.t[:, N:N + 3], func=AF.Copy),
                              reads=[cb.b], writes=[hist.b])
                        S.add("dve", lambda e, cb=cb, ca=ca, mt=mt: e.tensor_scalar(
                            out=ca.t[:, 0:N], in0=cb.t[:, 0:N], scalar1=cw.t[:, mt, 0:1], scalar2=None, op0=ALU.mult),
                            reads=[cb.b, cw.b], writes=[ca.b])
                        for i in range(1, 4):
                            S.add("dve", lambda e, cb=cb, ca=ca, mt=mt, i=i: e.scalar_tensor_tensor(
                                out=ca.t[:, 0:N], in0=cb.t[:, i:i + N], scalar=cw.t[:, mt, i:i + 1], in1=ca.t[:, 0:N],
                                op0=ALU.mult, op1=ALU.add),
                                reads=[cb.b, cw.b, ca.b], writes=[ca.b])
                        S.add("act", lambda e, ca=ca, mt=mt: e.activation(out=big.t[:, mt, 0:N], in_=ca.t[:, 0:N], func=AF.Silu),
                              reads=[ca.b], writes=[big.bs[mt]])
                    else:
                        S.add("act", lambda e, p=p, mt=mt: e.activation(out=big.t[:, mt, 0:N], in_=p.t[:, 0:N], func=AF.Copy),
                              reads=[p.b], writes=[big.bs[mt]])
            for cg in range(4):
                w = load_w(c_in_z[cg], 4096)
                wv = w.t[:].rearrange("p (k c) -> p k c", k=KC)
                for bi, (n, rows, orow) in enumerate(blocks):
                    p = mmr.get()
                    for kc in range(KC):
                        S.add("pe", lambda e, p=p, wv=wv, kc=kc, n=n, c0=c0s[bi]: e.matmul(
                            p.t[:n, 0:256], lhsT=actT.t[:, kc, c0:c0 + n], rhs=wv[:, kc, :], start=(kc == 0), stop=(kc == KC - 1)),
                            reads=[w.b, actT.b], writes=[p.b])
                    g_ = gz[bi]
                    S.add("act", lambda e, p=p, g_=g_, n=n, cg=cg: e.activation(out=g_.t[:n, cg * 256:(cg + 1) * 256], in_=p.t[:n, 0:256], func=AF.Silu),
                          reads=[p.b], writes=[g_.b])
                    for hh in range(2):
                        S.add("pool", lambda e, g_=g_, n=n, cg=cg, hh=hh: e.tensor_tensor(
                            out=g_.t[:n, cg * 256 + hh * 128:cg * 256 + hh * 128 + 128],
                            in0=g_.t[:n, cg * 256 + hh * 128:cg * 256 + hh * 128 + 128], in1=normw[:n, :], op=ALU.mult),
                            reads=[g_.b, dn.b], writes=[g_.b])

            def gdn_block(bi, n, c0, pa, pb):
                gs = gsm.t
                gb_ = gsm.b
                S.add("dve", lambda e: e.tensor_tensor(out=gs[:n, 0, :], in0=pa[:n, 0:8], in1=dtb[:n, :], op=ALU.add),
                      reads=[pb, dn.b], writes=[gb_])
                S.add("act", lambda e: e.activation(out=gs[:n, 5, :], in_=pa[:n, 8:16], func=AF.Tanh, scale=0.5),
                      reads=[pb], writes=[gb_])
                S.add("act", lambda e: e.activation(out=gs[:n, 0, :], in_=gs[:n, 0, :], func=AF.Exp),
                      reads=[gb_], writes=[gb_])
                S.add("act", lambda e: e.activation(out=gs[:n, 0, :], in_=gs[:n, 0, :], func=AF.Ln, bias=1.0),
                      reads=[gb_], writes=[gb_])
                S.add("dve", lambda e: e.tensor_tensor(out=gs[:n, 0, :], in0=gs[:n, 0, :], in1=negA.t[:n, :], op=ALU.mult),
                      reads=[gb_, negA.b], writes=[gb_])
                S.add("dve", lambda e: e.tensor_scalar(out=gs[:n, 5, :], in0=gs[:n, 5, :], scalar1=0.5, scalar2=0.5, op0=ALU.mult, op1=ALU.add),
                      reads=[gb_], writes=[gb_])
                S.add("dve", lambda e: e.tensor_scalar_mul(out=gs[:n, 6, :], in0=gs[:n, 5, :], scalar1=-1.0),
                      reads=[gb_], writes=[gb_])
                pc, pcb = gpr.get()
                S.add("pe", lambda e, pc=pc: e.matmul(pc[:n, 0:8], lhsT=uincl[:n, :n], rhs=gs[:n, 0, :], start=True, stop=True),
                      reads=[cm.b, gb_], writes=[pcb])
                S.add("pe", lambda e, pc=pc: e.matmul(pc[:, 8:16], lhsT=ones[:n, :], rhs=gs[:n, 0, :], start=True, stop=True),
                      reads=[cm.b, gb_], writes=[pcb])
                S.add("act", lambda e, pc=pc: e.activation(out=gs[:n, 1, :], in_=pc[:n, 0:8], func=AF.Copy),
                      reads=[pcb], writes=[gb_])
                S.add("act", lambda e, pc=pc: e.activation(out=gs[:n, 2, :], in_=pc[:n, 0:8], func=AF.Exp),
                      reads=[pcb], writes=[gb_])
                S.add("act", lambda e, pc=pc: e.activation(out=gs[:, 7, :], in_=pc[:, 8:16], func=AF.Exp),
                      reads=[pcb], writes=[gb_])
                S.add("dve", lambda e, pc=pc: e.tensor_tensor(out=gs[:n, 4, :], in0=pc[:n, 8:16], in1=gs[:n, 1, :], op=ALU.subtract),
                      reads=[pcb, gb_], writes=[gb_])
                S.add("act", lambda e: e.activation(out=gs[:n, 4, :], in_=gs[:n, 4, :], func=AF.Exp),
                      reads=[gb_], writes=[gb_])
                S.add("dve", lambda e: e.tensor_scalar_mul(out=gs[:n, 3, :], in0=gs[:n, 2, :], scalar1=-1.0),
                      reads=[gb_], writes=[gb_])
                for h in range(H):
                    p = tpr.get()
                    S.add("pe", lambda e, p=p, h=h: e.transpose(out=p.t[:n, 0, :], in_=big.t[:, 8 + h, c0:c0 + n], identity=identb.t[:, :]),
                          reads=[big.bs[8 + h], identb.b], writes=[p.b])
                    S.add("pe", lambda e, p=p, h=h: e.transpose(out=p.t[:n, 1, :], in_=big.t[:, h, c0:c0 + n], identity=identb.t[:, :]),
                          reads=[big.bs[h], identb.b], writes=[p.b])
                    S.add("pe", lambda e, p=p, h=h: e.transpose(out=p.t[:n, 2, :], in_=big.t[:, 16 + h, c0:c0 + n], identity=identb.t[:, :]),
                          reads=[big.bs[16 + h], identb.b], writes=[p.b])
                    jk = gtmp.get()
                    S.add("act", lambda e, p=p, h=h, jk=jk: e.activation(out=jk.t[:n, :], in_=p.t[:n, 0, :], func=AF.Square,
                                                                        accum_out=nrm.t[:n, 0, h:h + 1]),
                          reads=[p.b], writes=[jk.b, nrm.bs[0]])
                    jq = gtmp.get()
                    S.add("act", lambda e, p=p, h=h, jq=jq: e.activation(out=jq.t[:n, :], in_=p.t[:n, 1, :], func=AF.Square,
                                                                        accum_out=nrm.t[:n, 1, h:h + 1]),
                          reads=[p.b], writes=[jq.b, nrm.bs[1]])
                    S.add("act", lambda e, p=p, h=h: e.activation(out=vtm.t[:n, h, :], in_=p.t[:n, 2, :], func=AF.Copy),
                          reads=[p.b], writes=[vtm.bs[h]])
                    S.add("act", lambda e, p=p, h=h: e.activation(out=kdtm.t[:n, h, :], in_=p.t[:n, 0, :], func=AF.Copy),
                          reads=[p.b], writes=[kdtm.bs[h]])
                    S.add("act", lambda e, p=p, h=h: e.activation(out=ontm.t[:n, h, :], in_=p.t[:n, 1, :], func=AF.Copy),
                          reads=[p.b], writes=[ontm.bs[h]])
                    yield
                rsq(nrm.t[:n, 0, :], nrm.t[:n, 0, :], n, nrm.bs[0], 1.0, RMS_EPS)
                rsq(nrm.t[:n, 1, :], nrm.t[:n, 1, :], n, nrm.bs[1], 128.0, 128.0 * RMS_EPS)
                for h in range(H):
                    S.add("act", lambda e, h=h: e.activation(out=ktm.t[:n, h, :], in_=kdtm.t[:n, h, :], func=AF.Copy, scale=nrm.t[:n, 0, h:h + 1]),
                          reads=[kdtm.bs[h], nrm.bs[0]], writes=[ktm.bs[h]])
                    S.add("act", lambda e, h=h: e.activation(out=ontm.t[:n, h, :], in_=ontm.t[:n, h, :], func=AF.Copy, scale=nrm.t[:n, 1, h:h + 1]),
                          reads=[ontm.bs[h], nrm.bs[1]], writes=[ontm.bs[h]])
                    S.add("pool", lambda e, h=h: e.tensor_scalar(out=kdtm.t[:n, h, :], in0=ktm.t[:n, h, :], scalar1=gs[:n, 4, h:h + 1],
                                                                 scalar2=None, op0=ALU.mult),
                          reads=[ktm.bs[h], gb_], writes=[kdtm.bs[h]])
                    p = tpr.get()
                    S.add("pe", lambda e, p=p, h=h: e.transpose(out=p.t[:, 0, 0:n], in_=ktm.t[:n, h, :], identity=identb.t[:n, :n]),
                          reads=[ktm.bs[h], identb.b], writes=[p.b])
                    S.add("pe", lambda e, p=p, h=h: e.transpose(out=p.t[:, 1, 0:n], in_=ontm.t[:n, h, :], identity=identb.t[:n, :n]),
                          reads=[ontm.bs[h], identb.b], writes=[p.b])
                    S.add("act", lambda e, p=p, h=h: e.activation(out=kTn.t[:, h, 0:n], in_=p.t[:, 0, 0:n], func=AF.Copy),
                          reads=[p.b], writes=[kTn.bs[h]])
                    S.add("act", lambda e, p=p, h=h: e.activation(out=qTn.t[:, h, 0:n], in_=p.t[:, 1, 0:n], func=AF.Copy),
                          reads=[p.b], writes=[qTn.bs[h]])
                    yield
                def head_chain(h, gpr, gtmp, gtb):
                    kT = kTn.t[:, h, 0:n]
                    qT = qTn.t[:, h, 0:n]
                    yield
                    pkk, pkkb = gpr.get()
                    S.add("pe", lambda e, pkk=pkk, kT=kT: e.matmul(pkk[:n, :n], lhsT=kT, rhs=kT, start=True, stop=True),
                          reads=[kTn.bs[h]], writes=[pkkb])
                    pqk, pqkb = gpr.get()
                    S.add("pe", lambda e, pqk=pqk, kT=kT, qT=qT: e.matmul(pqk[:n, :n], lhsT=kT, rhs=qT, start=True, stop=True),
                          reads=[kTn.bs[h], qTn.bs[h]], writes=[pqkb])
                    dg = gtmp.get()
                    S.add("act", lambda e, dg=dg, h=h: e.activation(out=dg.t[:n, :n], in_=ident[:n, :n], func=AF.Copy, scale=gs[:n, 1, h:h + 1]),
                          reads=[cm.b, gb_], writes=[dg.b])
                    yield
                    pgb, pgbb = gpr.get()
                    S.add("pe", lambda e, pgb=pgb, dg=dg: e.matmul(pgb[:n, :n], lhsT=ones[:n, :n], rhs=dg.t[:n, :n], start=True, stop=True),
                          reads=[cm.b, dg.b], writes=[pgbb])
                    DT = gtmp.get()
                    S.add("dve", lambda e, DT=DT, pgb=pgb, h=h: e.scalar_tensor_tensor(
                        out=DT.t[:n, :n], in0=pgb[:n, :n], scalar=gs[:n, 1, h:h + 1], in1=negmT[:n, :n], op0=ALU.subtract, op1=ALU.add),
                        reads=[pgbb, gb_, cm.b], writes=[DT.b])
                    S.add("act", lambda e, DT=DT: e.activation(out=DT.t[:n, :n], in_=DT.t[:n, :n], func=AF.Exp),
                          reads=[DT.b], writes=[DT.b])
                    QKm = gtb.get()
                    S.add("dve", lambda e, QKm=QKm, pqk=pqk, DT=DT: e.tensor_tensor(out=QKm.t[:n, :n], in0=pqk[:n, :n], in1=DT.t[:n, :n], op=ALU.mult),
                          reads=[pqkb, DT.b], writes=[QKm.b])
                    DTs = gtmp.get()
                    S.add("pool", lambda e, DTs=DTs, DT=DT: e.tensor_tensor(out=DTs.t[:n, :n], in0=DT.t[:n, :n], in1=strict[:n, :n], op=ALU.mult),
                          reads=[DT.b, cm.b], writes=[DTs.b])
                    cur = gtmp.get()
                    S.add("dve", lambda e, cur=cur, pkk=pkk, DTs=DTs, h=h: e.scalar_tensor_tensor(
                        out=cur.t[:n, :n], in0=pkk[:n, :n], scalar=gs[:n, 6, h:h + 1], in1=DTs.t[:n, :n], op0=ALU.mult, op1=ALU.mult),
                        reads=[pkkb, gb_, DTs.b], writes=[cur.b])
                    yield
                    pt_, ptb = gpr.get()
                    S.add("pe", lambda e, pt_=pt_, cur=cur: e.transpose(out=pt_[:n, :n], in_=cur.t[:n, :n], identity=ident[:n, :n]),
                          reads=[cur.b, cm.b], writes=[ptb])
                    curT = gtmp.get()
                    S.add("act", lambda e, curT=curT, pt_=pt_: e.activation(out=curT.t[:n, :n], in_=pt_[:n, :n], func=AF.Copy),
                          reads=[ptb], writes=[curT.b])
                    P = gtmp.get()
                    S.add("dve", lambda e, P=P, cur=cur: e.tensor_tensor(out=P.t[:n, :n], in0=cur.t[:n, :n], in1=ident[:n, :n], op=ALU.add),
                          reads=[cur.b, cm.b], writes=[P.b])
                    nlev = 6 if n == 128 else 3
                    for k in range(1, nlev + 1):
                        nxt = None
                        if k < nlev:
                            yield
                            pn, pnb = gpr.get()
                            S.add("pe", lambda e, pn=pn, cur=cur, curT=curT: e.matmul(pn[:n, :n], lhsT=curT.t[:n, :n], rhs=cur.t[:n, :n], start=True, stop=True),
                                  reads=[cur.b, curT.b], writes=[pnb])
                            nxt = gtmp.get()
                            S.add("act", lambda e, nxt=nxt, pn=pn: e.activation(out=nxt.t[:n, :n], in_=pn[:n, :n], func=AF.Copy),
                                  reads=[pnb], writes=[nxt.b])
                        pn2, pn2b = gpr.get()
                        S.add("pe", lambda e, pn2=pn2, cur=cur, curT=curT: e.matmul(pn2[:n, :n], lhsT=cur.t[:n, :n], rhs=curT.t[:n, :n], start=True, stop=True),
                              reads=[cur.b, curT.b], writes=[pn2b])
                        nxtT = gtmp.get()
                        S.add("act", lambda e, nxtT=nxtT, pn2=pn2: e.activation(out=nxtT.t[:n, :n], in_=pn2[:n, :n], func=AF.Copy),
                              reads=[pn2b], writes=[nxtT.b])
                        yield
                        pp, ppb = gpr.get()
                        S.add("pe", lambda e, pp=pp, nxtT=nxtT, P=P: e.matmul(pp[:n, :n], lhsT=nxtT.t[:n, :n], rhs=P.t[:n, :n], start=True, stop=True),
                              reads=[nxtT.b, P.b], writes=[ppb])
                        P2 = gtmp.get()
                        S.add("dve", lambda e, P2=P2, P=P, pp=pp: e.tensor_tensor(out=P2.t[:n, :n], in0=pp[:n, :n], in1=P.t[:n, :n], op=ALU.add),
                              reads=[ppb, P.b], writes=[P2.b])
                        P = P2
                        cur, curT = nxt, nxtT
                    yield
                    pks, pksb = gpr.get()
                    S.add("pe", lambda e, pks=pks, kT=kT, h=h: e.matmul(pks[:n, :], lhsT=kT, rhs=Sbf.t[:, h, :], start=True, stop=True),
                          reads=[kTn.bs[h], Sbf.bs[h]], writes=[pksb])
                    U = gtmp.get()
                    S.add("dve", lambda e, U=U, pks=pks, h=h: e.scalar_tensor_tensor(
                        out=U.t[:n, :], in0=pks[:n, :], scalar=gs[:n, 3, h:h + 1], in1=vtm.t[:n, h, :], op0=ALU.mult, op1=ALU.add),
                        reads=[pksb, gb_, vtm.bs[h]], writes=[U.b])
                    yield
                    pw, pwb = gpr.get()
                    S.add("pe", lambda e, pw=pw, P=P, U=U: e.matmul(pw[:n, :], lhsT=P.t[:n, :n], rhs=U.t[:n, :], start=True, stop=True),
                          reads=[P.b, U.b], writes=[pwb])
                    wt = gtb.get()
                    S.add("act", lambda e, wt=wt, pw=pw, h=h: e.activation(out=wt.t[:n, :], in_=pw[:n, :], func=AF.Copy, scale=gs[:n, 5, h:h + 1]),
                          reads=[pwb, gb_], writes=[wt.b])
                    yield
                    po1, po1b = gpr.get()
                    S.add("pe", lambda e, po1=po1, qT=qT, h=h: e.matmul(po1[:n, :], lhsT=qT, rhs=Sbf.t[:, h, :], start=True, stop=True),
                          reads=[qTn.bs[h], Sbf.bs[h]], writes=[po1b])
                    po2, po2b = gpr.get()
                    S.add("pe", lambda e, po2=po2, QKm=QKm, wt=wt: e.matmul(po2[:n, :], lhsT=QKm.t[:n, :n], rhs=wt.t[:n, :], start=True, stop=True),
                          reads=[QKm.b, wt.b], writes=[po2b])
                    t1 = gtmp.get()
                    S.add("act", lambda e, t1=t1, po1=po1, h=h: e.activation(out=t1.t[:n, :], in_=po1[:n, :], func=AF.Copy, scale=gs[:n, 2, h:h + 1]),
                          reads=[po1b, gb_], writes=[t1.b])
                    S.add("dve", lambda e, t1=t1, po2=po2, h=h: e.tensor_tensor(out=otm.t[:n, h, :], in0=po2[:n, :], in1=t1.t[:n, :], op=ALU.add),
                          reads=[po2b, t1.b], writes=[otm.bs[h]])
                    psu, psub = gpr.get()
                    S.add("pe", lambda e, psu=psu, wt=wt, h=h: e.matmul(psu[:, :], lhsT=kdtm.t[:n, h, :], rhs=wt.t[:n, :], start=True, stop=True),
                          reads=[kdtm.bs[h], wt.b], writes=[psub])
                    S.add("dve", lambda e, psu=psu, h=h: e.scalar_tensor_tensor(
                        out=Sst.t[:, h, :], in0=Sst.t[:, h, :], scalar=gs[:, 7, h:h + 1], in1=psu[:, :], op0=ALU.mult, op1=ALU.add),
                        reads=[Sst.bs[h], gb_, psub], writes=[Sst.bs[h]])
                    S.add("act", lambda e, h=h: e.activation(out=Sbf.t[:, h, :], in_=Sst.t[:, h, :], func=AF.Copy),
                          reads=[Sst.bs[h]], writes=[Sbf.bs[h]])
                    j2 = gtmp.get()
                    S.add("act", lambda e, j2=j2, h=h: e.activation(out=j2.t[:n, :], in_=otm.t[:n, h, :], func=AF.Square, accum_out=nrm.t[:n, 2, h:h + 1]),
                          reads=[otm.bs[h]], writes=[j2.b, nrm.bs[2]])
                lane_ps = [[(gp_t[0].t[:, j, :], gp_t[0].bs[0]) for j in range(4)],
                           [(gp_t[1].t[:, j, :], gp_t[1].bs[0]) for j in range(4)],
                           [(mm[2].t[:, j * 128:(j + 1) * 128], mm[2].b) for j in range(4)],
                           [(mm[3].t[:, j * 128:(j + 1) * 128], mm[3].b) for j in range(4)]]
                lanes = [(Rot(lane_ps[i]), Rot(gtmp.items[i * 6:(i + 1) * 6]), Rot(gtb.items[i * 2:(i + 1) * 2])) for i in range(4)]
                for h0 in range(0, H, 4):
                    alive = [head_chain(h0 + i, *lanes[i]) for i in range(4)]
                    while alive:
                        for g in list(alive):
                            try:
                                next(g)
                            except StopIteration:
                                alive.remove(g)
                        yield
                rsq(nrm.t[:n, 2, :], nrm.t[:n, 2, :], n, nrm.bs[2], 1.0 / 128.0, RMS_EPS)
                g_ = gz[bi]
                for h in range(H):
                    S.add("dve", lambda e, h=h, g_=g_: e.scalar_tensor_tensor(
                        out=ontm.t[:n, h, :], in0=otm.t[:n, h, :], scalar=nrm.t[:n, 2, h:h + 1], in1=g_.t[:n, h * 128:(h + 1) * 128],
                        op0=ALU.mult, op1=ALU.mult),
                        reads=[otm.bs[h], nrm.bs[2], g_.b], writes=[ontm.bs[h]])
                p = tpr.get()
                for h in range(H):
                    S.add("pe", lambda e, p=p, h=h: e.transpose(out=p.t[:, h, 0:n], in_=ontm.t[:n, h, :], identity=identb.t[:n, :n]),
                          reads=[ontm.bs[h], identb.b], writes=[p.b])
                S.add("act", lambda e, p=p: e.activation(out=mixT.t[:, 0:8, c0:c0 + n], in_=p.t[:, :, 0:n], func=AF.Copy),
                      reads=[p.b], writes=[mixT.b])

            pending_gdn = [(bi, n) for bi, (n, rows, orow) in enumerate(blocks)]

            def s5_gen():
                C = N // 8
                L = C + 1
                u3 = [big.t[:, 24 + ft, 0:N].rearrange("p (c s) -> p c s", s=8) for ft in range(8)]
                for j in range(32):
                    ft = j // 4
                    wv = s5ws[j % 2]
                    S.add("sp", lambda e, wv=wv, j=j: e.dma_start(out=wv.t[:].rearrange("p a c -> p (a c)"), in_=s5w2[j, :, 0:2048]),
                          reads=[swbuf], writes=[wv.b], lane="s5%d" % (j % 2))
                    for c in range(2):
                        ps = mm[c]
                        col = (j % 16) * 32
                        for s_ in range(8):
                            S.add("pe", lambda e, ps=ps, col=col, wv=wv, s_=s_, c=c, ft=ft, j=j: e.matmul(
                                ps.t[:, col:col + C], lhsT=wv.t[:, s_ * 2 + c, :], rhs=u3[ft][:, :, s_], start=(s_ == 0), stop=(s_ == 7)),
                                reads=[wv.b, big.bs[24 + ft]], writes=[ps.b])
                    if j % 16 == 15:
                        for c in range(2):
                            S.add("act", lambda e, c=c, hf=j // 16: e.activation(
                                out=xs[c].t[:, hf * 16:hf * 16 + 16, 1:1 + C], in_=mm[c].t[:, :].rearrange("p (j q) -> p j q", q=32)[:, :, 0:C], func=AF.Copy),
                                reads=[mm[c].b], writes=[xs[c].b])
                    yield
                xa = [xs[0], xs[1]]
                for c in range(2):
                    S.add("pool", lambda e, c=c, X=xa[c]: e.tensor_copy(out=X.t[:, :, 0], in_=xprev.t[:, c, :]),
                          reads=[xprev.b, xa[c].b], writes=[xa[c].b])
                yield
                Xr, Xi = xa
                LR = l8.t[:, 0, 0, :]
                LI = l8.t[:, 0, 1, :]

                def tt(out_ap, a_ap, b_ap, op, rd, wr):
                    S.add("pool", lambda e: e.tensor_tensor(out=out_ap, in0=a_ap, in1=b_ap, op=op), reads=rd, writes=wr)
                for cc in range(1, L):
                    tq = xq[0]
                    q0, q1, q2, q3 = tq.bs
                    tt(tq.t[:, 0, :], Xr.t[:, :, cc - 1], LR, ALU.mult, [Xr.b, l8.b], [q0])
                    tt(tq.t[:, 1, :], Xi.t[:, :, cc - 1], LI, ALU.mult, [Xi.b, l8.b], [q1])
                    tt(tq.t[:, 2, :], Xi.t[:, :, cc - 1], LR, ALU.mult, [Xi.b, l8.b], [q2])
                    tt(tq.t[:, 3, :], Xr.t[:, :, cc - 1], LI, ALU.mult, [Xr.b, l8.b], [q3])
                    tt(tq.t[:, 0, :], tq.t[:, 0, :], tq.t[:, 1, :], ALU.subtract, [q0, q1], [q0])
                    tt(tq.t[:, 2, :], tq.t[:, 2, :], tq.t[:, 3, :], ALU.add, [q2, q3], [q2])
                    tt(Xr.t[:, :, cc], Xr.t[:, :, cc], tq.t[:, 0, :], ALU.add, [Xr.b, q0], [Xr.b])
                    tt(Xi.t[:, :, cc], Xi.t[:, :, cc], tq.t[:, 2, :], ALU.add, [Xi.b, q2], [Xi.b])
                    if cc % 4 == 0:
                        yield
                for c in range(2):
                    S.add("act", lambda e, c=c, X=xa[c]: e.activation(out=xb.t[:, c, :, 0:C], in_=X.t[:, :, 0:C], func=AF.Copy),
                          reads=[xa[c].b, xb.b], writes=[xb.b])
                    S.add("pool", lambda e, c=c, X=xa[c]: e.tensor_copy(out=xprev.t[:, c, :], in_=X.t[:, :, C]),
                          reads=[xa[c].b, xprev.b], writes=[xprev.b])
                for ft in range(8):
                    kb = s5ks[0]
                    S.add("sp", lambda e, kb=kb, ft=ft: e.dma_start(out=kb.t[:].rearrange("p a c -> p (a c)"), in_=kscr[ft]),
                          reads=[swbuf], writes=[kb.b], lane="s5k0")
                    yp = mm[ft % 2]
                    y3 = yp.t[:, 0:N].rearrange("p (c s) -> p c s", s=8)
                    for hf in range(4):
                        vb = s5ws[hf % 2]
                        S.add("sp", lambda e, vb=vb, ft=ft, hf=hf: e.dma_start(
                            out=vb.t[:].rearrange("p (q a) c -> p q (a c)", q=4),
                            in_=s5w2[ft * 4:ft * 4 + 4, :, 2048 + hf * 512:2048 + (hf + 1) * 512].rearrange("q p f -> p q f")),
                            reads=[swbuf], writes=[vb.b], lane="s5%d" % (hf % 2))
                        for s_ in range(2 * hf, 2 * hf + 2):
                            for sp_ in range(s_ + 1):
                                S.add("pe", lambda e, y3=y3, kb=kb, s_=s_, sp_=sp_, ft=ft: e.matmul(
                                    y3[:, :, s_], lhsT=kb.t[:, s_ - sp_, :], rhs=u3[ft][:, :, sp_], start=(sp_ == 0), stop=False),
                                    reads=[kb.b, big.bs[24 + ft]], writes=[yp.b])
                            for jj in range(4):
                                for c in range(2):
                                    S.add("pe", lambda e, y3=y3, vb=vb, s_=s_, c=c, jj=jj, ft=ft, hf=hf: e.matmul(
                                        y3[:, :, s_], lhsT=vb.t[:, jj * 4 + (s_ - 2 * hf) * 2 + c, :], rhs=xb.t[:, c, ft * 4 + jj, 0:C], start=False,
                                        stop=(jj == 3 and c == 1)),
                                        reads=[vb.b, xb.b], writes=[yp.b])
                            yield
                    if True:
                        yx = sgt.get()
                        S.add("dve", lambda e, yx=yx, yp=yp, ft=ft: e.scalar_tensor_tensor(
                            out=yx.t[:, 0:N], in0=big.t[:, 24 + ft, 0:N], scalar=d5.t[:, ft:ft + 1], in1=yp.t[:, 0:N], op0=ALU.mult, op1=ALU.add),
                            reads=[big.bs[24 + ft], d5.b, yp.b], writes=[yx.b])
                        sq = sgt.get()
                        S.add("act", lambda e, sq=sq, yx=yx: e.activation(out=sq.t[:, 0:N], in_=yx.t[:, 0:N], func=AF.Square),
                              reads=[yx.b], writes=[sq.b])
                        S.add("dve", lambda e, sq=sq: e.tensor_scalar(out=sq.t[:, 0:N], in0=sq.t[:, 0:N], scalar1=0.044715, scalar2=1.0, op0=ALU.mult, op1=ALU.add),
                              reads=[sq.b], writes=[sq.b])
                        S.add("pool", lambda e, sq=sq, yx=yx: e.tensor_tensor(out=sq.t[:, 0:N], in0=sq.t[:, 0:N], in1=yx.t[:, 0:N], op=ALU.mult),
                              reads=[sq.b, yx.b], writes=[sq.b])
                        S.add("act", lambda e, sq=sq: e.activation(out=sq.t[:, 0:N], in_=sq.t[:, 0:N], func=AF.Tanh, scale=math.sqrt(2.0 / math.pi)),
                              reads=[sq.b], writes=[sq.b])
                        S.add("dve", lambda e, sq=sq: e.tensor_scalar(out=sq.t[:, 0:N], in0=sq.t[:, 0:N], scalar1=0.5, scalar2=0.5, op0=ALU.mult, op1=ALU.add),
                              reads=[sq.b], writes=[sq.b])
                        S.add("pool", lambda e, sq=sq, yx=yx, ft=ft: e.tensor_tensor(out=big.t[:, 32 + ft, 0:N], in0=sq.t[:, 0:N], in1=yx.t[:, 0:N], op=ALU.mult),
                              reads=[sq.b, yx.b], writes=[big.bs[32 + ft]])
            s5g = s5_gen()
            s5_alive = True
            for bi, n in pending_gdn:
                for step, _ in enumerate(gdn_block(bi, n, c0s[bi], abps[bi][0], abps[bi][1])):
                    if s5_alive and step % 4 == 3:
                        try:
                            next(s5g)
                        except StopIteration:
                            s5_alive = False
            if s5_alive:
                for _ in s5g:
                    pass
            for mt in range(8):
                if mt % 4 == 0:
                    w = load_w(c_glu[mt // 4], 4096)
                    wv = w.t[:].rearrange("p (m k c) -> p m k c", m=4, k=8)
                p = mmr.get()
                for kc in range(8):
                    S.add("pe", lambda e, p=p, wv=wv, mt=mt, kc=kc: e.matmul(p.t[:, 0:N], lhsT=wv[:, mt % 4, kc, :], rhs=big.t[:, 32 + kc, 0:N],
                                                                            start=(kc == 0), stop=(kc == 7)),
                          reads=[w.b, big.bs[32 + kc]], writes=[p.b])
                sg = sgt.get()
                S.add("act", lambda e, p=p, sg=sg, mt=mt: e.activation(out=sg.t[:, 0:N], in_=p.t[:, 0:N], func=AF.Tanh, scale=0.5, bias=hbg.t[:, mt:mt + 1]),
                      reads=[p.b, hbg.b], writes=[sg.b])
                S.add("dve", lambda e, sg=sg: e.tensor_scalar(out=sg.t[:, 0:N], in0=sg.t[:, 0:N], scalar1=0.5, scalar2=0.5, op0=ALU.mult, op1=ALU.add),
                      reads=[sg.b], writes=[sg.b])
                S.add("pool", lambda e, sg=sg, mt=mt: e.tensor_tensor(out=mixT.t[:, 8 + mt, 0:N], in0=sg.t[:, 0:N], in1=big.t[:, 32 + mt, 0:N], op=ALU.mult),
                      reads=[sg.b, big.bs[32 + mt]], writes=[mixT.b])
            for cg in range(8):
                w = load_w(c_out[cg], 4096)
                wv = w.t[:].rearrange("p (k c) -> p k c", k=KC)
                for bi, (n, rows, orow) in enumerate(blocks):
                    hb = hbuf[bi]
                    p = mmr.get()
                    for kc in range(KC):
                        S.add("pe", lambda e, p=p, wv=wv, kc=kc, n=n, c0=c0s[bi]: e.matmul(
                            p.t[:n, 0:256], lhsT=mixT.t[:, kc, c0:c0 + n], rhs=wv[:, kc, :], start=(kc == 0), stop=(kc == KC - 1)),
                            reads=[w.b, mixT.b], writes=[p.b])
                    S.add("dve", lambda e, p=p, hb=hb, n=n, cg=cg: e.scalar_tensor_tensor(
                        out=hb.t[:n, cg * 256:(cg + 1) * 256], in0=hb.t[:n, cg * 256:(cg + 1) * 256], scalar=ALPHA, in1=p.t[:n, 0:256],
                        op0=ALU.mult, op1=ALU.add),
                        reads=[hb.b, p.b], writes=[hb.b])
            for bi, (n, rows, orow) in enumerate(blocks):
                layer_norm(hbuf[bi], n, 2)
                to_actT(hbuf[bi], n, c0s[bi], actT)
            for bt_ in range(22):
                if prefetch is not None and bt_ >= 2:
                    try:
                        next(prefetch)
                    except StopIteration:
                        prefetch = None
                wg = load_w(c_gate[bt_], 4096)
                wu = load_w(c_up[bt_], 4096)
                wgv = wg.t[:].rearrange("p (m k c) -> p m k c", m=2, k=KC)
                wuv = wu.t[:].rearrange("p (m k c) -> p m k c", m=2, k=KC)
                for mi in range(2):
                    mt = bt_ * 2 + mi
                    pg = mmr.get()
                    pu = mmr.get()
                    for kc in range(KC):
                        S.add("pe", lambda e, pg=pg, wgv=wgv, mi=mi, kc=kc: e.matmul(
                            pg.t[:, 0:N], lhsT=wgv[:, mi, kc, :], rhs=actT.t[:, kc, 0:N], start=(kc == 0), stop=(kc == KC - 1)),
                            reads=[wg.b, actT.b], writes=[pg.b])
                    for kc in range(KC):
                        S.add("pe", lambda e, pu=pu, wuv=wuv, mi=mi, kc=kc: e.matmul(
                            pu.t[:, 0:N], lhsT=wuv[:, mi, kc, :], rhs=actT.t[:, kc, 0:N], start=(kc == 0), stop=(kc == KC - 1)),
                            reads=[wu.b, actT.b], writes=[pu.b])
                    sg = sgt.get()
                    S.add("act", lambda e, pg=pg, sg=sg: e.activation(out=sg.t[:, 0:N], in_=pg.t[:, 0:N], func=AF.Silu),
                          reads=[pg.b], writes=[sg.b])
                    S.add("dve", lambda e, pu=pu, sg=sg, mt=mt: e.tensor_tensor(out=big.t[:, mt, 0:N], in0=pu.t[:, 0:N], in1=sg.t[:, 0:N], op=ALU.mult),
                          reads=[pu.b, sg.b], writes=[big.bs[mt]])
            if prefetch is not None:
                for _ in prefetch:
                    pass
            for cg in range(8):
                ps_ = [mmr.get() for _ in range(nb)]
                for ch in range(4):
                    w = load_w(c_down[cg, ch], 11 * 256)
                    wv = w.t[:, 0:11 * 256].rearrange("p (k c) -> p k c", k=11)
                    for bi, (n, rows, orow) in enumerate(blocks):
                        for kc in range(11):
                            S.add("pe", lambda e, p=ps_[bi], wv=wv, kc=kc, ch=ch, n=n, c0=c0s[bi]: e.matmul(
                                p.t[:n, 0:256], lhsT=big.t[:, ch * 11 + kc, c0:c0 + n], rhs=wv[:, kc, :],
                                start=(ch == 0 and kc == 0), stop=(ch == 3 and kc == 10)),
                                reads=[w.b, big.bs[ch * 11 + kc]], writes=[ps_[bi].b])
                for bi, (n, rows, orow) in enumerate(blocks):
                    hb = hbuf[bi]
                    S.add("dve", lambda e, p=ps_[bi], hb=hb, n=n, cg=cg: e.scalar_tensor_tensor(
                        out=hb.t[:n, cg * 256:(cg + 1) * 256], in0=hb.t[:n, cg * 256:(cg + 1) * 256], scalar=ALPHA, in1=p.t[:n, 0:256],
                        op0=ALU.mult, op1=ALU.add),
                        reads=[hb.b, ps_[bi].b], writes=[hb.b])
            for bi, (n, rows, orow) in enumerate(blocks):
                hb = hbuf[bi]
                layer_norm(hb, n, 4)
                for (r0, nr, dst) in orow:
                    S.add("sp", lambda e, hb=hb, r0=r0, nr=nr, dst=dst: e.dma_start(out=dst, in_=hb.t[r0:r0 + nr, :]),
                          reads=[hb.b, outbuf], lane="y%d" % ((2 * k + bi) % 4))

        def init_state(sample):
            if sample:
                S.add("sp", lambda e: e.dma_start(out=Sst.t[:], in_=st_delta.rearrange("h k v -> k h v")),
                      writes=Sst.bs, lane="st")
                S.add("sp", lambda e: e.dma_start(out=hist.t[:], in_=st_conv[:, :, :]), writes=[hist.b], lane="st")
                S.add("sp", lambda e: e.dma_start(out=xprev.t[:, 0, :], in_=st_re[:, :]), writes=[xprev.b], lane="st")
                S.add("sp", lambda e: e.dma_start(out=xprev.t[:, 1, :], in_=st_im[:, :]), writes=[xprev.b], lane="st")
            else:
                S.add("pool", lambda e: e.memset(Sst.t[:], 0.0), writes=Sst.bs)
                S.add("pool", lambda e: e.memset(hist.t[:], 0.0), writes=[hist.b])
                S.add("pool", lambda e: e.memset(xprev.t[:], 0.0), writes=[xprev.b])
            for h in range(H):
                S.add("act", lambda e, h=h: e.activation(out=Sbf.t[:, h, :], in_=Sst.t[:, h, :], func=AF.Copy),
                      reads=[Sst.bs[h]], writes=[Sbf.bs[h]])

        def store_state(o_conv, o_delta, o_re, o_im):
            S.add("sp", lambda e: e.dma_start(out=o_delta.rearrange("h k v -> k h v"), in_=Sst.t[:]),
                  reads=Sst.bs + [outbuf], lane="so")
            S.add("sp", lambda e: e.dma_start(out=o_conv[:, :, :], in_=hist.t[:]), reads=[hist.b, outbuf], lane="so")
            S.add("sp", lambda e: e.dma_start(out=o_re[:, :], in_=xprev.t[:, 0, :]), reads=[xprev.b, outbuf], lane="so")
            S.add("sp", lambda e: e.dma_start(out=o_im[:, :], in_=xprev.t[:, 1, :]), reads=[xprev.b, outbuf], lane="so")

        init_state(False)
        bl = []
        for b in range(nblk):
            s0 = 128 * b
            if b == 0:
                rows = [(0, NMETA, meta[:, :]), (NMETA, 128 - NMETA, x_p[0:128 - NMETA, :])]
                orow = [(NMETA, 128 - NMETA, y_p[0:128 - NMETA, :])]
            else:
                rows = [(0, 128, x_p[s0 - NMETA:s0 - NMETA + 128, :])]
                orow = [(0, 128, y_p[s0 - NMETA:s0 - NMETA + 128, :])]
            bl.append((128, rows, orow))
        tiles = [bl[i:i + 2] for i in range(0, nblk, 2)]
        tiles.append([(16, [(0, 16, x_p[seq - 16:seq, :])], [(0, 16, y_p[seq - 16:seq, :])])])
        tiles.append([(16, [(0, 16, x_s[:, :])], [(0, 16, y_s[:, :])])])
        for _ in phase_A(0, tiles[0]):
            pass
        for k, tl in enumerate(tiles):
            if k == len(tiles) - 1:
                store_state(o_conv_p, o_delta_p, o_re_p, o_im_p)
                init_state(True)
            run_tile(k, tl, phase_A(k + 1, tiles[k + 1]) if k + 1 < len(tiles) else None)
        store_state(o_conv_s, o_delta_s, o_re_s, o_im_s)
        S.add("sp", lambda e: e.nop(), writes=[outbuf])
        with nc.allow_non_contiguous_dma(reason="small state layouts"):
            S.emit(nc, es)
    return nc


def _consts():
    i = np.arange(128)
    ident = np.eye(128, dtype=np.float32)
    ones = np.ones((128, 128), np.float32)
    negmT = np.where(i[:, None] <= i[None, :], 0.0, NEG).astype(np.float32)
    strict = (i[:, None] < i[None, :]).astype(np.float32)
    uincl = (i[:, None] <= i[None, :]).astype(np.float32)
    return np.ascontiguousarray(np.stack([ident, ones, negmT, strict, uincl], axis=1).reshape(128, 5 * 128))


def _stat(w, nb):
    K, M = w.shape
    kc = K // 128
    mt = M // 128
    a = w.reshape(kc, 128, mt, 128).transpose(2, 1, 0, 3)
    a = a.reshape(mt // nb, nb, 128, kc, 128).transpose(0, 2, 1, 3, 4)
    return np.ascontiguousarray(a.reshape(mt // nb, 128, nb * kc * 128))


def _mov(w, ncg):
    K, M = w.shape
    kc = K // 128
    a = w.reshape(kc, 128, ncg, 256).transpose(2, 1, 0, 3)
    return np.ascontiguousarray(a.reshape(ncg, 128, kc * 256))


def _prep_shared(inp):
    f = lambda a: np.asarray(a, np.float32)
    w_in = f(inp["w_in"])[0]
    sh = {}
    sh["w_in_s"] = _stat(np.concatenate([w_in[:, 0:3072], w_in[:, 4112:5136]], axis=1), 2)
    sh["w_in_z"] = _mov(w_in[:, 3072:4096], 4)
    sh["w_in_ab"] = np.ascontiguousarray(w_in[:, 4096:4112].reshape(KC, 128, 16).transpose(1, 0, 2).reshape(128, KC * 16))
    sh["cw"] = np.ascontiguousarray(f(inp["conv_w"])[0].reshape(4, 24, 128).transpose(2, 1, 0).reshape(128, 96))
    dnp = np.zeros((3, 128), np.float32)
    dnp[0, :8] = f(inp["dn_a_log"])[0]
    dnp[1, :8] = f(inp["dn_dt_bias"])[0]
    dnp[2, :] = f(inp["dn_norm_w"])[0]
    sh["dnp"] = dnp
    sh["s5a"] = np.ascontiguousarray(np.stack([f(inp["s5_a_re"])[0], f(inp["s5_a_im"])[0],
                                               np.broadcast_to(f(inp["s5_log_dt"])[0][:, None], (64, 64))]))
    bpad = np.zeros((2, 32, 128, 128), np.float32)
    cpad = np.zeros((2, 32, 128, 128), np.float32)
    for c, (bk, ck) in enumerate((("s5_b_re", "s5_c_re"), ("s5_b_im", "s5_c_im"))):
        B = f(inp[bk])[0]
        C = f(inp[ck])[0]
        for g in range(64):
            j, g2, gl = g // 2, g % 2, g % 8
            bpad[c, j, g2 * 64:(g2 + 1) * 64, gl * 16:(gl + 1) * 16] = B[g]
            cpad[c, j, g2 * 64:(g2 + 1) * 64, gl * 16:(gl + 1) * 16] = C[g].T
    sh["bpadT"] = bpad
    sh["cpad"] = cpad
    sh["s5d"] = np.ascontiguousarray(f(inp["s5_d"])[0].reshape(8, 128).T)
    sh["bglu"] = np.ascontiguousarray(f(inp["s5_b_glu"])[0].reshape(8, 128).T)
    wg = f(inp["s5_w_glu"])[0]
    sh["w_glu"] = _stat(wg, 4)
    sh["w_out"] = _mov(f(inp["w_out"])[0], 8)
    sh["w_gate"] = _stat(f(inp["ffn_w_gate"])[0], 2)
    sh["w_up"] = _stat(f(inp["ffn_w_up"])[0], 2)
    wd = f(inp["ffn_w_down"])[0]
    a = wd.reshape(4, 11, 128, 8, 256).transpose(3, 0, 2, 1, 4)
    sh["w_down"] = np.ascontiguousarray(a.reshape(8, 4, 128, 11 * 256))
    sh["cmat"] = _consts()
    sh["meta"] = f(inp["meta_tokens"])
    sh["lnp"] = np.ascontiguousarray(np.stack([f(inp["ln_in_g"]), f(inp["ln_in_b"]), f(inp["ln1_g"])[0], f(inp["ln1_b"])[0],
                                               f(inp["ln2_g"])[0], f(inp["ln2_b"])[0]]))
    return sh


def _s5_in(a):
    return np.ascontiguousarray(a.reshape(32, 2, 64).transpose(1, 2, 0).reshape(128, 32))


def _s5_out(a):
    return np.ascontiguousarray(a.reshape(2, 64, 32).transpose(2, 0, 1).reshape(64, 64))


def _conv_in(a):
    return np.ascontiguousarray(a.reshape(3, 24, 128).transpose(2, 1, 0))


def _conv_out(a):
    return np.ascontiguousarray(a.transpose(2, 1, 0).reshape(3, 3072))


_NC_CACHE = {}


def run(inputs, ncores, nblk):
    f = lambda a: np.asarray(a, np.float32)
    sh = _prep_shared(inputs)
    if nblk not in _NC_CACHE:
        _NC_CACHE[nblk] = build(nblk)
    nc = _NC_CACHE[nblk]
    in_maps = []
    for c in range(ncores):
        m = dict(sh)
        m["x_p"] = np.ascontiguousarray(f(inputs["x_prompt"])[c])
        m["x_s"] = np.ascontiguousarray(f(inputs["x_sample"])[c])
        m["st_conv"] = _conv_in(f(inputs["state_conv_qkv"])[0, c])
        m["st_delta"] = np.ascontiguousarray(f(inputs["state_delta"])[0, c])
        m["st_re"] = _s5_in(f(inputs["state_s5_re"])[0, c])
        m["st_im"] = _s5_in(f(inputs["state_s5_im"])[0, c])
        in_maps.append(m)
    res = run_bass_kernel_spmd(nc, in_maps, core_ids=list(range(ncores)))
    R = res.results
    st = lambda k, fn=lambda a: a: np.stack([fn(np.asarray(R[c][k], np.float32)) for c in range(ncores)])[None]
    return (np.stack([np.asarray(R[c]["y_p"], np.float32) for c in range(ncores)]),
            np.stack([np.asarray(R[c]["y_s"], np.float32) for c in range(ncores)]),
            st("o_conv_p", _conv_out), st("o_delta_p"), st("o_re_p", _s5_out), st("o_im_p", _s5_out),
            st("o_conv_s", _conv_out), st("o_delta_s"), st("o_re_s", _s5_out), st("o_im_s", _s5_out))


def kernel(**inputs):
    return run(inputs, 8, 32)
```

```python
import math
from contextlib import ExitStack

import numpy as np

import concourse.bass as bass
import concourse.mybir as mybir
from concourse.bass_utils import run_bass_kernel_spmd

F32 = mybir.dt.float32
BF16 = mybir.dt.bfloat16
ALU = mybir.AluOpType
AF = mybir.ActivationFunctionType
AX = mybir.AxisListType

D = 2048
KC = 16
H = 8
QKV = 3072
FF = 5632
FKC = 44
NMETA = 16
ALPHA = 2.0 ** 0.25
LN_EPS = 1e-5
RMS_EPS = 1e-6
PAD = 258
NEG = -30000.0
MAGIC = 12582912.0
TWO_PI = 2.0 * math.pi


class Buf:
    __slots__ = ("name", "w", "r", "excl")

    def __init__(self, name):
        self.name = name
        self.w = None
        self.r = []
        self.excl = False


class Op:
    __slots__ = ("eng", "fn", "lane", "seq", "waits", "signal", "val")

    def __init__(self, eng, fn, lane):
        self.eng = eng
        self.fn = fn
        self.lane = lane
        self.seq = 0
        self.waits = []
        self.signal = False
        self.val = 0


EPOCH = 20000


class Sched:
    ENGS = ("pe", "act", "dve", "pool", "sp")

    def __init__(self):
        self.ops = {e: [] for e in self.ENGS}
        self.cnt = {e: 0 for e in self.ENGS}
        self.waited = {e: {} for e in self.ENGS}
        self.lane_cnt = {}
        self.lane_last = {}

    def add(self, eng, fn, reads=(), writes=(), lane=None):
        op = Op(eng, fn, lane)
        deps = []
        for b in reads:
            if b.w is not None:
                deps.append(b.w)
            if b.excl:
                deps.extend(r_ for r_ in b.r if r_.eng != eng)
        for b in writes:
            if b.w is not None:
                deps.append(b.w)
            deps.extend(b.r)
        if lane is not None:
            if lane in self.lane_last:
                deps.append(self.lane_last[lane])
            self.lane_cnt[lane] = self.lane_cnt.get(lane, 0) + 1
            op.seq = self.lane_cnt[lane]
            self.lane_last[lane] = op
        else:
            self.cnt[eng] += 1
            op.seq = self.cnt[eng]
        need = {}
        for d in deps:
            if d is op:
                continue
            key = ("L", d.lane) if d.lane is not None else ("E", d.eng)
            if d.lane is None and d.eng == eng and eng == "pe":
                continue
            if key not in need or d.seq > need[key].seq:
                need[key] = d
        for key, d in need.items():
            if self.waited[eng].get(key, 0) >= d.seq:
                continue
            self.waited[eng][key] = d.seq
            d.signal = True
            op.waits.append(d)
        for b in reads:
            b.r.append(op)
        for b in writes:
            b.w = op
            b.r = []
        self.ops[eng].append(op)
        return op

    def emit(self, nc, es):
        nsem = {}
        for e in self.ENGS:
            c = 0
            for op in self.ops[e]:
                if op.lane is None and op.signal:
                    c += 1
                    op.val = c
            nsem[e] = max(1, (c + EPOCH - 1) // EPOCH)
        esem = {e: [es.enter_context(nc.semaphore("s_%s_%d" % (e, i))) for i in range(nsem[e])]
                for e in self.ENGS}
        lsem = {l: es.enter_context(nc.semaphore("l_%s" % l)) for l in self.lane_cnt}

        def semval(d):
            if d.lane is not None:
                return lsem[d.lane], 16 * d.seq
            i = (d.val - 1) // EPOCH
            return esem[d.eng][i], d.val - i * EPOCH

        def run(engname, eobj):
            for op in self.ops[engname]:
                for d in op.waits:
                    s, v = semval(d)
                    eobj.wait_ge(s, v)
                ins = op.fn(eobj)
                if op.lane is not None:
                    ins.then_inc(lsem[op.lane], 16)
                elif op.signal:
                    s, _ = semval(op)
                    ins.then_inc(s, 1)

        with nc.Block() as block:
            @block.tensor
            def _(e):
                run("pe", e)

            @block.scalar
            def _(e):
                run("act", e)

            @block.vector
            def _(e):
                run("dve", e)

            @block.gpsimd
            def _(e):
                run("pool", e)

            @block.sync
            def _(e):
                run("sp", e)


class T:
    def __init__(self, t, n=1, name="t"):
        self.t = t
        self.bs = [Buf("%s%d" % (name, i)) for i in range(n)]

    @property
    def b(self):
        return self.bs[0]


class Rot:
    def __init__(self, items):
        self.items = items
        self.i = 0

    def get(self):
        x = self.items[self.i % len(self.items)]
        self.i += 1
        return x


def build(nblk):
    seq = 128 * nblk
    nc = bass.Bass("TRN2", target_bir_lowering=False)
    S = Sched()

    def din(name, shape, dt=F32):
        return nc.dram_tensor(name, list(shape), dt, kind="ExternalInput").ap()

    def dout(name, shape):
        return nc.dram_tensor(name, list(shape), F32, kind="ExternalOutput").ap()

    x_p = din("x_p", [seq, D])
    x_s = din("x_s", [16, D])
    st_conv = din("st_conv", [128, 24, 3])
    st_delta = din("st_delta", [H, 128, 128])
    st_re = din("st_re", [128, 32])
    st_im = din("st_im", [128, 32])
    meta = din("meta", [NMETA, D])
    lnp = din("lnp", [6, D])
    w_in_s = din("w_in_s", [16, 128, 4096])
    w_in_z = din("w_in_z", [4, 128, 4096])
    w_in_ab = din("w_in_ab", [128, KC * 16])
    cw_d = din("cw", [128, 24 * 4])
    dnp = din("dnp", [3, 128])
    s5a = din("s5a", [3, 64, 64])
    bpadT = din("bpadT", [2, 32, 128, 128])
    cpad = din("cpad", [2, 32, 128, 128])
    s5d = din("s5d", [128, 8])
    bglu = din("bglu", [128, 8])
    w_glu = din("w_glu", [2, 128, 4096])
    w_out = din("w_out", [8, 128, 4096])
    w_gate = din("w_gate", [22, 128, 4096])
    w_up = din("w_up", [22, 128, 4096])
    w_down = din("w_down", [8, 4, 128, 11 * 256])
    cmat = din("cmat", [128, 5 * 128])

    y_p = dout("y_p", [seq, D])
    y_s = dout("y_s", [16, D])
    o_conv_p = dout("o_conv_p", [128, 24, 3])
    o_delta_p = dout("o_delta_p", [H, 128, 128])
    o_re_p = dout("o_re_p", [128, 32])
    o_im_p = dout("o_im_p", [128, 32])
    o_conv_s = dout("o_conv_s", [128, 24, 3])
    o_delta_s = dout("o_delta_s", [H, 128, 128])
    o_re_s = dout("o_re_s", [128, 32])
    o_im_s = dout("o_im_s", [128, 32])
    s5scr = nc.dram_tensor("s5scr", [4, 64, 64], F32, kind="Internal").ap()
    s5w2 = nc.dram_tensor("s5w2", [32, 128, 4096], BF16, kind="Internal").ap()
    kscr = nc.dram_tensor("kscr", [8, 128, 1024], BF16, kind="Internal").ap()
    def dscr(name, shape):
        return nc.dram_tensor(name, list(shape), BF16, kind="Internal").ap()
    c_in_s = dscr("c_in_s", [16, 128, 4096])
    c_in_z = dscr("c_in_z", [4, 128, 4096])
    c_glu = dscr("c_glu", [2, 128, 4096])
    c_out = dscr("c_out", [8, 128, 4096])
    c_gate = dscr("c_gate", [22, 128, 4096])
    c_up = dscr("c_up", [22, 128, 4096])
    c_down = dscr("c_down", [8, 4, 128, 11 * 256])
    wsbuf = Buf("wscratch")
    outbuf = Buf("dram_out")
    scrbuf = Buf("s5scr")
    swbuf = Buf("s5w")

    with ExitStack() as es:
        def sb(name, shape, dt=F32, n=1):
            return T(es.enter_context(nc.sbuf_tensor("sb_" + name, list(shape), dt)), n, name)

        def psum(name, shape, dt=F32, n=1):
            t_ = T(es.enter_context(nc.psum_tensor("ps_" + name, list(shape), dt)), n, name)
            for b_ in t_.bs:
                b_.excl = True
            return t_

        cm = sb("cm", [128, 5, 128])
        identb = sb("identb", [128, 128], BF16)
        ident = cm.t[:, 0, :]
        ones = cm.t[:, 1, :]
        negmT = cm.t[:, 2, :]
        strict = cm.t[:, 3, :]
        uincl = cm.t[:, 4, :]
        S.add("sp", lambda e: e.dma_start(out=cm.t[:].rearrange("p a b -> p (a b)"), in_=cmat[:, :]),
              writes=[cm.b], lane="c0")
        S.add("act", lambda e: e.activation(out=identb.t[:], in_=ident, func=AF.Copy),
              reads=[cm.b], writes=[identb.b])
        dn = sb("dn", [128, 3, 128])
        S.add("sp", lambda e: e.dma_start(out=dn.t[:], in_=dnp.partition_broadcast(128)),
              writes=[dn.b], lane="c0")
        negA = sb("negA", [128, 8])
        S.add("act", lambda e: e.activation(out=negA.t[:], in_=dn.t[:, 0, 0:8], func=AF.Exp),
              reads=[dn.b], writes=[negA.b])
        S.add("dve", lambda e: e.tensor_scalar_mul(out=negA.t[:], in0=negA.t[:], scalar1=-1.0),
              reads=[negA.b], writes=[negA.b])
        dtb = dn.t[:, 1, 0:8]
        normw = dn.t[:, 2, :]
        cw = sb("cw", [128, 24, 4])
        S.add("sp", lambda e: e.dma_start(out=cw.t[:].rearrange("p a b -> p (a b)"), in_=cw_d[:, :]),
              writes=[cw.b], lane="c0")
        d5 = sb("d5", [128, 8])
        hbg = sb("hbg", [128, 8])
        S.add("sp", lambda e: e.dma_start(out=d5.t[:], in_=s5d[:, :]), writes=[d5.b], lane="c0")
        S.add("sp", lambda e: e.dma_start(out=hbg.t[:], in_=bglu[:, :]), writes=[hbg.b], lane="c0")
        S.add("dve", lambda e: e.tensor_scalar_mul(out=hbg.t[:], in0=hbg.t[:], scalar1=0.5),
              reads=[hbg.b], writes=[hbg.b])
        mhalf = sb("mhalf", [128, 8])
        S.add("pool", lambda e: e.memset(mhalf.t[:], -0.5), writes=[mhalf.b])

        mm = [psum("mm%d" % i, [128, 512]) for i in range(4)]
        tp = [psum("tp%d" % i, [128, 8, 128], BF16) for i in range(2)]
        gp_t = [psum("gp%d" % i, [128, 4, 128], F32, 4) for i in range(2)]
        mmr = Rot(mm)
        tpr = Rot(tp)
        gpr = Rot([(gp_t[i].t[:, j, :], gp_t[i].bs[0]) for j in range(4) for i in range(2)])

        a3 = sb("a3", [64, 3, 64])
        S.add("sp", lambda e: e.dma_start(out=a3.t[:], in_=s5a.rearrange("a g p -> g a p")),
              writes=[a3.b], lane="c0")
        s5t = sb("s5t", [64, 12, 64])
        tb = s5t.b

        def s5op(eng, fn):
            S.add(eng, fn, reads=[a3.b, tb], writes=[tb])
        st = s5t.t
        s5op("act", lambda e: e.activation(out=st[:, 0, :], in_=a3.t[:, 2, :], func=AF.Exp))
        s5op("dve", lambda e: e.tensor_tensor(out=st[:, 1, :], in0=a3.t[:, 0, :], in1=st[:, 0, :], op=ALU.mult))
        s5op("dve", lambda e: e.tensor_tensor(out=st[:, 2, :], in0=a3.t[:, 1, :], in1=st[:, 0, :], op=ALU.mult))
        s5op("act", lambda e: e.activation(out=st[:, 3, :], in_=st[:, 1, :], func=AF.Exp))
        for (dst, shift) in ((5, 0.0), (6, math.pi / 2)):
            s5op("dve", lambda e, shift=shift: e.tensor_scalar(out=st[:, 11, :], in0=st[:, 2, :], scalar1=shift,
                                                              scalar2=None, op0=ALU.add))
            s5op("dve", lambda e: e.tensor_scalar(out=st[:, 4, :], in0=st[:, 11, :], scalar1=1.0 / TWO_PI,
                                                  scalar2=MAGIC, op0=ALU.mult, op1=ALU.add))
            s5op("dve", lambda e: e.tensor_scalar(out=st[:, 4, :], in0=st[:, 4, :], scalar1=-MAGIC,
                                                  scalar2=-TWO_PI, op0=ALU.add, op1=ALU.mult))
            s5op("dve", lambda e: e.tensor_tensor(out=st[:, 4, :], in0=st[:, 4, :], in1=st[:, 11, :], op=ALU.add))
            s5op("act", lambda e, dst=dst: e.activation(out=st[:, dst, :], in_=st[:, 4, :], func=AF.Sin))
        s5op("dve", lambda e: e.tensor_tensor(out=st[:, 7, :], in0=st[:, 3, :], in1=st[:, 6, :], op=ALU.mult))
        s5op("dve", lambda e: e.tensor_tensor(out=st[:, 8, :], in0=st[:, 3, :], in1=st[:, 5, :], op=ALU.mult))
        s5op("dve", lambda e: e.tensor_tensor(out=st[:, 11, :], in0=a3.t[:, 0, :], in1=a3.t[:, 0, :], op=ALU.mult))
        s5op("dve", lambda e: e.tensor_tensor(out=st[:, 0, :], in0=a3.t[:, 1, :], in1=a3.t[:, 1, :], op=ALU.mult))
        s5op("dve", lambda e: e.tensor_tensor(out=st[:, 11, :], in0=st[:, 11, :], in1=st[:, 0, :], op=ALU.add))
        s5op("dve", lambda e: e.reciprocal(out=st[:, 11, :], in_=st[:, 11, :]))
        s5op("dve", lambda e: e.tensor_scalar(out=st[:, 4, :], in0=st[:, 7, :], scalar1=-1.0, scalar2=None, op0=ALU.add))
        s5op("dve", lambda e: e.tensor_tensor(out=st[:, 9, :], in0=st[:, 4, :], in1=a3.t[:, 0, :], op=ALU.mult))
        s5op("dve", lambda e: e.tensor_tensor(out=st[:, 0, :], in0=st[:, 8, :], in1=a3.t[:, 1, :], op=ALU.mult))
        s5op("dve", lambda e: e.tensor_tensor(out=st[:, 9, :], in0=st[:, 9, :], in1=st[:, 0, :], op=ALU.add))
        s5op("dve", lambda e: e.tensor_tensor(out=st[:, 9, :], in0=st[:, 9, :], in1=st[:, 11, :], op=ALU.mult))
        s5op("dve", lambda e: e.tensor_tensor(out=st[:, 10, :], in0=st[:, 8, :], in1=a3.t[:, 0, :], op=ALU.mult))
        s5op("dve", lambda e: e.tensor_tensor(out=st[:, 0, :], in0=st[:, 4, :], in1=a3.t[:, 1, :], op=ALU.mult))
        s5op("dve", lambda e: e.tensor_tensor(out=st[:, 10, :], in0=st[:, 10, :], in1=st[:, 0, :], op=ALU.subtract))
        s5op("dve", lambda e: e.tensor_tensor(out=st[:, 10, :], in0=st[:, 10, :], in1=st[:, 11, :], op=ALU.mult))
        S.add("sp", lambda e: e.dma_start(out=s5scr.rearrange("a g p -> g a p"), in_=st[:, 7:11, :]),
              reads=[tb], writes=[scrbuf], lane="c0")
        Sst = sb("Sst", [128, H, 128], F32, H)
        Sbf = sb("Sbf", [128, H, 128], BF16, H)
        hist = sb("hist", [128, 24, 3])
        xprev = sb("xprev", [128, 2, 32])

        NT = 256
        hb4 = [sb("h%d" % i, [128, D]) for i in range(4)]
        hb16 = sb("hb16", [128, D], BF16)
        lnbc = sb("lnbc", [128, 2, D])
        actTs = [sb("actT%d" % i, [128, KC, NT], BF16) for i in range(2)]
        mixT = sb("mixT", [128, KC, NT], BF16)
        big = sb("big", [128, FKC, NT], BF16, FKC)
        Wsl = [sb("W%d" % i, [128, 4096], BF16) for i in range(4)]
        wab = sb("wab", [128, KC, 16], BF16)
        wr = Rot(Wsl)
        wlane = Rot(["w0", "w1", "w2", "w3"])
        stats_l = [sb("stats%d" % i, [128, 4, 6]) for i in range(2)]
        mv_l = [sb("mv%d" % i, [128, 4]) for i in range(2)]
        ln_i = [0]
        cbufs = Rot([sb("cb%d" % i, [128, NT + 3]) for i in range(2)])
        caccs = Rot([sb("ca%d" % i, [128, NT]) for i in range(2)])
        absb = sb("absb", [128, 2, 16], F32, 2)
        gsm = sb("gsm", [128, 12, 8])
        gtmp = Rot([sb("gt%d" % i, [128, 128]) for i in range(14)])
        gtb = Rot([sb("gb%d" % i, [128, 128], BF16) for i in range(4)])
        ktm = sb("ktm", [128, H, 128], BF16, H)
        kdtm = sb("kdtm", [128, H, 128], BF16, H)
        vtm = sb("vtm", [128, H, 128], BF16, H)
        kTn = sb("kTn", [128, H, 128], BF16, H)
        qTn = sb("qTn", [128, H, 128], BF16, H)
        otm = sb("otm", [128, H, 128], F32, H)
        ontm = sb("ontm", [128, H, 128], BF16, H)
        nrm = sb("nrm", [128, 3, 8], F32, 3)
        gz = [sb("gz%d" % i, [128, 1024], BF16) for i in range(2)]
        xs = [sb("xs%d" % i, [128, 32, 33]) for i in range(2)]
        xq = [sb("xq%d" % i, [128, 4, 32], F32, 4) for i in range(2)]
        xb = sb("xb", [128, 2, 32, 32], BF16)
        s5ws = [sb("s5ws%d" % i, [128, 16, 128], BF16) for i in range(2)]
        s5ks = [sb("s5ks%d" % i, [128, 8, 128], BF16) for i in range(2)]
        sgt = Rot([sb("sg%d" % i, [128, NT]) for i in range(3)])

        pwv = lnbc.t[:].rearrange("p a d -> p (a d)")[:, 0:4 * 17 * 32].rearrange("p (c k j) -> p c k j", c=4, k=17)
        pwb = lnbc.b
        ct = sb("ct", [128, 2, 32])
        l8 = sb("l8", [128, 1, 2, 32])
        with nc.allow_non_contiguous_dma(reason="tiny s5 param reshuffle"):
            for c in range(2):
                S.add("sp", lambda e, c=c: e.dma_start(
                    out=pwv[:, c, 1, :], in_=s5scr[c].rearrange("(j g) p -> (g p) j", g=2)),
                    reads=[scrbuf], writes=[pwb], lane="c0")
                S.add("sp", lambda e, c=c: e.dma_start(
                    out=pwv[:, c, 9, :], in_=s5scr[2 + c].rearrange("(j g) p -> (g p) j", g=2)),
                    reads=[scrbuf], writes=[pwb], lane="c0")
        S.add("dve", lambda e: e.memset(pwv[:, 0, 0, :], 1.0), writes=[pwb])
        S.add("dve", lambda e: e.memset(pwv[:, 1, 0, :], 0.0), writes=[pwb])

        def cmul(dr, di, ar, ai, br, bi, db, sbs):
            rd = sbs + [ct.b]
            S.add("dve", lambda e: e.tensor_tensor(out=ct.t[:, 0, :], in0=ar, in1=br, op=ALU.mult), reads=rd, writes=[ct.b])
            S.add("dve", lambda e: e.tensor_tensor(out=ct.t[:, 1, :], in0=ai, in1=bi, op=ALU.mult), reads=rd, writes=[ct.b])
            S.add("dve", lambda e: e.tensor_tensor(out=dr, in0=ct.t[:, 0, :], in1=ct.t[:, 1, :], op=ALU.subtract), reads=rd + [db], writes=[db])
            S.add("dve", lambda e: e.tensor_tensor(out=ct.t[:, 0, :], in0=ar, in1=bi, op=ALU.mult), reads=rd + [db], writes=[ct.b])
            S.add("dve", lambda e: e.tensor_tensor(out=ct.t[:, 1, :], in0=ai, in1=br, op=ALU.mult), reads=rd + [db], writes=[ct.b])
            S.add("dve", lambda e: e.tensor_tensor(out=di, in0=ct.t[:, 0, :], in1=ct.t[:, 1, :], op=ALU.add), reads=rd + [db], writes=[db])
        for k in range(1, 8):
            cmul(pwv[:, 0, k + 1, :], pwv[:, 1, k + 1, :], pwv[:, 0, k, :], pwv[:, 1, k, :], pwv[:, 0, 1, :], pwv[:, 1, 1, :], pwb, [pwb])
            cmul(pwv[:, 0, 9 + k, :], pwv[:, 1, 9 + k, :], pwv[:, 0, k, :], pwv[:, 1, k, :], pwv[:, 0, 9, :], pwv[:, 1, 9, :], pwb, [pwb])
        S.add("dve", lambda e: e.tensor_scalar_mul(out=pwv[:, 2:4, :, :], in0=pwv[:, 0:2, :, :], scalar1=-1.0), reads=[pwb], writes=[pwb])
        S.add("dve", lambda e: e.tensor_copy(out=l8.t[:, 0, :, :], in_=pwv[:, 0:2, 8, :]), reads=[pwb], writes=[l8.b])
        hv0 = hb4[0].t[:, :]
        hv1 = hb4[1].t[:, :]
        ldt = [T(hv1[:, i * 512:(i + 1) * 512].rearrange("p (a c) -> p a c", a=4), 1, "ld%d" % i) for i in range(2)]
        ncit = [T(hv1[:, 1024 + i * 128:1024 + (i + 1) * 128], 1, "nci%d" % i) for i in range(2)]
        wtt = Rot([T(hv1[:, 1280 + i * 128:1280 + (i + 1) * 128], 1, "wt%d" % i) for i in range(6)] + [T(hv0[:, 1024 + i * 128:1024 + (i + 1) * 128], 1, "wu%d" % i) for i in range(8)])
        kacc = T(hv0[:, 0:1024].rearrange("p (a c) -> p a c", a=8), 1, "kacc")
        kbf = T(hb16.t[:, 0:1024].rearrange("p (a c) -> p a c", a=8), 1, "kbf")
        spr = Rot([(gp_t[0].t[:, 0, :], gp_t[0].bs[0]), (mm[2].t[:, 0:128], mm[2].b), (gp_t[1].t[:, 0, :], gp_t[1].bs[0]), (mm[3].t[:, 0:128], mm[3].b)])
        for j in range(32):
            ld = ldt[j % 2]
            nci = ncit[j % 2]
            wst = s5ws[0]
            vst = s5ws[1]
            S.add("sp", lambda e, ld=ld, j=j: e.dma_start(out=ld.t[:, 0:2, :], in_=bpadT[:, j].rearrange("a p c -> p a c")),
                  writes=[ld.b], lane="b%d" % (j % 2))
            S.add("sp", lambda e, ld=ld, j=j: e.dma_start(out=ld.t[:, 2:4, :], in_=cpad[:, j].rearrange("a p c -> p a c")),
                  writes=[ld.b], lane="b%d" % (j % 2))
            S.add("act", lambda e, ld=ld, nci=nci: e.activation(out=nci.t[:, :], in_=ld.t[:, 3, :], func=AF.Copy, scale=-1.0),
                  reads=[ld.b], writes=[nci.b])
            Br, Bi, Cr, Ci = ld.t[:, 0, :], ld.t[:, 1, :], ld.t[:, 2, :], ld.t[:, 3, :]
            for k in range(8):
                gr, gi, ngi = pwv[:, 0, 9 + k, j:j + 1], pwv[:, 1, 9 + k, j:j + 1], pwv[:, 3, 9 + k, j:j + 1]
                wtr = wtt.get()
                wti = wtt.get()
                S.add("act", lambda e, wtr=wtr, Br=Br, gr=gr: e.activation(out=wtr.t[:, :], in_=Br, func=AF.Copy, scale=gr),
                      reads=[ld.b, pwb], writes=[wtr.b])
                S.add("dve", lambda e, wtr=wtr, Bi=Bi, ngi=ngi: e.scalar_tensor_tensor(out=wtr.t[:, :], in0=Bi, scalar=ngi, in1=wtr.t[:, :], op0=ALU.mult, op1=ALU.add),
                      reads=[ld.b, pwb, wtr.b], writes=[wtr.b])
                S.add("act", lambda e, wti=wti, Br=Br, gi=gi: e.activation(out=wti.t[:, :], in_=Br, func=AF.Copy, scale=gi),
                      reads=[ld.b, pwb], writes=[wti.b])
                S.add("dve", lambda e, wti=wti, Bi=Bi, gr=gr: e.scalar_tensor_tensor(out=wti.t[:, :], in0=Bi, scalar=gr, in1=wti.t[:, :], op0=ALU.mult, op1=ALU.add),
                      reads=[ld.b, pwb, wti.b], writes=[wti.b])
                kp = mm[k // 4]
                S.add("pe", lambda e, kp=kp, k=k, wtr=wtr, Cr=Cr: e.matmul(kp.t[:, (k % 4) * 128:(k % 4 + 1) * 128], lhsT=wtr.t[:, :], rhs=Cr, start=True, stop=False),
                      reads=[wtr.b, ld.b], writes=[kp.b])
                S.add("pe", lambda e, kp=kp, k=k, wti=wti, nci=nci: e.matmul(kp.t[:, (k % 4) * 128:(k % 4 + 1) * 128], lhsT=wti.t[:, :], rhs=nci.t[:, :], start=False, stop=True),
                      reads=[wti.b, nci.b], writes=[kp.b])
                sp_ = 7 - k
                for c, wsrc in ((0, wtr), (1, wti)):
                    pt_, ptb = spr.get()
                    S.add("pe", lambda e, pt_=pt_, wsrc=wsrc: e.transpose(out=pt_[:, :], in_=wsrc.t[:, :], identity=ident),
                          reads=[wsrc.b, cm.b], writes=[ptb])
                    S.add("act", lambda e, pt_=pt_, sp_=sp_, c=c, wst=wst: e.activation(out=wst.t[:, sp_ * 2 + c, :], in_=pt_[:, :], func=AF.Copy),
                          reads=[ptb], writes=[wst.b])
                ar, ai, nar, nai = pwv[:, 0, k + 1, j:j + 1], pwv[:, 1, k + 1, j:j + 1], pwv[:, 2, k + 1, j:j + 1], pwv[:, 3, k + 1, j:j + 1]
                v1 = wtt.get()
                S.add("act", lambda e, v1=v1, Cr=Cr, ar=ar: e.activation(out=v1.t[:, :], in_=Cr, func=AF.Copy, scale=ar),
                      reads=[ld.b, pwb], writes=[v1.b])
                S.add("dve", lambda e, v1=v1, Ci=Ci, nai=nai, k=k, vst=vst: e.scalar_tensor_tensor(out=vst.t[:, k * 2, :], in0=Ci, scalar=nai, in1=v1.t[:, :], op0=ALU.mult, op1=ALU.add),
                      reads=[ld.b, pwb, v1.b], writes=[vst.b])
                v2 = wtt.get()
                S.add("act", lambda e, v2=v2, Cr=Cr, nai=nai: e.activation(out=v2.t[:, :], in_=Cr, func=AF.Copy, scale=nai),
                      reads=[ld.b, pwb], writes=[v2.b])
                S.add("dve", lambda e, v2=v2, Ci=Ci, nar=nar, k=k, vst=vst: e.scalar_tensor_tensor(out=vst.t[:, k * 2 + 1, :], in0=Ci, scalar=nar, in1=v2.t[:, :], op0=ALU.mult, op1=ALU.add),
                      reads=[ld.b, pwb, v2.b], writes=[vst.b])
            for half in range(2):
                kp = mm[half]
                if j % 4 == 0:
                    S.add("act", lambda e, kp=kp, half=half: e.activation(out=kacc.t[:, half * 4:half * 4 + 4, :], in_=kp.t[:, :].rearrange("p (a c) -> p a c", a=4), func=AF.Copy),
                          reads=[kp.b], writes=[kacc.b])
                else:
                    S.add("dve", lambda e, kp=kp, half=half: e.tensor_tensor(out=kacc.t[:, half * 4:half * 4 + 4, :], in0=kp.t[:, :].rearrange("p (a c) -> p a c", a=4),
                                                                          in1=kacc.t[:, half * 4:half * 4 + 4, :], op=ALU.add),
                          reads=[kp.b, kacc.b], writes=[kacc.b])
            if j % 4 == 3:
                S.add("act", lambda e: e.activation(out=kbf.t[:, :, :], in_=kacc.t[:, :, :], func=AF.Copy), reads=[kacc.b], writes=[kbf.b])
                S.add("sp", lambda e, j=j: e.dma_start(out=kscr[j // 4], in_=kbf.t[:, :, :].rearrange("p a c -> p (a c)")),
                      reads=[kbf.b], writes=[swbuf], lane="b%d" % (j % 2))
            S.add("sp", lambda e, j=j, wst=wst: e.dma_start(out=s5w2[j, :, 0:2048], in_=wst.t[:].rearrange("p a c -> p (a c)")),
                  reads=[wst.b], writes=[swbuf], lane="b%d" % (j % 2))
            S.add("sp", lambda e, j=j, vst=vst: e.dma_start(out=s5w2[j, :, 2048:4096], in_=vst.t[:].rearrange("p a c -> p (a c)")),
                  reads=[vst.b], writes=[swbuf], lane="b%d" % (j % 2))
        S.add("pool", lambda e: e.memset(ct.t[:], 0.0),
              reads=[b_.b for b_ in ldt + ncit + wtt.items + [kacc, kbf]] + [pwb],
              writes=[hb4[0].b, hb4[1].b, hb16.b, lnbc.b, ct.b])
        def load_w(src_ap, nel):
            w = wr.get()
            S.add("sp", lambda e, w=w, src_ap=src_ap, nel=nel: e.dma_start(out=w.t[:, 0:nel], in_=src_ap),
                  reads=[wsbuf], writes=[w.b], lane=wlane.get())
            return w

        def convert_w(src_ap, dst_ap, nel):
            w = wr.get()
            S.add("pool", lambda e, w=w: e.dma_start(out=w.t[:, 0:nel], in_=src_ap), writes=[w.b], lane=cwl.get())
            S.add("sp", lambda e, w=w: e.dma_start(out=dst_ap, in_=w.t[:, 0:nel]), reads=[w.b, wsbuf], lane=cvl.get())

        cvl = Rot(["cv0", "cv1", "cv2", "cv3"])
        cwl = Rot(["cw0", "cw1", "cw2", "cw3"])
        S.add("pool", lambda e: e.dma_start(out=wab.t[:].rearrange("p a b -> p (a b)"), in_=w_in_ab[:, :]),
              writes=[wab.b], lane="wab")
        for i in range(16):
            convert_w(w_in_s[i], c_in_s[i], 4096)
        for i in range(4):
            convert_w(w_in_z[i], c_in_z[i], 4096)
        for i in range(2):
            convert_w(w_glu[i], c_glu[i], 4096)
        for i in range(8):
            convert_w(w_out[i], c_out[i], 4096)
        for i in range(22):
            convert_w(w_gate[i], c_gate[i], 4096)
            convert_w(w_up[i], c_up[i], 4096)
        for i in range(8):
            for k in range(4):
                convert_w(w_down[i, k], c_down[i, k], 11 * 256)
        S.add("sp", lambda e: e.nop(), writes=[wsbuf])

        ln_cur = [-1]

        def layer_norm(hb, n, gi):
            stats = stats_l[ln_i[0] % 2]
            mv = mv_l[ln_i[0] % 2]
            ln_i[0] += 1
            if ln_cur[0] != gi:
                ln_cur[0] = gi
                S.add("sp", lambda e: e.dma_start(out=lnbc.t[:].rearrange("p a d -> p (a d)"),
                                                  in_=lnp[gi:gi + 2, :].rearrange("a d -> (a d)").partition_broadcast(128)),
                      writes=[lnbc.b], lane="ln")
            for c in range(4):
                S.add("dve", lambda e, c=c: e.bn_stats(out=stats.t[:n, c, :], in_=hb.t[:n, c * 512:(c + 1) * 512]),
                      reads=[hb.b], writes=[stats.b])
            S.add("dve", lambda e: e.bn_aggr(out=mv.t[:n, 0:2], in_=stats.t[:n].rearrange("p a b -> p (a b)")),
                  reads=[stats.b], writes=[mv.b])
            S.add("dve", lambda e: e.tensor_scalar(out=mv.t[:n, 2:3], in0=mv.t[:n, 1:2], scalar1=LN_EPS, scalar2=None, op0=ALU.add),
                  reads=[mv.b], writes=[mv.b])
            S.add("pool", lambda e: e.tensor_tensor(out=mv.t[:n, 2:3], in0=mv.t[:n, 2:3], in1=mhalf.t[:n, 0:1], op=ALU.pow),
                  reads=[mv.b, mhalf.b], writes=[mv.b])
            S.add("dve", lambda e: e.scalar_tensor_tensor(out=mv.t[:n, 3:4], in0=mv.t[:n, 0:1], scalar=-1.0, in1=mv.t[:n, 2:3],
                                                          op0=ALU.mult, op1=ALU.mult),
                  reads=[mv.b], writes=[mv.b])
            S.add("act", lambda e: e.activation(out=hb.t[:n, :], in_=hb.t[:n, :], func=AF.Identity,
                                                scale=mv.t[:n, 2:3], bias=mv.t[:n, 3:4]),
                  reads=[hb.b, mv.b], writes=[hb.b])
            S.add("dve", lambda e: e.tensor_tensor(out=hb.t[:n, :], in0=hb.t[:n, :], in1=lnbc.t[:n, 0, :], op=ALU.mult),
                  reads=[hb.b, lnbc.b], writes=[hb.b])
            S.add("dve", lambda e: e.tensor_tensor(out=hb.t[:n, :], in0=hb.t[:n, :], in1=lnbc.t[:n, 1, :], op=ALU.add),
                  reads=[hb.b, lnbc.b], writes=[hb.b])

        def to_actT(hb, n, c0, actT):
            S.add("act", lambda e: e.activation(out=hb16.t[:n, :], in_=hb.t[:n, :], func=AF.Copy),
                  reads=[hb.b], writes=[hb16.b])
            for half in range(2):
                p = tpr.get()
                for k in range(8):
                    kc = half * 8 + k
                    S.add("pe", lambda e, p=p, k=k, kc=kc: e.transpose(out=p.t[:, k, 0:n], in_=hb16.t[:n, kc * 128:(kc + 1) * 128],
                                                                       identity=identb.t[:n, :n]),
                          reads=[hb16.b, identb.b], writes=[p.b])
                S.add("act", lambda e, p=p, half=half: e.activation(out=actT.t[:, half * 8:half * 8 + 8, c0:c0 + n],
                                                                    in_=p.t[:, :, 0:n], func=AF.Copy),
                      reads=[p.b], writes=[actT.b])

        def pe_T(dst_ap, dst_b, src_ap, src_b, np_, nf, evac="act", scale=None, scale_b=None):
            p = tpr.get()
            S.add("pe", lambda e: e.transpose(out=p.t[:nf, 0, 0:np_], in_=src_ap, identity=identb.t[:np_, :np_]),
                  reads=[src_b, identb.b], writes=[p.b])
            S.add("act", lambda e: e.activation(out=dst_ap, in_=p.t[:nf, 0, 0:np_], func=AF.Copy),
                  reads=[p.b], writes=[dst_b])

        def rsq(dst, src, n, b, mul, add):
            S.add("dve", lambda e: e.tensor_scalar(out=dst, in0=src, scalar1=mul, scalar2=add, op0=ALU.mult, op1=ALU.add),
                  reads=[b], writes=[b])
            S.add("pool", lambda e: e.tensor_tensor(out=dst, in0=dst, in1=mhalf.t[:n, :], op=ALU.pow),
                  reads=[b, mhalf.b], writes=[b])

        def phase_A(k, blocks):
            hbuf = [hb4[(2 * k) % 4], hb4[(2 * k + 1) % 4]]
            actT = actTs[k % 2]
            c = 0
            for bi, (n, rows, orow) in enumerate(blocks):
                hb = hbuf[bi]
                for (r0, nr, src) in rows:
                    S.add("sp", lambda e, hb=hb, r0=r0, nr=nr, src=src: e.dma_start(out=hb.t[r0:r0 + nr, :], in_=src),
                          writes=[hb.b], lane="x%d" % ((2 * k + bi) % 4))
                yield
                layer_norm(hb, n, 0)
                for _ in range(6):
                    yield
                to_actT(hb, n, c, actT)
                yield
                c += n

        def run_tile(k, blocks, prefetch):
            hbuf = [hb4[(2 * k) % 4], hb4[(2 * k + 1) % 4]]
            actT = actTs[k % 2]
            N = sum(b_[0] for b_ in blocks)
            c0s = []
            c = 0
            for b_ in blocks:
                c0s.append(c)
                c += b_[0]
            nb = len(blocks)
            abps = []
            for bi, (n, rows, orow) in enumerate(blocks):
                pa, pb = gpr.get()
                for kc in range(KC):
                    S.add("pe", lambda e, pa=pa, kc=kc, n=n, c0=c0s[bi]: e.matmul(
                        pa[:n, 0:16], lhsT=actT.t[:, kc, c0:c0 + n], rhs=wab.t[:, kc, :], start=(kc == 0), stop=(kc == KC - 1)),
                        reads=[actT.b, wab.b], writes=[pb])
                S.add("act", lambda e, pa=pa, n=n, bi=bi: e.activation(out=absb.t[:n, bi, :], in_=pa[:n, 0:16], func=AF.Copy),
                      reads=[pb], writes=[absb.bs[bi]])
                abps.append((absb.t[:, bi, :], absb.bs[bi]))
            for bt_ in range(16):
                w = load_w(c_in_s[bt_], 4096)
                wv = w.t[:].rearrange("p (m k c) -> p m k c", m=2, k=KC)
                for mi in range(2):
                    mt = bt_ * 2 + mi
                    p = mmr.get()
                    for kc in range(KC):
                        S.add("pe", lambda e, p=p, wv=wv, mi=mi, kc=kc: e.matmul(
                            p.t[:, 0:N], lhsT=wv[:, mi, kc, :], rhs=actT.t[:, kc, 0:N], start=(kc == 0), stop=(kc == KC - 1)),
                            reads=[w.b, actT.b], writes=[p.b])
                    if mt < 24:
                        cb = cbufs.get()
                        ca = caccs.get()
                        hsrc = hist
                        S.add("act", lambda e, cb=cb, p=p: e.activation(out=cb.t[:, 3:3 + N], in_=p.t[:, 0:N], func=AF.Copy),
                              reads=[p.b], writes=[cb.b])
                        S.add("act", lambda e, cb=cb, mt=mt: e.activation(out=cb.t[:, 0:3], in_=hist.t[:, mt, :], func=AF.Copy),
                              reads=[hist.b, cb.b], writes=[cb.b])
                        S.add("act", lambda e, cb=cb, mt=mt: e.activation(out=hist.t[:, mt, :], in_=cb.t[:, N:N + 3], func=AF.Copy),
                              reads=[cb.b], writes=[hist.b])
                        S.add("dve", lambda e, cb=cb, ca=ca, mt=mt: e.tensor_scalar(
                            out=ca.t[:, 0:N], in0=cb.t[:, 0:N], scalar1=cw.t[:, mt, 0:1], scalar2=None, op0=ALU.mult),
                            reads=[cb.b, cw.b], writes=[ca.b])
                        for i in range(1, 4):
                            S.add("dve", lambda e, cb=cb, ca=ca, mt=mt, i=i: e.scalar_tensor_tensor(
                                out=ca.t[:, 0:N], in0=cb.t[:, i:i + N], scalar=cw.t[:, mt, i:i + 1], in1=ca.t[:, 0:N],
                                op0=ALU.mult, op1=ALU.add),
                                reads=[cb.b, cw.b, ca.b], writes=[ca.b])
                        S.add("act", lambda e, ca=ca, mt=mt: e.activation(out=big.t[:, mt, 0:N], in_=ca.t[:, 0:N], func=AF.Silu),
                              reads=[ca.b], writes=[big.bs[mt]])
                    else:
                        S.add("act", lambda e, p=p, mt=mt: e.activation(out=big.t[:, mt, 0:N], in_=p.t[:, 0:N], func=AF.Copy),
                              reads=[p.b], writes=[big.bs[mt]])
            for cg in range(4):
                w = load_w(c_in_z[cg], 4096)
                wv = w.t[:].rearrange("p (k c) -> p k c", k=KC)
                for bi, (n, rows, orow) in enumerate(blocks):
                    p = mmr.get()
                    for kc in range(KC):
                        S.add("pe", lambda e, p=p, wv=wv, kc=kc, n=n, c0=c0s[bi]: e.matmul(
                            p.t[:n, 0:256], lhsT=actT.t[:, kc, c0:c0 + n], rhs=wv[:, kc, :], start=(kc == 0), stop=(kc == KC - 1)),
                            reads=[w.b, actT.b], writes=[p.b])
                    g_ = gz[bi]
                    S.add("act", lambda e, p=p, g_=g_, n=n, cg=cg: e.activation(out=g_.t[:n, cg * 256:(cg + 1) * 256], in_=p.t[:n, 0:256], func=AF.Silu),
                          reads=[p.b], writes=[g_.b])
                    for hh in range(2):
                        S.add("pool", lambda e, g_=g_, n=n, cg=cg, hh=hh: e.tensor_tensor(
                            out=g_.t[:n, cg * 256 + hh * 128:cg * 256 + hh * 128 + 128],
                            in0=g_.t[:n, cg * 256 + hh * 128:cg * 256 + hh * 128 + 128], in1=normw[:n, :], op=ALU.mult),
                            reads=[g_.b, dn.b], writes=[g_.b])

            def gdn_block(bi, n, c0, pa, pb):
                gs = gsm.t
                gb_ = gsm.b
                S.add("dve", lambda e: e.tensor_tensor(out=gs[:n, 0, :], in0=pa[:n, 0:8], in1=dtb[:n, :], op=ALU.add),
                      reads=[pb, dn.b], writes=[gb_])
                S.add("act", lambda e: e.activation(out=gs[:n, 5, :], in_=pa[:n, 8:16], func=AF.Tanh, scale=0.5),
                      reads=[pb], writes=[gb_])
                S.add("act", lambda e: e.activation(out=gs[:n, 0, :], in_=gs[:n, 0, :], func=AF.Exp),
                      reads=[gb_], writes=[gb_])
                S.add("act", lambda e: e.activation(out=gs[:n, 0, :], in_=gs[:n, 0, :], func=AF.Ln, bias=1.0),
                      reads=[gb_], writes=[gb_])
                S.add("dve", lambda e: e.tensor_tensor(out=gs[:n, 0, :], in0=gs[:n, 0, :], in1=negA.t[:n, :], op=ALU.mult),
                      reads=[gb_, negA.b], writes=[gb_])
                S.add("dve", lambda e: e.tensor_scalar(out=gs[:n, 5, :], in0=gs[:n, 5, :], scalar1=0.5, scalar2=0.5, op0=ALU.mult, op1=ALU.add),
                      reads=[gb_], writes=[gb_])
                S.add("dve", lambda e: e.tensor_scalar_mul(out=gs[:n, 6, :], in0=gs[:n, 5, :], scalar1=-1.0),
                      reads=[gb_], writes=[gb_])
                pc, pcb = gpr.get()
                S.add("pe", lambda e, pc=pc: e.matmul(pc[:n, 0:8], lhsT=uincl[:n, :n], rhs=gs[:n, 0, :], start=True, stop=True),
                      reads=[cm.b, gb_], writes=[pcb])
                S.add("pe", lambda e, pc=pc: e.matmul(pc[:, 8:16], lhsT=ones[:n, :], rhs=gs[:n, 0, :], start=True, stop=True),
                      reads=[cm.b, gb_], writes=[pcb])
                S.add("act", lambda e, pc=pc: e.activation(out=gs[:n, 1, :], in_=pc[:n, 0:8], func=AF.Copy),
                      reads=[pcb], writes=[gb_])
                S.add("act", lambda e, pc=pc: e.activation(out=gs[:n, 2, :], in_=pc[:n, 0:8], func=AF.Exp),
                      reads=[pcb], writes=[gb_])
                S.add("act", lambda e, pc=pc: e.activation(out=gs[:, 7, :], in_=pc[:, 8:16], func=AF.Exp),
                      reads=[pcb], writes=[gb_])
                S.add("dve", lambda e, pc=pc: e.tensor_tensor(out=gs[:n, 4, :], in0=pc[:n, 8:16], in1=gs[:n, 1, :], op=ALU.subtract),
                      reads=[pcb, gb_], writes=[gb_])
                S.add("act", lambda e: e.activation(out=gs[:n, 4, :], in_=gs[:n, 4, :], func=AF.Exp),
                      reads=[gb_], writes=[gb_])
                S.add("dve", lambda e: e.tensor_scalar_mul(out=gs[:n, 3, :], in0=gs[:n, 2, :], scalar1=-1.0),
                      reads=[gb_], writes=[gb_])
                for h in range(H):
                    p = tpr.get()
                    S.add("pe", lambda e, p=p, h=h: e.transpose(out=p.t[:n, 0, :], in_=big.t[:, 8 + h, c0:c0 + n], identity=identb.t[:, :]),
                          reads=[big.bs[8 + h], identb.b], writes=[p.b])
                    S.add("pe", lambda e, p=p, h=h: e.transpose(out=p.t[:n, 1, :], in_=big.t[:, h, c0:c0 + n], identity=identb.t[:, :]),
                          reads=[big.bs[h], identb.b], writes=[p.b])
                    S.add("pe", lambda e, p=p, h=h: e.transpose(out=p.t[:n, 2, :], in_=big.t[:, 16 + h, c0:c0 + n], identity=identb.t[:, :]),
                          reads=[big.bs[16 + h], identb.b], writes=[p.b])
                    jk = gtmp.get()
                    S.add("act", lambda e, p=p, h=h, jk=jk: e.activation(out=jk.t[:n, :], in_=p.t[:n, 0, :], func=AF.Square,
                                                                        accum_out=nrm.t[:n, 0, h:h + 1]),
                          reads=[p.b], writes=[jk.b, nrm.bs[0]])
                    jq = gtmp.get()
                    S.add("act", lambda e, p=p, h=h, jq=jq: e.activation(out=jq.t[:n, :], in_=p.t[:n, 1, :], func=AF.Square,
                                                                        accum_out=nrm.t[:n, 1, h:h + 1]),
                          reads=[p.b], writes=[jq.b, nrm.bs[1]])
                    S.add("act", lambda e, p=p, h=h: e.activation(out=vtm.t[:n, h, :], in_=p.t[:n, 2, :], func=AF.Copy),
                          reads=[p.b], writes=[vtm.bs[h]])
                    S.add("act", lambda e, p=p, h=h: e.activation(out=kdtm.t[:n, h, :], in_=p.t[:n, 0, :], func=AF.Copy),
                          reads=[p.b], writes=[kdtm.bs[h]])
                    S.add("act", lambda e, p=p, h=h: e.activation(out=ontm.t[:n, h, :], in_=p.t[:n, 1, :], func=AF.Copy),
                          reads=[p.b], writes=[ontm.bs[h]])
                    yield
                rsq(nrm.t[:n, 0, :], nrm.t[:n, 0, :], n, nrm.bs[0], 1.0, RMS_EPS)
                rsq(nrm.t[:n, 1, :], nrm.t[:n, 1, :], n, nrm.bs[1], 128.0, 128.0 * RMS_EPS)
                for h in range(H):
                    S.add("act", lambda e, h=h: e.activation(out=ktm.t[:n, h, :], in_=kdtm.t[:n, h, :], func=AF.Copy, scale=nrm.t[:n, 0, h:h + 1]),
                          reads=[kdtm.bs[h], nrm.bs[0]], writes=[ktm.bs[h]])
                    S.add("act", lambda e, h=h: e.activation(out=ontm.t[:n, h, :], in_=ontm.t[:n, h, :], func=AF.Copy, scale=nrm.t[:n, 1, h:h + 1]),
                          reads=[ontm.bs[h], nrm.bs[1]], writes=[ontm.bs[h]])
                    S.add("pool", lambda e, h=h: e.tensor_scalar(out=kdtm.t[:n, h, :], in0=ktm.t[:n, h, :], scalar1=gs[:n, 4, h:h + 1],
                                                                 scalar2=None, op0=ALU.mult),
                          reads=[ktm.bs[h], gb_], writes=[kdtm.bs[h]])
                    p = tpr.get()
                    S.add("pe", lambda e, p=p, h=h: e.transpose(out=p.t[:, 0, 0:n], in_=ktm.t[:n, h, :], identity=identb.t[:n, :n]),
                          reads=[ktm.bs[h], identb.b], writes=[p.b])
                    S.add("pe", lambda e, p=p, h=h: e.transpose(out=p.t[:, 1, 0:n], in_=ontm.t[:n, h, :], identity=identb.t[:n, :n]),
                          reads=[ontm.bs[h], identb.b], writes=[p.b])
                    S.add("act", lambda e, p=p, h=h: e.activation(out=kTn.t[:, h, 0:n], in_=p.t[:, 0, 0:n], func=AF.Copy),
                          reads=[p.b], writes=[kTn.bs[h]])
                    S.add("act", lambda e, p=p, h=h: e.activation(out=qTn.t[:, h, 0:n], in_=p.t[:, 1, 0:n], func=AF.Copy),
                          reads=[p.b], writes=[qTn.bs[h]])
                    yield
                def head_chain(h, gpr, gtmp, gtb):
                    kT = kTn.t[:, h, 0:n]
                    qT = qTn.t[:, h, 0:n]
                    yield
                    pkk, pkkb = gpr.get()
                    S.add("pe", lambda e, pkk=pkk, kT=kT: e.matmul(pkk[:n, :n], lhsT=kT, rhs=kT, start=True, stop=True),
                          reads=[kTn.bs[h]], writes=[pkkb])
                    pqk, pqkb = gpr.get()
                    S.add("pe", lambda e, pqk=pqk, kT=kT, qT=qT: e.matmul(pqk[:n, :n], lhsT=kT, rhs=qT, start=True, stop=True),
                          reads=[kTn.bs[h], qTn.bs[h]], writes=[pqkb])
                    dg = gtmp.get()
                    S.add("act", lambda e, dg=dg, h=h: e.activation(out=dg.t[:n, :n], in_=ident[:n, :n], func=AF.Copy, scale=gs[:n, 1, h:h + 1]),
                          reads=[cm.b, gb_], writes=[dg.b])
                    yield
                    pgb, pgbb = gpr.get()
                    S.add("pe", lambda e, pgb=pgb, dg=dg: e.matmul(pgb[:n, :n], lhsT=ones[:n, :n], rhs=dg.t[:n, :n], start=True, stop=True),
                          reads=[cm.b, dg.b], writes=[pgbb])
                    DT = gtmp.get()
                    S.add("dve", lambda e, DT=DT, pgb=pgb, h=h: e.scalar_tensor_tensor(
                        out=DT.t[:n, :n], in0=pgb[:n, :n], scalar=gs[:n, 1, h:h + 1], in1=negmT[:n, :n], op0=ALU.subtract, op1=ALU.add),
                        reads=[pgbb, gb_, cm.b], writes=[DT.b])
                    S.add("act", lambda e, DT=DT: e.activation(out=DT.t[:n, :n], in_=DT.t[:n, :n], func=AF.Exp),
                          reads=[DT.b], writes=[DT.b])
                    QKm = gtb.get()
                    S.add("dve", lambda e, QKm=QKm, pqk=pqk, DT=DT: e.tensor_tensor(out=QKm.t[:n, :n], in0=pqk[:n, :n], in1=DT.t[:n, :n], op=ALU.mult),
                          reads=[pqkb, DT.b], writes=[QKm.b])
                    DTs = gtmp.get()
                    S.add("pool", lambda e, DTs=DTs, DT=DT: e.tensor_tensor(out=DTs.t[:n, :n], in0=DT.t[:n, :n], in1=strict[:n, :n], op=ALU.mult),
                          reads=[DT.b, cm.b], writes=[DTs.b])
                    cur = gtmp.get()
                    S.add("dve", lambda e, cur=cur, pkk=pkk, DTs=DTs, h=h: e.scalar_tensor_tensor(
                        out=cur.t[:n, :n], in0=pkk[:n, :n], scalar=gs[:n, 6, h:h + 1], in1=DTs.t[:n, :n], op0=ALU.mult, op1=ALU.mult),
                        reads=[pkkb, gb_, DTs.b], writes=[cur.b])
                    yield
                    pt_, ptb = gpr.get()
                    S.add("pe", lambda e, pt_=pt_, cur=cur: e.transpose(out=pt_[:n, :n], in_=cur.t[:n, :n], identity=ident[:n, :n]),
                          reads=[cur.b, cm.b], writes=[ptb])
                    curT = gtmp.get()
                    S.add("act", lambda e, curT=curT, pt_=pt_: e.activation(out=curT.t[:n, :n], in_=pt_[:n, :n], func=AF.Copy),
                          reads=[ptb], writes=[curT.b])
                    P = gtmp.get()
                    S.add("dve", lambda e, P=P, cur=cur: e.tensor_tensor(out=P.t[:n, :n], in0=cur.t[:n, :n], in1=ident[:n, :n], op=ALU.add),
                          reads=[cur.b, cm.b], writes=[P.b])
                    nlev = 6 if n == 128 else 3
                    for k in range(1, nlev + 1):
                        nxt = None
                        if k < nlev:
                            yield
                            pn, pnb = gpr.get()
                            S.add("pe", lambda e, pn=pn, cur=cur, curT=curT: e.matmul(pn[:n, :n], lhsT=curT.t[:n, :n], rhs=cur.t[:n, :n], start=True, stop=True),
                                  reads=[cur.b, curT.b], writes=[pnb])
                            nxt = gtmp.get()
                            S.add("act", lambda e, nxt=nxt, pn=pn: e.activation(out=nxt.t[:n, :n], in_=pn[:n, :n], func=AF.Copy),
                                  reads=[pnb], writes=[nxt.b])
                        pn2, pn2b = gpr.get()
                        S.add("pe", lambda e, pn2=pn2, cur=cur, curT=curT: e.matmul(pn2[:n, :n], lhsT=cur.t[:n, :n], rhs=curT.t[:n, :n], start=True, stop=True),
                              reads=[cur.b, curT.b], writes=[pn2b])
                        nxtT = gtmp.get()
                        S.add("act", lambda e, nxtT=nxtT, pn2=pn2: e.activation(out=nxtT.t[:n, :n], in_=pn2[:n, :n], func=AF.Copy),
                              reads=[pn2b], writes=[nxtT.b])
                        yield
                        pp, ppb = gpr.get()
                        S.add("pe", lambda e, pp=pp, nxtT=nxtT, P=P: e.matmul(pp[:n, :n], lhsT=nxtT.t[:n, :n], rhs=P.t[:n, :n], start=True, stop=True),
                              reads=[nxtT.b, P.b], writes=[ppb])
                        P2 = gtmp.get()
                        S.add("dve", lambda e, P2=P2, P=P, pp=pp: e.tensor_tensor(out=P2.t[:n, :n], in0=pp[:n, :n], in1=P.t[:n, :n], op=ALU.add),
                              reads=[ppb, P.b], writes=[P2.b])
                        P = P2
                        cur, curT = nxt, nxtT
                    yield
                    pks, pksb = gpr.get()
                    S.add("pe", lambda e, pks=pks, kT=kT, h=h: e.matmul(pks[:n, :], lhsT=kT, rhs=Sbf.t[:, h, :], start=True, stop=True),
                          reads=[kTn.bs[h], Sbf.bs[h]], writes=[pksb])
                    U = gtmp.get()
                    S.add("dve", lambda e, U=U, pks=pks, h=h: e.scalar_tensor_tensor(
                        out=U.t[:n, :], in0=pks[:n, :], scalar=gs[:n, 3, h:h + 1], in1=vtm.t[:n, h, :], op0=ALU.mult, op1=ALU.add),
                        reads=[pksb, gb_, vtm.bs[h]], writes=[U.b])
                    yield
                    pw, pwb = gpr.get()
                    S.add("pe", lambda e, pw=pw, P=P, U=U: e.matmul(pw[:n, :], lhsT=P.t[:n, :n], rhs=U.t[:n, :], start=True, stop=True),
                          reads=[P.b, U.b], writes=[pwb])
                    wt = gtb.get()
                    S.add("act", lambda e, wt=wt, pw=pw, h=h: e.activation(out=wt.t[:n, :], in_=pw[:n, :], func=AF.Copy, scale=gs[:n, 5, h:h + 1]),
                          reads=[pwb, gb_], writes=[wt.b])
                    yield
                    po1, po1b = gpr.get()
                    S.add("pe", lambda e, po1=po1, qT=qT, h=h: e.matmul(po1[:n, :], lhsT=qT, rhs=Sbf.t[:, h, :], start=True, stop=True),
                          reads=[qTn.bs[h], Sbf.bs[h]], writes=[po1b])
                    po2, po2b = gpr.get()
                    S.add("pe", lambda e, po2=po2, QKm=QKm, wt=wt: e.matmul(po2[:n, :], lhsT=QKm.t[:n, :n], rhs=wt.t[:n, :], start=True, stop=True),
                          reads=[QKm.b, wt.b], writes=[po2b])
                    t1 = gtmp.get()
                    S.add("act", lambda e, t1=t1, po1=po1, h=h: e.activation(out=t1.t[:n, :], in_=po1[:n, :], func=AF.Copy, scale=gs[:n, 2, h:h + 1]),
                          reads=[po1b, gb_], writes=[t1.b])
                    S.add("dve", lambda e, t1=t1, po2=po2, h=h: e.tensor_tensor(out=otm.t[:n, h, :], in0=po2[:n, :], in1=t1.t[:n, :], op=ALU.add),
                          reads=[po2b, t1.b], writes=[otm.bs[h]])
                    psu, psub = gpr.get()
                    S.add("pe", lambda e, psu=psu, wt=wt, h=h: e.matmul(psu[:, :], lhsT=kdtm.t[:n, h, :], rhs=wt.t[:n, :], start=True, stop=True),
                          reads=[kdtm.bs[h], wt.b], writes=[psub])
                    S.add("dve", lambda e, psu=psu, h=h: e.scalar_tensor_tensor(
                        out=Sst.t[:, h, :], in0=Sst.t[:, h, :], scalar=gs[:, 7, h:h + 1], in1=psu[:, :], op0=ALU.mult, op1=ALU.add),
                        reads=[Sst.bs[h], gb_, psub], writes=[Sst.bs[h]])
                    S.add("act", lambda e, h=h: e.activation(out=Sbf.t[:, h, :], in_=Sst.t[:, h, :], func=AF.Copy),
                          reads=[Sst.bs[h]], writes=[Sbf.bs[h]])
                    j2 = gtmp.get()
                    S.add("act", lambda e, j2=j2, h=h: e.activation(out=j2.t[:n, :], in_=otm.t[:n, h, :], func=AF.Square, accum_out=nrm.t[:n, 2, h:h + 1]),
                          reads=[otm.bs[h]], writes=[j2.b, nrm.bs[2]])
                lanes = [(Rot([(gp_t[i].t[:, j, :], gp_t[i].bs[0]) for j in range(4)]),
                          Rot(gtmp.items[i * 7:(i + 1) * 7]), Rot(gtb.items[i * 2:(i + 1) * 2])) for i in range(2)]
                for h0 in range(0, H, 2):
                    alive = [head_chain(h0 + i, *lanes[i]) for i in range(2)]
                    while alive:
                        for g in list(alive):
                            try:
                                next(g)
                            except StopIteration:
                                alive.remove(g)
                        yield
                rsq(nrm.t[:n, 2, :], nrm.t[:n, 2, :], n, nrm.bs[2], 1.0 / 128.0, RMS_EPS)
                g_ = gz[bi]
                for h in range(H):
                    S.add("dve", lambda e, h=h, g_=g_: e.scalar_tensor_tensor(
                        out=ontm.t[:n, h, :], in0=otm.t[:n, h, :], scalar=nrm.t[:n, 2, h:h + 1], in1=g_.t[:n, h * 128:(h + 1) * 128],
                        op0=ALU.mult, op1=ALU.mult),
                        reads=[otm.bs[h], nrm.bs[2], g_.b], writes=[ontm.bs[h]])
                p = tpr.get()
                for h in range(H):
                    S.add("pe", lambda e, p=p, h=h: e.transpose(out=p.t[:, h, 0:n], in_=ontm.t[:n, h, :], identity=identb.t[:n, :n]),
                          reads=[ontm.bs[h], identb.b], writes=[p.b])
                S.add("act", lambda e, p=p: e.activation(out=mixT.t[:, 0:8, c0:c0 + n], in_=p.t[:, :, 0:n], func=AF.Copy),
                      reads=[p.b], writes=[mixT.b])

            pending_gdn = [(bi, n) for bi, (n, rows, orow) in enumerate(blocks)]

            def s5_gen():
                C = N // 8
                L = C + 1
                u3 = [big.t[:, 24 + ft, 0:N].rearrange("p (c s) -> p c s", s=8) for ft in range(8)]
                for j in range(32):
                    ft = j // 4
                    wv = s5ws[j % 2]
                    S.add("sp", lambda e, wv=wv, j=j: e.dma_start(out=wv.t[:].rearrange("p a c -> p (a c)"), in_=s5w2[j, :, 0:2048]),
                          reads=[swbuf], writes=[wv.b], lane="s5%d" % (j % 2))
                    for c in range(2):
                        ps = mm[2 * c + j // 16]
                        col = (j % 16) * 32
                        for s_ in range(8):
                            S.add("pe", lambda e, ps=ps, col=col, wv=wv, s_=s_, c=c, ft=ft, j=j: e.matmul(
                                ps.t[:, col:col + C], lhsT=wv.t[:, s_ * 2 + c, :], rhs=u3[ft][:, :, s_], start=(s_ == 0), stop=(s_ == 7)),
                                reads=[wv.b, big.bs[24 + ft]], writes=[ps.b])
                    yield
                xa = [xs[0], xs[1]]
                for c in range(2):
                    for hf in range(2):
                        ps = mm[2 * c + hf]
                        S.add("act", lambda e, ps=ps, c=c, hf=hf, X=xa[c]: e.activation(
                            out=X.t[:, hf * 16:hf * 16 + 16, 1:1 + C], in_=ps.t[:, :].rearrange("p (j q) -> p j q", q=32)[:, :, 0:C], func=AF.Copy),
                            reads=[ps.b], writes=[xa[c].b])
                    S.add("pool", lambda e, c=c, X=xa[c]: e.tensor_copy(out=X.t[:, :, 0], in_=xprev.t[:, c, :]),
                          reads=[xprev.b, xa[c].b], writes=[xa[c].b])
                yield
                Xr, Xi = xa
                LR = l8.t[:, 0, 0, :]
                LI = l8.t[:, 0, 1, :]

                def tt(out_ap, a_ap, b_ap, op, rd, wr):
                    S.add("pool", lambda e: e.tensor_tensor(out=out_ap, in0=a_ap, in1=b_ap, op=op), reads=rd, writes=wr)
                for cc in range(1, L):
                    tq = xq[cc % 2]
                    q0, q1, q2, q3 = tq.bs
                    tt(tq.t[:, 0, :], Xr.t[:, :, cc - 1], LR, ALU.mult, [Xr.b, l8.b], [q0])
                    tt(tq.t[:, 1, :], Xi.t[:, :, cc - 1], LI, ALU.mult, [Xi.b, l8.b], [q1])
                    tt(tq.t[:, 2, :], Xi.t[:, :, cc - 1], LR, ALU.mult, [Xi.b, l8.b], [q2])
                    tt(tq.t[:, 3, :], Xr.t[:, :, cc - 1], LI, ALU.mult, [Xr.b, l8.b], [q3])
                    tt(tq.t[:, 0, :], tq.t[:, 0, :], tq.t[:, 1, :], ALU.subtract, [q0, q1], [q0])
                    tt(tq.t[:, 2, :], tq.t[:, 2, :], tq.t[:, 3, :], ALU.add, [q2, q3], [q2])
                    tt(Xr.t[:, :, cc], Xr.t[:, :, cc], tq.t[:, 0, :], ALU.add, [Xr.b, q0], [Xr.b])
                    tt(Xi.t[:, :, cc], Xi.t[:, :, cc], tq.t[:, 2, :], ALU.add, [Xi.b, q2], [Xi.b])
                    if cc % 4 == 0:
                        yield
                for c in range(2):
                    S.add("act", lambda e, c=c, X=xa[c]: e.activation(out=xb.t[:, c, :, 0:C], in_=X.t[:, :, 0:C], func=AF.Copy),
                          reads=[xa[c].b, xb.b], writes=[xb.b])
                    S.add("pool", lambda e, c=c, X=xa[c]: e.tensor_copy(out=xprev.t[:, c, :], in_=X.t[:, :, C]),
                          reads=[xa[c].b, xprev.b], writes=[xprev.b])
                for ft in range(8):
                    kb = s5ks[ft % 2]
                    S.add("sp", lambda e, kb=kb, ft=ft: e.dma_start(out=kb.t[:].rearrange("p a c -> p (a c)"), in_=kscr[ft]),
                          reads=[swbuf], writes=[kb.b], lane="s5k%d" % (ft % 2))
                    yp = mmr.get()
                    y3 = yp.t[:, 0:N].rearrange("p (c s) -> p c s", s=8)
                    for hf in range(4):
                        vb = s5ws[hf % 2]
                        S.add("sp", lambda e, vb=vb, ft=ft, hf=hf: e.dma_start(
                            out=vb.t[:].rearrange("p (q a) c -> p q (a c)", q=4),
                            in_=s5w2[ft * 4:ft * 4 + 4, :, 2048 + hf * 512:2048 + (hf + 1) * 512].rearrange("q p f -> p q f")),
                            reads=[swbuf], writes=[vb.b], lane="s5%d" % (hf % 2))
                        for s_ in range(2 * hf, 2 * hf + 2):
                            for sp_ in range(s_ + 1):
                                S.add("pe", lambda e, y3=y3, kb=kb, s_=s_, sp_=sp_, ft=ft: e.matmul(
                                    y3[:, :, s_], lhsT=kb.t[:, s_ - sp_, :], rhs=u3[ft][:, :, sp_], start=(sp_ == 0), stop=False),
                                    reads=[kb.b, big.bs[24 + ft]], writes=[yp.b])
                            for jj in range(4):
                                for c in range(2):
                                    S.add("pe", lambda e, y3=y3, vb=vb, s_=s_, c=c, jj=jj, ft=ft, hf=hf: e.matmul(
                                        y3[:, :, s_], lhsT=vb.t[:, jj * 4 + (s_ - 2 * hf) * 2 + c, :], rhs=xb.t[:, c, ft * 4 + jj, 0:C], start=False,
                                        stop=(jj == 3 and c == 1)),
                                        reads=[vb.b, xb.b], writes=[yp.b])
                            yield
                    if True:
                        yx = sgt.get()
                        S.add("dve", lambda e, yx=yx, yp=yp, ft=ft: e.scalar_tensor_tensor(
                            out=yx.t[:, 0:N], in0=big.t[:, 24 + ft, 0:N], scalar=d5.t[:, ft:ft + 1], in1=yp.t[:, 0:N], op0=ALU.mult, op1=ALU.add),
                            reads=[big.bs[24 + ft], d5.b, yp.b], writes=[yx.b])
                        sq = sgt.get()
                        S.add("act", lambda e, sq=sq, yx=yx: e.activation(out=sq.t[:, 0:N], in_=yx.t[:, 0:N], func=AF.Square),
                              reads=[yx.b], writes=[sq.b])
                        S.add("dve", lambda e, sq=sq: e.tensor_scalar(out=sq.t[:, 0:N], in0=sq.t[:, 0:N], scalar1=0.044715, scalar2=1.0, op0=ALU.mult, op1=ALU.add),
                              reads=[sq.b], writes=[sq.b])
                        S.add("pool", lambda e, sq=sq, yx=yx: e.tensor_tensor(out=sq.t[:, 0:N], in0=sq.t[:, 0:N], in1=yx.t[:, 0:N], op=ALU.mult),
                              reads=[sq.b, yx.b], writes=[sq.b])
                        S.add("act", lambda e, sq=sq: e.activation(out=sq.t[:, 0:N], in_=sq.t[:, 0:N], func=AF.Tanh, scale=math.sqrt(2.0 / math.pi)),
                              reads=[sq.b], writes=[sq.b])
                        S.add("dve", lambda e, sq=sq: e.tensor_scalar(out=sq.t[:, 0:N], in0=sq.t[:, 0:N], scalar1=0.5, scalar2=0.5, op0=ALU.mult, op1=ALU.add),
                              reads=[sq.b], writes=[sq.b])
                        S.add("pool", lambda e, sq=sq, yx=yx, ft=ft: e.tensor_tensor(out=big.t[:, 32 + ft, 0:N], in0=sq.t[:, 0:N], in1=yx.t[:, 0:N], op=ALU.mult),
                              reads=[sq.b, yx.b], writes=[big.bs[32 + ft]])
            s5g = s5_gen()
            s5_alive = True
            for bi, n in pending_gdn:
                for step, _ in enumerate(gdn_block(bi, n, c0s[bi], abps[bi][0], abps[bi][1])):
                    if s5_alive and step % 4 == 3:
                        try:
                            next(s5g)
                        except StopIteration:
                            s5_alive = False
            if s5_alive:
                for _ in s5g:
                    pass
            for mt in range(8):
                if mt % 4 == 0:
                    w = load_w(c_glu[mt // 4], 4096)
                    wv = w.t[:].rearrange("p (m k c) -> p m k c", m=4, k=8)
                p = mmr.get()
                for kc in range(8):
                    S.add("pe", lambda e, p=p, wv=wv, mt=mt, kc=kc: e.matmul(p.t[:, 0:N], lhsT=wv[:, mt % 4, kc, :], rhs=big.t[:, 32 + kc, 0:N],
                                                                            start=(kc == 0), stop=(kc == 7)),
                          reads=[w.b, big.bs[32 + kc]], writes=[p.b])
                sg = sgt.get()
                S.add("act", lambda e, p=p, sg=sg, mt=mt: e.activation(out=sg.t[:, 0:N], in_=p.t[:, 0:N], func=AF.Tanh, scale=0.5, bias=hbg.t[:, mt:mt + 1]),
                      reads=[p.b, hbg.b], writes=[sg.b])
                S.add("dve", lambda e, sg=sg: e.tensor_scalar(out=sg.t[:, 0:N], in0=sg.t[:, 0:N], scalar1=0.5, scalar2=0.5, op0=ALU.mult, op1=ALU.add),
                      reads=[sg.b], writes=[sg.b])
                S.add("pool", lambda e, sg=sg, mt=mt: e.tensor_tensor(out=mixT.t[:, 8 + mt, 0:N], in0=sg.t[:, 0:N], in1=big.t[:, 32 + mt, 0:N], op=ALU.mult),
                      reads=[sg.b, big.bs[32 + mt]], writes=[mixT.b])
            for cg in range(8):
                w = load_w(c_out[cg], 4096)
                wv = w.t[:].rearrange("p (k c) -> p k c", k=KC)
                for bi, (n, rows, orow) in enumerate(blocks):
                    hb = hbuf[bi]
                    p = mmr.get()
                    for kc in range(KC):
                        S.add("pe", lambda e, p=p, wv=wv, kc=kc, n=n, c0=c0s[bi]: e.matmul(
                            p.t[:n, 0:256], lhsT=mixT.t[:, kc, c0:c0 + n], rhs=wv[:, kc, :], start=(kc == 0), stop=(kc == KC - 1)),
                            reads=[w.b, mixT.b], writes=[p.b])
                    S.add("dve", lambda e, p=p, hb=hb, n=n, cg=cg: e.scalar_tensor_tensor(
                        out=hb.t[:n, cg * 256:(cg + 1) * 256], in0=hb.t[:n, cg * 256:(cg + 1) * 256], scalar=ALPHA, in1=p.t[:n, 0:256],
                        op0=ALU.mult, op1=ALU.add),
                        reads=[hb.b, p.b], writes=[hb.b])
            for bi, (n, rows, orow) in enumerate(blocks):
                layer_norm(hbuf[bi], n, 2)
                to_actT(hbuf[bi], n, c0s[bi], actT)
            for bt_ in range(22):
                if prefetch is not None and bt_ >= 2:
                    try:
                        next(prefetch)
                    except StopIteration:
                        prefetch = None
                wg = load_w(c_gate[bt_], 4096)
                wu = load_w(c_up[bt_], 4096)
                wgv = wg.t[:].rearrange("p (m k c) -> p m k c", m=2, k=KC)
                wuv = wu.t[:].rearrange("p (m k c) -> p m k c", m=2, k=KC)
                for mi in range(2):
                    mt = bt_ * 2 + mi
                    pg = mmr.get()
                    pu = mmr.get()
                    for kc in range(KC):
                        S.add("pe", lambda e, pg=pg, wgv=wgv, mi=mi, kc=kc: e.matmul(
                            pg.t[:, 0:N], lhsT=wgv[:, mi, kc, :], rhs=actT.t[:, kc, 0:N], start=(kc == 0), stop=(kc == KC - 1)),
                            reads=[wg.b, actT.b], writes=[pg.b])
                    for kc in range(KC):
                        S.add("pe", lambda e, pu=pu, wuv=wuv, mi=mi, kc=kc: e.matmul(
                            pu.t[:, 0:N], lhsT=wuv[:, mi, kc, :], rhs=actT.t[:, kc, 0:N], start=(kc == 0), stop=(kc == KC - 1)),
                            reads=[wu.b, actT.b], writes=[pu.b])
                    sg = sgt.get()
                    S.add("act", lambda e, pg=pg, sg=sg: e.activation(out=sg.t[:, 0:N], in_=pg.t[:, 0:N], func=AF.Silu),
                          reads=[pg.b], writes=[sg.b])
                    S.add("dve", lambda e, pu=pu, sg=sg, mt=mt: e.tensor_tensor(out=big.t[:, mt, 0:N], in0=pu.t[:, 0:N], in1=sg.t[:, 0:N], op=ALU.mult),
                          reads=[pu.b, sg.b], writes=[big.bs[mt]])
            if prefetch is not None:
                for _ in prefetch:
                    pass
            for cg in range(8):
                ps_ = [mmr.get() for _ in range(nb)]
                for ch in range(4):
                    w = load_w(c_down[cg, ch], 11 * 256)
                    wv = w.t[:, 0:11 * 256].rearrange("p (k c) -> p k c", k=11)
                    for bi, (n, rows, orow) in enumerate(blocks):
                        for kc in range(11):
                            S.add("pe", lambda e, p=ps_[bi], wv=wv, kc=kc, ch=ch, n=n, c0=c0s[bi]: e.matmul(
                                p.t[:n, 0:256], lhsT=big.t[:, ch * 11 + kc, c0:c0 + n], rhs=wv[:, kc, :],
                                start=(ch == 0 and kc == 0), stop=(ch == 3 and kc == 10)),
                                reads=[w.b, big.bs[ch * 11 + kc]], writes=[ps_[bi].b])
                for bi, (n, rows, orow) in enumerate(blocks):
                    hb = hbuf[bi]
                    S.add("dve", lambda e, p=ps_[bi], hb=hb, n=n, cg=cg: e.scalar_tensor_tensor(
                        out=hb.t[:n, cg * 256:(cg + 1) * 256], in0=hb.t[:n, cg * 256:(cg + 1) * 256], scalar=ALPHA, in1=p.t[:n, 0:256],
                        op0=ALU.mult, op1=ALU.add),
                        reads=[hb.b, ps_[bi].b], writes=[hb.b])
            for bi, (n, rows, orow) in enumerate(blocks):
                hb = hbuf[bi]
                layer_norm(hb, n, 4)
                for (r0, nr, dst) in orow:
                    S.add("sp", lambda e, hb=hb, r0=r0, nr=nr, dst=dst: e.dma_start(out=dst, in_=hb.t[r0:r0 + nr, :]),
                          reads=[hb.b, outbuf], lane="y%d" % ((2 * k + bi) % 4))

        def init_state(sample):
            if sample:
                S.add("sp", lambda e: e.dma_start(out=Sst.t[:], in_=st_delta.rearrange("h k v -> k h v")),
                      writes=Sst.bs, lane="st")
                S.add("sp", lambda e: e.dma_start(out=hist.t[:], in_=st_conv[:, :, :]), writes=[hist.b], lane="st")
                S.add("sp", lambda e: e.dma_start(out=xprev.t[:, 0, :], in_=st_re[:, :]), writes=[xprev.b], lane="st")
                S.add("sp", lambda e: e.dma_start(out=xprev.t[:, 1, :], in_=st_im[:, :]), writes=[xprev.b], lane="st")
            else:
                S.add("pool", lambda e: e.memset(Sst.t[:], 0.0), writes=Sst.bs)
                S.add("pool", lambda e: e.memset(hist.t[:], 0.0), writes=[hist.b])
                S.add("pool", lambda e: e.memset(xprev.t[:], 0.0), writes=[xprev.b])
            for h in range(H):
                S.add("act", lambda e, h=h: e.activation(out=Sbf.t[:, h, :], in_=Sst.t[:, h, :], func=AF.Copy),
                      reads=[Sst.bs[h]], writes=[Sbf.bs[h]])

        def store_state(o_conv, o_delta, o_re, o_im):
            S.add("sp", lambda e: e.dma_start(out=o_delta.rearrange("h k v -> k h v"), in_=Sst.t[:]),
                  reads=Sst.bs + [outbuf], lane="so")
            S.add("sp", lambda e: e.dma_start(out=o_conv[:, :, :], in_=hist.t[:]), reads=[hist.b, outbuf], lane="so")
            S.add("sp", lambda e: e.dma_start(out=o_re[:, :], in_=xprev.t[:, 0, :]), reads=[xprev.b, outbuf], lane="so")
            S.add("sp", lambda e: e.dma_start(out=o_im[:, :], in_=xprev.t[:, 1, :]), reads=[xprev.b, outbuf], lane="so")

        init_state(False)
        bl = []
        for b in range(nblk):
            s0 = 128 * b
            if b == 0:
                rows = [(0, NMETA, meta[:, :]), (NMETA, 128 - NMETA, x_p[0:128 - NMETA, :])]
                orow = [(NMETA, 128 - NMETA, y_p[0:128 - NMETA, :])]
            else:
                rows = [(0, 128, x_p[s0 - NMETA:s0 - NMETA + 128, :])]
                orow = [(0, 128, y_p[s0 - NMETA:s0 - NMETA + 128, :])]
            bl.append((128, rows, orow))
        tiles = [bl[i:i + 2] for i in range(0, nblk, 2)]
        tiles.append([(16, [(0, 16, x_p[seq - 16:seq, :])], [(0, 16, y_p[seq - 16:seq, :])])])
        tiles.append([(16, [(0, 16, x_s[:, :])], [(0, 16, y_s[:, :])])])
        for _ in phase_A(0, tiles[0]):
            pass
        for k, tl in enumerate(tiles):
            if k == len(tiles) - 1:
                store_state(o_conv_p, o_delta_p, o_re_p, o_im_p)
                init_state(True)
            run_tile(k, tl, phase_A(k + 1, tiles[k + 1]) if k + 1 < len(tiles) else None)
        store_state(o_conv_s, o_delta_s, o_re_s, o_im_s)
        S.add("sp", lambda e: e.nop(), writes=[outbuf])
        with nc.allow_non_contiguous_dma(reason="small state layouts"):
            S.emit(nc, es)
    return nc


def _consts():
    i = np.arange(128)
    ident = np.eye(128, dtype=np.float32)
    ones = np.ones((128, 128), np.float32)
    negmT = np.where(i[:, None] <= i[None, :], 0.0, NEG).astype(np.float32)
    strict = (i[:, None] < i[None, :]).astype(np.float32)
    uincl = (i[:, None] <= i[None, :]).astype(np.float32)
    return np.ascontiguousarray(np.stack([ident, ones, negmT, strict, uincl], axis=1).reshape(128, 5 * 128))


def _stat(w, nb):
    K, M = w.shape
    kc = K // 128
    mt = M // 128
    a = w.reshape(kc, 128, mt, 128).transpose(2, 1, 0, 3)
    a = a.reshape(mt // nb, nb, 128, kc, 128).transpose(0, 2, 1, 3, 4)
    return np.ascontiguousarray(a.reshape(mt // nb, 128, nb * kc * 128))


def _mov(w, ncg):
    K, M = w.shape
    kc = K // 128
    a = w.reshape(kc, 128, ncg, 256).transpose(2, 1, 0, 3)
    return np.ascontiguousarray(a.reshape(ncg, 128, kc * 256))


def _prep_shared(inp):
    f = lambda a: np.asarray(a, np.float32)
    w_in = f(inp["w_in"])[0]
    sh = {}
    sh["w_in_s"] = _stat(np.concatenate([w_in[:, 0:3072], w_in[:, 4112:5136]], axis=1), 2)
    sh["w_in_z"] = _mov(w_in[:, 3072:4096], 4)
    sh["w_in_ab"] = np.ascontiguousarray(w_in[:, 4096:4112].reshape(KC, 128, 16).transpose(1, 0, 2).reshape(128, KC * 16))
    sh["cw"] = np.ascontiguousarray(f(inp["conv_w"])[0].reshape(4, 24, 128).transpose(2, 1, 0).reshape(128, 96))
    dnp = np.zeros((3, 128), np.float32)
    dnp[0, :8] = f(inp["dn_a_log"])[0]
    dnp[1, :8] = f(inp["dn_dt_bias"])[0]
    dnp[2, :] = f(inp["dn_norm_w"])[0]
    sh["dnp"] = dnp
    sh["s5a"] = np.ascontiguousarray(np.stack([f(inp["s5_a_re"])[0], f(inp["s5_a_im"])[0],
                                               np.broadcast_to(f(inp["s5_log_dt"])[0][:, None], (64, 64))]))
    bpad = np.zeros((2, 32, 128, 128), np.float32)
    cpad = np.zeros((2, 32, 128, 128), np.float32)
    for c, (bk, ck) in enumerate((("s5_b_re", "s5_c_re"), ("s5_b_im", "s5_c_im"))):
        B = f(inp[bk])[0]
        C = f(inp[ck])[0]
        for g in range(64):
            j, g2, gl = g // 2, g % 2, g % 8
            bpad[c, j, g2 * 64:(g2 + 1) * 64, gl * 16:(gl + 1) * 16] = B[g]
            cpad[c, j, g2 * 64:(g2 + 1) * 64, gl * 16:(gl + 1) * 16] = C[g].T
    sh["bpadT"] = bpad
    sh["cpad"] = cpad
    sh["s5d"] = np.ascontiguousarray(f(inp["s5_d"])[0].reshape(8, 128).T)
    sh["bglu"] = np.ascontiguousarray(f(inp["s5_b_glu"])[0].reshape(8, 128).T)
    wg = f(inp["s5_w_glu"])[0]
    sh["w_glu"] = _stat(wg, 4)
    sh["w_out"] = _mov(f(inp["w_out"])[0], 8)
    sh["w_gate"] = _stat(f(inp["ffn_w_gate"])[0], 2)
    sh["w_up"] = _stat(f(inp["ffn_w_up"])[0], 2)
    wd = f(inp["ffn_w_down"])[0]
    a = wd.reshape(4, 11, 128, 8, 256).transpose(3, 0, 2, 1, 4)
    sh["w_down"] = np.ascontiguousarray(a.reshape(8, 4, 128, 11 * 256))
    sh["cmat"] = _consts()
    sh["meta"] = f(inp["meta_tokens"])
    sh["lnp"] = np.ascontiguousarray(np.stack([f(inp["ln_in_g"]), f(inp["ln_in_b"]), f(inp["ln1_g"])[0], f(inp["ln1_b"])[0],
                                               f(inp["ln2_g"])[0], f(inp["ln2_b"])[0]]))
    return sh


def _s5_in(a):
    return np.ascontiguousarray(a.reshape(32, 2, 64).transpose(1, 2, 0).reshape(128, 32))


def _s5_out(a):
    return np.ascontiguousarray(a.reshape(2, 64, 32).transpose(2, 0, 1).reshape(64, 64))


def _conv_in(a):
    return np.ascontiguousarray(a.reshape(3, 24, 128).transpose(2, 1, 0))


def _conv_out(a):
    return np.ascontiguousarray(a.transpose(2, 1, 0).reshape(3, 3072))


_NC_CACHE = {}


def run(inputs, ncores, nblk):
    f = lambda a: np.asarray(a, np.float32)
    sh = _prep_shared(inputs)
    if nblk not in _NC_CACHE:
        _NC_CACHE[nblk] = build(nblk)
    nc = _NC_CACHE[nblk]
    in_maps = []
    for c in range(ncores):
        m = dict(sh)
        m["x_p"] = np.ascontiguousarray(f(inputs["x_prompt"])[c])
        m["x_s"] = np.ascontiguousarray(f(inputs["x_sample"])[c])
        m["st_conv"] = _conv_in(f(inputs["state_conv_qkv"])[0, c])
        m["st_delta"] = np.ascontiguousarray(f(inputs["state_delta"])[0, c])
        m["st_re"] = _s5_in(f(inputs["state_s5_re"])[0, c])
        m["st_im"] = _s5_in(f(inputs["state_s5_im"])[0, c])
        in_maps.append(m)
    res = run_bass_kernel_spmd(nc, in_maps, core_ids=list(range(ncores)))
    R = res.results
    st = lambda k, fn=lambda a: a: np.stack([fn(np.asarray(R[c][k], np.float32)) for c in range(ncores)])[None]
    return (np.stack([np.asarray(R[c]["y_p"], np.float32) for c in range(ncores)]),
            np.stack([np.asarray(R[c]["y_s"], np.float32) for c in range(ncores)]),
            st("o_conv_p", _conv_out), st("o_delta_p"), st("o_re_p", _s5_out), st("o_im_p", _s5_out),
            st("o_conv_s", _conv_out), st("o_delta_s"), st("o_re_s", _s5_out), st("o_im_s", _s5_out))


def kernel(**inputs):
    return run(inputs, 8, 32)
```

```python
import math
from contextlib import ExitStack

import numpy as np

import concourse.bass as bass
import concourse.mybir as mybir
from concourse.bass_utils import run_bass_kernel_spmd

F32 = mybir.dt.float32
BF16 = mybir.dt.bfloat16
ALU = mybir.AluOpType
AF = mybir.ActivationFunctionType
AX = mybir.AxisListType

D = 2048
KC = 16
H = 8
QKV = 3072
FF = 5632
FKC = 44
NMETA = 16
ALPHA = 2.0 ** 0.25
LN_EPS = 1e-5
RMS_EPS = 1e-6
PAD = 258
NEG = -30000.0
MAGIC = 12582912.0
TWO_PI = 2.0 * math.pi


class Buf:
    __slots__ = ("name", "w", "r", "excl")

    def __init__(self, name):
        self.name = name
        self.w = None
        self.r = []
        self.excl = False


class Op:
    __slots__ = ("eng", "fn", "lane", "seq", "waits", "signal", "val")

    def __init__(self, eng, fn, lane):
        self.eng = eng
        self.fn = fn
        self.lane = lane
        self.seq = 0
        self.waits = []
        self.signal = False
        self.val = 0


EPOCH = 20000


class Sched:
    ENGS = ("pe", "act", "dve", "pool", "sp")

    def __init__(self):
        self.ops = {e: [] for e in self.ENGS}
        self.cnt = {e: 0 for e in self.ENGS}
        self.waited = {e: {} for e in self.ENGS}
        self.lane_cnt = {}
        self.lane_last = {}

    def add(self, eng, fn, reads=(), writes=(), lane=None):
        op = Op(eng, fn, lane)
        deps = []
        for b in reads:
            if b.w is not None:
                deps.append(b.w)
            if b.excl:
                deps.extend(r_ for r_ in b.r if r_.eng != eng)
        for b in writes:
            if b.w is not None:
                deps.append(b.w)
            deps.extend(b.r)
        if lane is not None:
            if lane in self.lane_last:
                deps.append(self.lane_last[lane])
            self.lane_cnt[lane] = self.lane_cnt.get(lane, 0) + 1
            op.seq = self.lane_cnt[lane]
            self.lane_last[lane] = op
        else:
            self.cnt[eng] += 1
            op.seq = self.cnt[eng]
        need = {}
        for d in deps:
            if d is op:
                continue
            key = ("L", d.lane) if d.lane is not None else ("E", d.eng)
            if d.lane is None and d.eng == eng and eng == "pe":
                continue
            if key not in need or d.seq > need[key].seq:
                need[key] = d
        for key, d in need.items():
            if self.waited[eng].get(key, 0) >= d.seq:
                continue
            self.waited[eng][key] = d.seq
            d.signal = True
            op.waits.append(d)
        for b in reads:
            b.r.append(op)
        for b in writes:
            b.w = op
            b.r = []
        self.ops[eng].append(op)
        return op

    def emit(self, nc, es):
        nsem = {}
        for e in self.ENGS:
            c = 0
            for op in self.ops[e]:
                if op.lane is None and op.signal:
                    c += 1
                    op.val = c
            nsem[e] = max(1, (c + EPOCH - 1) // EPOCH)
        esem = {e: [es.enter_context(nc.semaphore("s_%s_%d" % (e, i))) for i in range(nsem[e])]
                for e in self.ENGS}
        lsem = {l: es.enter_context(nc.semaphore("l_%s" % l)) for l in self.lane_cnt}

        def semval(d):
            if d.lane is not None:
                return lsem[d.lane], 16 * d.seq
            i = (d.val - 1) // EPOCH
            return esem[d.eng][i], d.val - i * EPOCH

        def run(engname, eobj):
            for op in self.ops[engname]:
                for d in op.waits:
                    s, v = semval(d)
                    eobj.wait_ge(s, v)
                ins = op.fn(eobj)
                if op.lane is not None:
                    ins.then_inc(lsem[op.lane], 16)
                elif op.signal:
                    s, _ = semval(op)
                    ins.then_inc(s, 1)

        with nc.Block() as block:
            @block.tensor
            def _(e):
                run("pe", e)

            @block.scalar
            def _(e):
                run("act", e)

            @block.vector
            def _(e):
                run("dve", e)

            @block.gpsimd
            def _(e):
                run("pool", e)

            @block.sync
            def _(e):
                run("sp", e)


class T:
    def __init__(self, t, n=1, name="t"):
        self.t = t
        self.bs = [Buf("%s%d" % (name, i)) for i in range(n)]

    @property
    def b(self):
        return self.bs[0]


class Rot:
    def __init__(self, items):
        self.items = items
        self.i = 0

    def get(self):
        x = self.items[self.i % len(self.items)]
        self.i += 1
        return x


def build(nblk):
    seq = 128 * nblk
    nc = bass.Bass("TRN2", target_bir_lowering=False)
    S = Sched()

    def din(name, shape, dt=F32):
        return nc.dram_tensor(name, list(shape), dt, kind="ExternalInput").ap()

    def dout(name, shape):
        return nc.dram_tensor(name, list(shape), F32, kind="ExternalOutput").ap()

    x_p = din("x_p", [seq, D])
    x_s = din("x_s", [16, D])
    st_conv = din("st_conv", [128, 24, 3])
    st_delta = din("st_delta", [H, 128, 128])
    st_re = din("st_re", [128, 32])
    st_im = din("st_im", [128, 32])
    meta = din("meta", [NMETA, D])
    lnp = din("lnp", [6, D])
    w_in_s = din("w_in_s", [16, 128, 4096])
    w_in_z = din("w_in_z", [4, 128, 4096])
    w_in_ab = din("w_in_ab", [128, KC * 16])
    cw_d = din("cw", [128, 24 * 4])
    dnp = din("dnp", [3, 128])
    s5a = din("s5a", [3, 64, 64])
    bpadT = din("bpadT", [2, 32, 128, 128])
    cpad = din("cpad", [2, 32, 128, 128])
    s5d = din("s5d", [128, 8])
    bglu = din("bglu", [128, 8])
    w_glu = din("w_glu", [2, 128, 4096])
    w_out = din("w_out", [8, 128, 4096])
    w_gate = din("w_gate", [22, 128, 4096])
    w_up = din("w_up", [22, 128, 4096])
    w_down = din("w_down", [8, 4, 128, 11 * 256])
    cmat = din("cmat", [128, 5 * 128])

    y_p = dout("y_p", [seq, D])
    y_s = dout("y_s", [16, D])
    o_conv_p = dout("o_conv_p", [128, 24, 3])
    o_delta_p = dout("o_delta_p", [H, 128, 128])
    o_re_p = dout("o_re_p", [128, 32])
    o_im_p = dout("o_im_p", [128, 32])
    o_conv_s = dout("o_conv_s", [128, 24, 3])
    o_delta_s = dout("o_delta_s", [H, 128, 128])
    o_re_s = dout("o_re_s", [128, 32])
    o_im_s = dout("o_im_s", [128, 32])
    s5scr = nc.dram_tensor("s5scr", [4, 64, 64], F32, kind="Internal").ap()
    s5w2 = nc.dram_tensor("s5w2", [32, 128, 4096], BF16, kind="Internal").ap()
    kscr = nc.dram_tensor("kscr", [8, 128, 1024], BF16, kind="Internal").ap()
    def dscr(name, shape):
        return nc.dram_tensor(name, list(shape), BF16, kind="Internal").ap()
    c_in_s = dscr("c_in_s", [16, 128, 4096])
    c_in_z = dscr("c_in_z", [4, 128, 4096])
    c_glu = dscr("c_glu", [2, 128, 4096])
    c_out = dscr("c_out", [8, 128, 4096])
    c_gate = dscr("c_gate", [22, 128, 4096])
    c_up = dscr("c_up", [22, 128, 4096])
    c_down = dscr("c_down", [8, 4, 128, 11 * 256])
    wsbuf = Buf("wscratch")
    outbuf = Buf("dram_out")
    scrbuf = Buf("s5scr")
    swbuf = Buf("s5w")

    with ExitStack() as es:
        def sb(name, shape, dt=F32, n=1):
            return T(es.enter_context(nc.sbuf_tensor("sb_" + name, list(shape), dt)), n, name)

        def psum(name, shape, dt=F32, n=1):
            t_ = T(es.enter_context(nc.psum_tensor("ps_" + name, list(shape), dt)), n, name)
            for b_ in t_.bs:
                b_.excl = True
            return t_

        cm = sb("cm", [128, 5, 128])
        identb = sb("identb", [128, 128], BF16)
        ident = cm.t[:, 0, :]
        ones = cm.t[:, 1, :]
        negmT = cm.t[:, 2, :]
        strict = cm.t[:, 3, :]
        uincl = cm.t[:, 4, :]
        S.add("sp", lambda e: e.dma_start(out=cm.t[:].rearrange("p a b -> p (a b)"), in_=cmat[:, :]),
              writes=[cm.b], lane="c0")
        S.add("act", lambda e: e.activation(out=identb.t[:], in_=ident, func=AF.Copy),
              reads=[cm.b], writes=[identb.b])
        dn = sb("dn", [128, 3, 128])
        S.add("sp", lambda e: e.dma_start(out=dn.t[:], in_=dnp.partition_broadcast(128)),
              writes=[dn.b], lane="c0")
        negA = sb("negA", [128, 8])
        S.add("act", lambda e: e.activation(out=negA.t[:], in_=dn.t[:, 0, 0:8], func=AF.Exp),
              reads=[dn.b], writes=[negA.b])
        S.add("dve", lambda e: e.tensor_scalar_mul(out=negA.t[:], in0=negA.t[:], scalar1=-1.0),
              reads=[negA.b], writes=[negA.b])
        dtb = dn.t[:, 1, 0:8]
        normw = dn.t[:, 2, :]
        cw = sb("cw", [128, 24, 4])
        S.add("sp", lambda e: e.dma_start(out=cw.t[:].rearrange("p a b -> p (a b)"), in_=cw_d[:, :]),
              writes=[cw.b], lane="c0")
        d5 = sb("d5", [128, 8])
        hbg = sb("hbg", [128, 8])
        S.add("sp", lambda e: e.dma_start(out=d5.t[:], in_=s5d[:, :]), writes=[d5.b], lane="c0")
        S.add("sp", lambda e: e.dma_start(out=hbg.t[:], in_=bglu[:, :]), writes=[hbg.b], lane="c0")
        S.add("dve", lambda e: e.tensor_scalar_mul(out=hbg.t[:], in0=hbg.t[:], scalar1=0.5),
              reads=[hbg.b], writes=[hbg.b])
        mhalf = sb("mhalf", [128, 8])
        S.add("pool", lambda e: e.memset(mhalf.t[:], -0.5), writes=[mhalf.b])

        mm = [psum("mm%d" % i, [128, 512]) for i in range(4)]
        tp = [psum("tp%d" % i, [128, 8, 128], BF16) for i in range(2)]
        gp_t = [psum("gp%d" % i, [128, 4, 128], F32, 4) for i in range(2)]
        mmr = Rot(mm)
        tpr = Rot(tp)
        gpr = Rot([(gp_t[i].t[:, j, :], gp_t[i].bs[0]) for j in range(4) for i in range(2)])

        a3 = sb("a3", [64, 3, 64])
        S.add("sp", lambda e: e.dma_start(out=a3.t[:], in_=s5a.rearrange("a g p -> g a p")),
              writes=[a3.b], lane="c0")
        s5t = sb("s5t", [64, 12, 64])
        tb = s5t.b

        def s5op(eng, fn):
            S.add(eng, fn, reads=[a3.b, tb], writes=[tb])
        st = s5t.t
        s5op("act", lambda e: e.activation(out=st[:, 0, :], in_=a3.t[:, 2, :], func=AF.Exp))
        s5op("dve", lambda e: e.tensor_tensor(out=st[:, 1, :], in0=a3.t[:, 0, :], in1=st[:, 0, :], op=ALU.mult))
        s5op("dve", lambda e: e.tensor_tensor(out=st[:, 2, :], in0=a3.t[:, 1, :], in1=st[:, 0, :], op=ALU.mult))
        s5op("act", lambda e: e.activation(out=st[:, 3, :], in_=st[:, 1, :], func=AF.Exp))
        for (dst, shift) in ((5, 0.0), (6, math.pi / 2)):
            s5op("dve", lambda e, shift=shift: e.tensor_scalar(out=st[:, 11, :], in0=st[:, 2, :], scalar1=shift,
                                                              scalar2=None, op0=ALU.add))
            s5op("dve", lambda e: e.tensor_scalar(out=st[:, 4, :], in0=st[:, 11, :], scalar1=1.0 / TWO_PI,
                                                  scalar2=MAGIC, op0=ALU.mult, op1=ALU.add))
            s5op("dve", lambda e: e.tensor_scalar(out=st[:, 4, :], in0=st[:, 4, :], scalar1=-MAGIC,
                                                  scalar2=-TWO_PI, op0=ALU.add, op1=ALU.mult))
            s5op("dve", lambda e: e.tensor_tensor(out=st[:, 4, :], in0=st[:, 4, :], in1=st[:, 11, :], op=ALU.add))
            s5op("act", lambda e, dst=dst: e.activation(out=st[:, dst, :], in_=st[:, 4, :], func=AF.Sin))
        s5op("dve", lambda e: e.tensor_tensor(out=st[:, 7, :], in0=st[:, 3, :], in1=st[:, 6, :], op=ALU.mult))
        s5op("dve", lambda e: e.tensor_tensor(out=st[:, 8, :], in0=st[:, 3, :], in1=st[:, 5, :], op=ALU.mult))
        s5op("dve", lambda e: e.tensor_tensor(out=st[:, 11, :], in0=a3.t[:, 0, :], in1=a3.t[:, 0, :], op=ALU.mult))
        s5op("dve", lambda e: e.tensor_tensor(out=st[:, 0, :], in0=a3.t[:, 1, :], in1=a3.t[:, 1, :], op=ALU.mult))
        s5op("dve", lambda e: e.tensor_tensor(out=st[:, 11, :], in0=st[:, 11, :], in1=st[:, 0, :], op=ALU.add))
        s5op("dve", lambda e: e.reciprocal(out=st[:, 11, :], in_=st[:, 11, :]))
        s5op("dve", lambda e: e.tensor_scalar(out=st[:, 4, :], in0=st[:, 7, :], scalar1=-1.0, scalar2=None, op0=ALU.add))
        s5op("dve", lambda e: e.tensor_tensor(out=st[:, 9, :], in0=st[:, 4, :], in1=a3.t[:, 0, :], op=ALU.mult))
        s5op("dve", lambda e: e.tensor_tensor(out=st[:, 0, :], in0=st[:, 8, :], in1=a3.t[:, 1, :], op=ALU.mult))
        s5op("dve", lambda e: e.tensor_tensor(out=st[:, 9, :], in0=st[:, 9, :], in1=st[:, 0, :], op=ALU.add))
        s5op("dve", lambda e: e.tensor_tensor(out=st[:, 9, :], in0=st[:, 9, :], in1=st[:, 11, :], op=ALU.mult))
        s5op("dve", lambda e: e.tensor_tensor(out=st[:, 10, :], in0=st[:, 8, :], in1=a3.t[:, 0, :], op=ALU.mult))
        s5op("dve", lambda e: e.tensor_tensor(out=st[:, 0, :], in0=st[:, 4, :], in1=a3.t[:, 1, :], op=ALU.mult))
        s5op("dve", lambda e: e.tensor_tensor(out=st[:, 10, :], in0=st[:, 10, :], in1=st[:, 0, :], op=ALU.subtract))
        s5op("dve", lambda e: e.tensor_tensor(out=st[:, 10, :], in0=st[:, 10, :], in1=st[:, 11, :], op=ALU.mult))
        S.add("sp", lambda e: e.dma_start(out=s5scr.rearrange("a g p -> g a p"), in_=st[:, 7:11, :]),
              reads=[tb], writes=[scrbuf], lane="c0")
        Sst = sb("Sst", [128, H, 128], F32, H)
        Sbf = sb("Sbf", [128, H, 128], BF16, H)
        hist = sb("hist", [128, 24, 3])
        xprev = sb("xprev", [128, 2, 32])

        NT = 256
        hb4 = [sb("h%d" % i, [128, D]) for i in range(4)]
        hb16 = sb("hb16", [128, D], BF16)
        lnbc = sb("lnbc", [128, 2, D])
        actTs = [sb("actT%d" % i, [128, KC, NT], BF16) for i in range(2)]
        mixT = sb("mixT", [128, KC, NT], BF16)
        big = sb("big", [128, FKC, NT], BF16, FKC)
        Wsl = [sb("W%d" % i, [128, 4096], BF16) for i in range(4)]
        wab = sb("wab", [128, KC, 16], BF16)
        wr = Rot(Wsl)
        wlane = Rot(["w0", "w1", "w2", "w3"])
        stats_l = [sb("stats%d" % i, [128, 4, 6]) for i in range(2)]
        mv_l = [sb("mv%d" % i, [128, 4]) for i in range(2)]
        ln_i = [0]
        cbufs = Rot([sb("cb%d" % i, [128, NT + 3]) for i in range(2)])
        caccs = Rot([sb("ca%d" % i, [128, NT]) for i in range(2)])
        absb = sb("absb", [128, 2, 16], F32, 2)
        gsm = sb("gsm", [128, 12, 8])
        gtmp = Rot([sb("gt%d" % i, [128, 128]) for i in range(14)])
        gtb = Rot([sb("gb%d" % i, [128, 128], BF16) for i in range(4)])
        ktm = sb("ktm", [128, H, 128], BF16, H)
        kdtm = sb("kdtm", [128, H, 128], BF16, H)
        vtm = sb("vtm", [128, H, 128], BF16, H)
        kTn = sb("kTn", [128, H, 128], BF16, H)
        qTn = sb("qTn", [128, H, 128], BF16, H)
        otm = sb("otm", [128, H, 128], F32, H)
        ontm = sb("ontm", [128, H, 128], BF16, H)
        nrm = sb("nrm", [128, 3, 8], F32, 3)
        gz = [sb("gz%d" % i, [128, 1024], BF16) for i in range(2)]
        xs = [sb("xs%d" % i, [128, 32, 33]) for i in range(2)]
        xq = [sb("xq%d" % i, [128, 4, 32], F32, 4) for i in range(2)]
        xb = sb("xb", [128, 2, 32, 32], BF16)
        s5ws = [sb("s5ws%d" % i, [128, 16, 128], BF16) for i in range(2)]
        s5ks = [sb("s5ks%d" % i, [128, 8, 128], BF16) for i in range(2)]
        sgt = Rot([sb("sg%d" % i, [128, NT]) for i in range(3)])

        pwv = lnbc.t[:].rearrange("p a d -> p (a d)")[:, 0:4 * 17 * 32].rearrange("p (c k j) -> p c k j", c=4, k=17)
        pwb = lnbc.b
        ct = sb("ct", [128, 2, 32])
        l8 = sb("l8", [128, 1, 2, 32])
        with nc.allow_non_contiguous_dma(reason="tiny s5 param reshuffle"):
            for c in range(2):
                S.add("sp", lambda e, c=c: e.dma_start(
                    out=pwv[:, c, 1, :], in_=s5scr[c].rearrange("(j g) p -> (g p) j", g=2)),
                    reads=[scrbuf], writes=[pwb], lane="c0")
                S.add("sp", lambda e, c=c: e.dma_start(
                    out=pwv[:, c, 9, :], in_=s5scr[2 + c].rearrange("(j g) p -> (g p) j", g=2)),
                    reads=[scrbuf], writes=[pwb], lane="c0")
        S.add("dve", lambda e: e.memset(pwv[:, 0, 0, :], 1.0), writes=[pwb])
        S.add("dve", lambda e: e.memset(pwv[:, 1, 0, :], 0.0), writes=[pwb])

        def cmul(dr, di, ar, ai, br, bi, db, sbs):
            rd = sbs + [ct.b]
            S.add("dve", lambda e: e.tensor_tensor(out=ct.t[:, 0, :], in0=ar, in1=br, op=ALU.mult), reads=rd, writes=[ct.b])
            S.add("dve", lambda e: e.tensor_tensor(out=ct.t[:, 1, :], in0=ai, in1=bi, op=ALU.mult), reads=rd, writes=[ct.b])
            S.add("dve", lambda e: e.tensor_tensor(out=dr, in0=ct.t[:, 0, :], in1=ct.t[:, 1, :], op=ALU.subtract), reads=rd + [db], writes=[db])
            S.add("dve", lambda e: e.tensor_tensor(out=ct.t[:, 0, :], in0=ar, in1=bi, op=ALU.mult), reads=rd + [db], writes=[ct.b])
            S.add("dve", lambda e: e.tensor_tensor(out=ct.t[:, 1, :], in0=ai, in1=br, op=ALU.mult), reads=rd + [db], writes=[ct.b])
            S.add("dve", lambda e: e.tensor_tensor(out=di, in0=ct.t[:, 0, :], in1=ct.t[:, 1, :], op=ALU.add), reads=rd + [db], writes=[db])
        for k in range(1, 8):
            cmul(pwv[:, 0, k + 1, :], pwv[:, 1, k + 1, :], pwv[:, 0, k, :], pwv[:, 1, k, :], pwv[:, 0, 1, :], pwv[:, 1, 1, :], pwb, [pwb])
            cmul(pwv[:, 0, 9 + k, :], pwv[:, 1, 9 + k, :], pwv[:, 0, k, :], pwv[:, 1, k, :], pwv[:, 0, 9, :], pwv[:, 1, 9, :], pwb, [pwb])
        S.add("dve", lambda e: e.tensor_scalar_mul(out=pwv[:, 2:4, :, :], in0=pwv[:, 0:2, :, :], scalar1=-1.0), reads=[pwb], writes=[pwb])
        S.add("dve", lambda e: e.tensor_copy(out=l8.t[:, 0, :, :], in_=pwv[:, 0:2, 8, :]), reads=[pwb], writes=[l8.b])
        hv0 = hb4[0].t[:, :]
        hv1 = hb4[1].t[:, :]
        ldt = [T(hv1[:, i * 512:(i + 1) * 512].rearrange("p (a c) -> p a c", a=4), 1, "ld%d" % i) for i in range(2)]
        ncit = [T(hv1[:, 1024 + i * 128:1024 + (i + 1) * 128], 1, "nci%d" % i) for i in range(2)]
        wtt = Rot([T(hv1[:, 1280 + i * 128:1280 + (i + 1) * 128], 1, "wt%d" % i) for i in range(6)] + [T(hv0[:, 1024 + i * 128:1024 + (i + 1) * 128], 1, "wu%d" % i) for i in range(8)])
        kacc = T(hv0[:, 0:1024].rearrange("p (a c) -> p a c", a=8), 1, "kacc")
        kbf = T(hb16.t[:, 0:1024].rearrange("p (a c) -> p a c", a=8), 1, "kbf")
        spr = Rot([(gp_t[0].t[:, 0, :], gp_t[0].bs[0]), (mm[2].t[:, 0:128], mm[2].b), (gp_t[1].t[:, 0, :], gp_t[1].bs[0]), (mm[3].t[:, 0:128], mm[3].b)])
        for j in range(32):
            ld = ldt[j % 2]
            nci = ncit[j % 2]
            wst = s5ws[0]
            vst = s5ws[1]
            S.add("sp", lambda e, ld=ld, j=j: e.dma_start(out=ld.t[:, 0:2, :], in_=bpadT[:, j].rearrange("a p c -> p a c")),
                  writes=[ld.b], lane="b%d" % (j % 2))
            S.add("sp", lambda e, ld=ld, j=j: e.dma_start(out=ld.t[:, 2:4, :], in_=cpad[:, j].rearrange("a p c -> p a c")),
                  writes=[ld.b], lane="b%d" % (j % 2))
            S.add("act", lambda e, ld=ld, nci=nci: e.activation(out=nci.t[:, :], in_=ld.t[:, 3, :], func=AF.Copy, scale=-1.0),
                  reads=[ld.b], writes=[nci.b])
            Br, Bi, Cr, Ci = ld.t[:, 0, :], ld.t[:, 1, :], ld.t[:, 2, :], ld.t[:, 3, :]
            for k in range(8):
                gr, gi, ngi = pwv[:, 0, 9 + k, j:j + 1], pwv[:, 1, 9 + k, j:j + 1], pwv[:, 3, 9 + k, j:j + 1]
                wtr = wtt.get()
                wti = wtt.get()
                S.add("act", lambda e, wtr=wtr, Br=Br, gr=gr: e.activation(out=wtr.t[:, :], in_=Br, func=AF.Copy, scale=gr),
                      reads=[ld.b, pwb], writes=[wtr.b])
                S.add("dve", lambda e, wtr=wtr, Bi=Bi, ngi=ngi: e.scalar_tensor_tensor(out=wtr.t[:, :], in0=Bi, scalar=ngi, in1=wtr.t[:, :], op0=ALU.mult, op1=ALU.add),
                      reads=[ld.b, pwb, wtr.b], writes=[wtr.b])
                S.add("act", lambda e, wti=wti, Br=Br, gi=gi: e.activation(out=wti.t[:, :], in_=Br, func=AF.Copy, scale=gi),
                      reads=[ld.b, pwb], writes=[wti.b])
                S.add("dve", lambda e, wti=wti, Bi=Bi, gr=gr: e.scalar_tensor_tensor(out=wti.t[:, :], in0=Bi, scalar=gr, in1=wti.t[:, :], op0=ALU.mult, op1=ALU.add),
                      reads=[ld.b, pwb, wti.b], writes=[wti.b])
                kp = mm[k // 4]
                S.add("pe", lambda e, kp=kp, k=k, wtr=wtr, Cr=Cr: e.matmul(kp.t[:, (k % 4) * 128:(k % 4 + 1) * 128], lhsT=wtr.t[:, :], rhs=Cr, start=True, stop=False),
                      reads=[wtr.b, ld.b], writes=[kp.b])
                S.add("pe", lambda e, kp=kp, k=k, wti=wti, nci=nci: e.matmul(kp.t[:, (k % 4) * 128:(k % 4 + 1) * 128], lhsT=wti.t[:, :], rhs=nci.t[:, :], start=False, stop=True),
                      reads=[wti.b, nci.b], writes=[kp.b])
                sp_ = 7 - k
                for c, wsrc in ((0, wtr), (1, wti)):
                    pt_, ptb = spr.get()
                    S.add("pe", lambda e, pt_=pt_, wsrc=wsrc: e.transpose(out=pt_[:, :], in_=wsrc.t[:, :], identity=ident),
                          reads=[wsrc.b, cm.b], writes=[ptb])
                    S.add("act", lambda e, pt_=pt_, sp_=sp_, c=c, wst=wst: e.activation(out=wst.t[:, sp_ * 2 + c, :], in_=pt_[:, :], func=AF.Copy),
                          reads=[ptb], writes=[wst.b])
                ar, ai, nar, nai = pwv[:, 0, k + 1, j:j + 1], pwv[:, 1, k + 1, j:j + 1], pwv[:, 2, k + 1, j:j + 1], pwv[:, 3, k + 1, j:j + 1]
                v1 = wtt.get()
                S.add("act", lambda e, v1=v1, Cr=Cr, ar=ar: e.activation(out=v1.t[:, :], in_=Cr, func=AF.Copy, scale=ar),
                      reads=[ld.b, pwb], writes=[v1.b])
                S.add("dve", lambda e, v1=v1, Ci=Ci, nai=nai, k=k, vst=vst: e.scalar_tensor_tensor(out=vst.t[:, k * 2, :], in0=Ci, scalar=nai, in1=v1.t[:, :], op0=ALU.mult, op1=ALU.add),
                      reads=[ld.b, pwb, v1.b], writes=[vst.b])
                v2 = wtt.get()
                S.add("act", lambda e, v2=v2, Cr=Cr, nai=nai: e.activation(out=v2.t[:, :], in_=Cr, func=AF.Copy, scale=nai),
                      reads=[ld.b, pwb], writes=[v2.b])
                S.add("dve", lambda e, v2=v2, Ci=Ci, nar=nar, k=k, vst=vst: e.scalar_tensor_tensor(out=vst.t[:, k * 2 + 1, :], in0=Ci, scalar=nar, in1=v2.t[:, :], op0=ALU.mult, op1=ALU.add),
                      reads=[ld.b, pwb, v2.b], writes=[vst.b])
            for half in range(2):
                kp = mm[half]
                if j % 4 == 0:
                    S.add("act", lambda e, kp=kp, half=half: e.activation(out=kacc.t[:, half * 4:half * 4 + 4, :], in_=kp.t[:, :].rearrange("p (a c) -> p a c", a=4), func=AF.Copy),
                          reads=[kp.b], writes=[kacc.b])
                else:
                    S.add("dve", lambda e, kp=kp, half=half: e.tensor_tensor(out=kacc.t[:, half * 4:half * 4 + 4, :], in0=kp.t[:, :].rearrange("p (a c) -> p a c", a=4),
                                                                          in1=kacc.t[:, half * 4:half * 4 + 4, :], op=ALU.add),
                          reads=[kp.b, kacc.b], writes=[kacc.b])
            if j % 4 == 3:
                S.add("act", lambda e: e.activation(out=kbf.t[:, :, :], in_=kacc.t[:, :, :], func=AF.Copy), reads=[kacc.b], writes=[kbf.b])
                S.add("sp", lambda e, j=j: e.dma_start(out=kscr[j // 4], in_=kbf.t[:, :, :].rearrange("p a c -> p (a c)")),
                      reads=[kbf.b], writes=[swbuf], lane="b%d" % (j % 2))
            S.add("sp", lambda e, j=j, wst=wst: e.dma_start(out=s5w2[j, :, 0:2048], in_=wst.t[:].rearrange("p a c -> p (a c)")),
                  reads=[wst.b], writes=[swbuf], lane="b%d" % (j % 2))
            S.add("sp", lambda e, j=j, vst=vst: e.dma_start(out=s5w2[j, :, 2048:4096], in_=vst.t[:].rearrange("p a c -> p (a c)")),
                  reads=[vst.b], writes=[swbuf], lane="b%d" % (j % 2))
        S.add("pool", lambda e: e.memset(ct.t[:], 0.0),
              reads=[b_.b for b_ in ldt + ncit + wtt.items + [kacc, kbf]] + [pwb],
              writes=[hb4[0].b, hb4[1].b, hb16.b, lnbc.b, ct.b])
        def load_w(src_ap, nel):
            w = wr.get()
            S.add("sp", lambda e, w=w, src_ap=src_ap, nel=nel: e.dma_start(out=w.t[:, 0:nel], in_=src_ap),
                  reads=[wsbuf], writes=[w.b], lane=wlane.get())
            return w

        def convert_w(src_ap, dst_ap, nel):
            w = wr.get()
            S.add("pool", lambda e, w=w: e.dma_start(out=w.t[:, 0:nel], in_=src_ap), writes=[w.b], lane=cwl.get())
            S.add("sp", lambda e, w=w: e.dma_start(out=dst_ap, in_=w.t[:, 0:nel]), reads=[w.b, wsbuf], lane=cvl.get())

        cvl = Rot(["cv0", "cv1", "cv2", "cv3"])
        cwl = Rot(["cw0", "cw1", "cw2", "cw3"])
        S.add("pool", lambda e: e.dma_start(out=wab.t[:].rearrange("p a b -> p (a b)"), in_=w_in_ab[:, :]),
              writes=[wab.b], lane="wab")
        for i in range(16):
            convert_w(w_in_s[i], c_in_s[i], 4096)
        for i in range(4):
            convert_w(w_in_z[i], c_in_z[i], 4096)
        for i in range(2):
            convert_w(w_glu[i], c_glu[i], 4096)
        for i in range(8):
            convert_w(w_out[i], c_out[i], 4096)
        for i in range(22):
            convert_w(w_gate[i], c_gate[i], 4096)
            convert_w(w_up[i], c_up[i], 4096)
        for i in range(8):
            for k in range(4):
                convert_w(w_down[i, k], c_down[i, k], 11 * 256)
        S.add("sp", lambda e: e.nop(), writes=[wsbuf])

        ln_cur = [-1]

        def layer_norm(hb, n, gi):
            stats = stats_l[ln_i[0] % 2]
            mv = mv_l[ln_i[0] % 2]
            ln_i[0] += 1
            if ln_cur[0] != gi:
                ln_cur[0] = gi
                S.add("sp", lambda e: e.dma_start(out=lnbc.t[:].rearrange("p a d -> p (a d)"),
                                                  in_=lnp[gi:gi + 2, :].rearrange("a d -> (a d)").partition_broadcast(128)),
                      writes=[lnbc.b], lane="ln")
            for c in range(4):
                S.add("dve", lambda e, c=c: e.bn_stats(out=stats.t[:n, c, :], in_=hb.t[:n, c * 512:(c + 1) * 512]),
                      reads=[hb.b], writes=[stats.b])
            S.add("dve", lambda e: e.bn_aggr(out=mv.t[:n, 0:2], in_=stats.t[:n].rearrange("p a b -> p (a b)")),
                  reads=[stats.b], writes=[mv.b])
            S.add("dve", lambda e: e.tensor_scalar(out=mv.t[:n, 2:3], in0=mv.t[:n, 1:2], scalar1=LN_EPS, scalar2=None, op0=ALU.add),
                  reads=[mv.b], writes=[mv.b])
            S.add("pool", lambda e: e.tensor_tensor(out=mv.t[:n, 2:3], in0=mv.t[:n, 2:3], in1=mhalf.t[:n, 0:1], op=ALU.pow),
                  reads=[mv.b, mhalf.b], writes=[mv.b])
            S.add("dve", lambda e: e.scalar_tensor_tensor(out=mv.t[:n, 3:4], in0=mv.t[:n, 0:1], scalar=-1.0, in1=mv.t[:n, 2:3],
                                                          op0=ALU.mult, op1=ALU.mult),
                  reads=[mv.b], writes=[mv.b])
            S.add("act", lambda e: e.activation(out=hb.t[:n, :], in_=hb.t[:n, :], func=AF.Identity,
                                                scale=mv.t[:n, 2:3], bias=mv.t[:n, 3:4]),
                  reads=[hb.b, mv.b], writes=[hb.b])
            S.add("dve", lambda e: e.tensor_tensor(out=hb.t[:n, :], in0=hb.t[:n, :], in1=lnbc.t[:n, 0, :], op=ALU.mult),
                  reads=[hb.b, lnbc.b], writes=[hb.b])
            S.add("dve", lambda e: e.tensor_tensor(out=hb.t[:n, :], in0=hb.t[:n, :], in1=lnbc.t[:n, 1, :], op=ALU.add),
                  reads=[hb.b, lnbc.b], writes=[hb.b])

        def to_actT(hb, n, c0, actT):
            S.add("act", lambda e: e.activation(out=hb16.t[:n, :], in_=hb.t[:n, :], func=AF.Copy),
                  reads=[hb.b], writes=[hb16.b])
            for half in range(2):
                p = tpr.get()
                for k in range(8):
                    kc = half * 8 + k
                    S.add("pe", lambda e, p=p, k=k, kc=kc: e.transpose(out=p.t[:, k, 0:n], in_=hb16.t[:n, kc * 128:(kc + 1) * 128],
                                                                       identity=identb.t[:n, :n]),
                          reads=[hb16.b, identb.b], writes=[p.b])
                S.add("act", lambda e, p=p, half=half: e.activation(out=actT.t[:, half * 8:half * 8 + 8, c0:c0 + n],
                                                                    in_=p.t[:, :, 0:n], func=AF.Copy),
                      reads=[p.b], writes=[actT.b])

        def pe_T(dst_ap, dst_b, src_ap, src_b, np_, nf, evac="act", scale=None, scale_b=None):
            p = tpr.get()
            S.add("pe", lambda e: e.transpose(out=p.t[:nf, 0, 0:np_], in_=src_ap, identity=identb.t[:np_, :np_]),
                  reads=[src_b, identb.b], writes=[p.b])
            S.add("act", lambda e: e.activation(out=dst_ap, in_=p.t[:nf, 0, 0:np_], func=AF.Copy),
                  reads=[p.b], writes=[dst_b])

        def rsq(dst, src, n, b, mul, add):
            S.add("dve", lambda e: e.tensor_scalar(out=dst, in0=src, scalar1=mul, scalar2=add, op0=ALU.mult, op1=ALU.add),
                  reads=[b], writes=[b])
            S.add("pool", lambda e: e.tensor_tensor(out=dst, in0=dst, in1=mhalf.t[:n, :], op=ALU.pow),
                  reads=[b, mhalf.b], writes=[b])

        def phase_A(k, blocks):
            hbuf = [hb4[(2 * k) % 4], hb4[(2 * k + 1) % 4]]
            actT = actTs[k % 2]
            c = 0
            for bi, (n, rows, orow) in enumerate(blocks):
                hb = hbuf[bi]
                for (r0, nr, src) in rows:
                    S.add("sp", lambda e, hb=hb, r0=r0, nr=nr, src=src: e.dma_start(out=hb.t[r0:r0 + nr, :], in_=src),
                          writes=[hb.b], lane="x%d" % ((2 * k + bi) % 4))
                yield
                layer_norm(hb, n, 0)
                for _ in range(6):
                    yield
                to_actT(hb, n, c, actT)
                yield
                c += n

        preloaded = []

        def run_tile(k, blocks, prefetch):
            hbuf = [hb4[(2 * k) % 4], hb4[(2 * k + 1) % 4]]
            actT = actTs[k % 2]
            has_next = prefetch is not None
            N = sum(b_[0] for b_ in blocks)
            c0s = []
            c = 0
            for b_ in blocks:
                c0s.append(c)
                c += b_[0]
            nb = len(blocks)
            abps = []
            for bi, (n, rows, orow) in enumerate(blocks):
                pa, pb = gpr.get()
                for kc in range(KC):
                    S.add("pe", lambda e, pa=pa, kc=kc, n=n, c0=c0s[bi]: e.matmul(
                        pa[:n, 0:16], lhsT=actT.t[:, kc, c0:c0 + n], rhs=wab.t[:, kc, :], start=(kc == 0), stop=(kc == KC - 1)),
                        reads=[actT.b, wab.b], writes=[pb])
                S.add("act", lambda e, pa=pa, n=n, bi=bi: e.activation(out=absb.t[:n, bi, :], in_=pa[:n, 0:16], func=AF.Copy),
                      reads=[pb], writes=[absb.bs[bi]])
                abps.append((absb.t[:, bi, :], absb.bs[bi]))
            for bt_ in range(16):
                w = preloaded.pop(0) if preloaded else load_w(c_in_s[bt_], 4096)
                wv = w.t[:].rearrange("p (m k c) -> p m k c", m=2, k=KC)
                for mi in range(2):
                    mt = bt_ * 2 + mi
                    p = mmr.get()
                    for kc in range(KC):
                        S.add("pe", lambda e, p=p, wv=wv, mi=mi, kc=kc: e.matmul(
                            p.t[:, 0:N], lhsT=wv[:, mi, kc, :], rhs=actT.t[:, kc, 0:N], start=(kc == 0), stop=(kc == KC - 1)),
                            reads=[w.b, actT.b], writes=[p.b])
                    if mt < 24:
                        cb = cbufs.get()
                        ca = caccs.get()
                        hsrc = hist
                        S.add("act", lambda e, cb=cb, p=p: e.activation(out=cb.t[:, 3:3 + N], in_=p.t[:, 0:N], func=AF.Copy),
                              reads=[p.b], writes=[cb.b])
                        S.add("act", lambda e, cb=cb, mt=mt: e.activation(out=cb.t[:, 0:3], in_=hist.t[:, mt, :], func=AF.Copy),
                              reads=[hist.b, cb.b], writes=[cb.b])
                        S.add("act", lambda e, cb=cb, mt=mt: e.activation(out=hist.t[:, mt, :], in_=cb.t[:, N:N + 3], func=AF.Copy),
                              reads=[cb.b], writes=[hist.b])
                        S.add("dve", lambda e, cb=cb, ca=ca, mt=mt: e.tensor_scalar(
                            out=ca.t[:, 0:N], in0=cb.t[:, 0:N], scalar1=cw.t[:, mt, 0:1], scalar2=None, op0=ALU.mult),
                            reads=[cb.b, cw.b], writes=[ca.b])
                        for i in range(1, 4):
                            S.add("dve", lambda e, cb=cb, ca=ca, mt=mt, i=i: e.scalar_tensor_tensor(
                                out=ca.t[:, 0:N], in0=cb.t[:, i:i + N], scalar=cw.t[:, mt, i:i + 1], in1=ca.t[:, 0:N],
                                op0=ALU.mult, op1=ALU.add),
                                reads=[cb.b, cw.b, ca.b], writes=[ca.b])
                        S.add("act", lambda e, ca=ca, mt=mt: e.activation(out=big.t[:, mt, 0:N], in_=ca.t[:, 0:N], func=AF.Silu),
                              reads=[ca.b], writes=[big.bs[mt]])
                    else:
                        S.add("act", lambda e, p=p, mt=mt: e.activation(out=big.t[:, mt, 0:N], in_=p.t[:, 0:N], func=AF.Copy),
                              reads=[p.b], writes=[big.bs[mt]])
            for cg in range(4):
                w = load_w(c_in_z[cg], 4096)
                wv = w.t[:].rearrange("p (k c) -> p k c", k=KC)
                for bi, (n, rows, orow) in enumerate(blocks):
                    p = mmr.get()
                    for kc in range(KC):
                        S.add("pe", lambda e, p=p, wv=wv, kc=kc, n=n, c0=c0s[bi]: e.matmul(
                            p.t[:n, 0:256], lhsT=actT.t[:, kc, c0:c0 + n], rhs=wv[:, kc, :], start=(kc == 0), stop=(kc == KC - 1)),
                            reads=[w.b, actT.b], writes=[p.b])
                    g_ = gz[bi]
                    S.add("act", lambda e, p=p, g_=g_, n=n, cg=cg: e.activation(out=g_.t[:n, cg * 256:(cg + 1) * 256], in_=p.t[:n, 0:256], func=AF.Silu),
                          reads=[p.b], writes=[g_.b])
                    for hh in range(2):
                        S.add("pool", lambda e, g_=g_, n=n, cg=cg, hh=hh: e.tensor_tensor(
                            out=g_.t[:n, cg * 256 + hh * 128:cg * 256 + hh * 128 + 128],
                            in0=g_.t[:n, cg * 256 + hh * 128:cg * 256 + hh * 128 + 128], in1=normw[:n, :], op=ALU.mult),
                            reads=[g_.b, dn.b], writes=[g_.b])

            def gdn_block(bi, n, c0, pa, pb):
                gs = gsm.t
                gb_ = gsm.b
                S.add("dve", lambda e: e.tensor_tensor(out=gs[:n, 0, :], in0=pa[:n, 0:8], in1=dtb[:n, :], op=ALU.add),
                      reads=[pb, dn.b], writes=[gb_])
                S.add("act", lambda e: e.activation(out=gs[:n, 5, :], in_=pa[:n, 8:16], func=AF.Tanh, scale=0.5),
                      reads=[pb], writes=[gb_])
                S.add("act", lambda e: e.activation(out=gs[:n, 0, :], in_=gs[:n, 0, :], func=AF.Exp),
                      reads=[gb_], writes=[gb_])
                S.add("act", lambda e: e.activation(out=gs[:n, 0, :], in_=gs[:n, 0, :], func=AF.Ln, bias=1.0),
                      reads=[gb_], writes=[gb_])
                S.add("dve", lambda e: e.tensor_tensor(out=gs[:n, 0, :], in0=gs[:n, 0, :], in1=negA.t[:n, :], op=ALU.mult),
                      reads=[gb_, negA.b], writes=[gb_])
                S.add("dve", lambda e: e.tensor_scalar(out=gs[:n, 5, :], in0=gs[:n, 5, :], scalar1=0.5, scalar2=0.5, op0=ALU.mult, op1=ALU.add),
                      reads=[gb_], writes=[gb_])
                S.add("dve", lambda e: e.tensor_scalar_mul(out=gs[:n, 6, :], in0=gs[:n, 5, :], scalar1=-1.0),
                      reads=[gb_], writes=[gb_])
                pc, pcb = gpr.get()
                S.add("pe", lambda e, pc=pc: e.matmul(pc[:n, 0:8], lhsT=uincl[:n, :n], rhs=gs[:n, 0, :], start=True, stop=True),
                      reads=[cm.b, gb_], writes=[pcb])
                S.add("pe", lambda e, pc=pc: e.matmul(pc[:, 8:16], lhsT=ones[:n, :], rhs=gs[:n, 0, :], start=True, stop=True),
                      reads=[cm.b, gb_], writes=[pcb])
                S.add("act", lambda e, pc=pc: e.activation(out=gs[:n, 1, :], in_=pc[:n, 0:8], func=AF.Copy),
                      reads=[pcb], writes=[gb_])
                S.add("act", lambda e, pc=pc: e.activation(out=gs[:n, 2, :], in_=pc[:n, 0:8], func=AF.Exp),
                      reads=[pcb], writes=[gb_])
                S.add("act", lambda e, pc=pc: e.activation(out=gs[:, 7, :], in_=pc[:, 8:16], func=AF.Exp),
                      reads=[pcb], writes=[gb_])
                S.add("dve", lambda e, pc=pc: e.tensor_tensor(out=gs[:n, 4, :], in0=pc[:n, 8:16], in1=gs[:n, 1, :], op=ALU.subtract),
                      reads=[pcb, gb_], writes=[gb_])
                S.add("act", lambda e: e.activation(out=gs[:n, 4, :], in_=gs[:n, 4, :], func=AF.Exp),
                      reads=[gb_], writes=[gb_])
                S.add("dve", lambda e: e.tensor_scalar_mul(out=gs[:n, 3, :], in0=gs[:n, 2, :], scalar1=-1.0),
                      reads=[gb_], writes=[gb_])
                for h in range(H):
                    p = tpr.get()
                    S.add("pe", lambda e, p=p, h=h: e.transpose(out=p.t[:n, 0, :], in_=big.t[:, 8 + h, c0:c0 + n], identity=identb.t[:, :]),
                          reads=[big.bs[8 + h], identb.b], writes=[p.b])
                    S.add("pe", lambda e, p=p, h=h: e.transpose(out=p.t[:n, 1, :], in_=big.t[:, h, c0:c0 + n], identity=identb.t[:, :]),
                          reads=[big.bs[h], identb.b], writes=[p.b])
                    S.add("pe", lambda e, p=p, h=h: e.transpose(out=p.t[:n, 2, :], in_=big.t[:, 16 + h, c0:c0 + n], identity=identb.t[:, :]),
                          reads=[big.bs[16 + h], identb.b], writes=[p.b])
                    jk = gtmp.get()
                    S.add("act", lambda e, p=p, h=h, jk=jk: e.activation(out=jk.t[:n, :], in_=p.t[:n, 0, :], func=AF.Square,
                                                                        accum_out=nrm.t[:n, 0, h:h + 1]),
                          reads=[p.b], writes=[jk.b, nrm.bs[0]])
                    jq = gtmp.get()
                    S.add("act", lambda e, p=p, h=h, jq=jq: e.activation(out=jq.t[:n, :], in_=p.t[:n, 1, :], func=AF.Square,
                                                                        accum_out=nrm.t[:n, 1, h:h + 1]),
                          reads=[p.b], writes=[jq.b, nrm.bs[1]])
                    S.add("act", lambda e, p=p, h=h: e.activation(out=vtm.t[:n, h, :], in_=p.t[:n, 2, :], func=AF.Copy),
                          reads=[p.b], writes=[vtm.bs[h]])
                    S.add("act", lambda e, p=p, h=h: e.activation(out=kdtm.t[:n, h, :], in_=p.t[:n, 0, :], func=AF.Copy),
                          reads=[p.b], writes=[kdtm.bs[h]])
                    S.add("act", lambda e, p=p, h=h: e.activation(out=ontm.t[:n, h, :], in_=p.t[:n, 1, :], func=AF.Copy),
                          reads=[p.b], writes=[ontm.bs[h]])
                    yield
                rsq(nrm.t[:n, 0, :], nrm.t[:n, 0, :], n, nrm.bs[0], 1.0, RMS_EPS)
                rsq(nrm.t[:n, 1, :], nrm.t[:n, 1, :], n, nrm.bs[1], 128.0, 128.0 * RMS_EPS)
                for h in range(H):
                    S.add("act", lambda e, h=h: e.activation(out=ktm.t[:n, h, :], in_=kdtm.t[:n, h, :], func=AF.Copy, scale=nrm.t[:n, 0, h:h + 1]),
                          reads=[kdtm.bs[h], nrm.bs[0]], writes=[ktm.bs[h]])
                    S.add("act", lambda e, h=h: e.activation(out=ontm.t[:n, h, :], in_=ontm.t[:n, h, :], func=AF.Copy, scale=nrm.t[:n, 1, h:h + 1]),
                          reads=[ontm.bs[h], nrm.bs[1]], writes=[ontm.bs[h]])
                    S.add("pool", lambda e, h=h: e.tensor_scalar(out=kdtm.t[:n, h, :], in0=ktm.t[:n, h, :], scalar1=gs[:n, 4, h:h + 1],
                                                                 scalar2=None, op0=ALU.mult),
                          reads=[ktm.bs[h], gb_], writes=[kdtm.bs[h]])
                    p = tpr.get()
                    S.add("pe", lambda e, p=p, h=h: e.transpose(out=p.t[:, 0, 0:n], in_=ktm.t[:n, h, :], identity=identb.t[:n, :n]),
                          reads=[ktm.bs[h], identb.b], writes=[p.b])
                    S.add("pe", lambda e, p=p, h=h: e.transpose(out=p.t[:, 1, 0:n], in_=ontm.t[:n, h, :], identity=identb.t[:n, :n]),
                          reads=[ontm.bs[h], identb.b], writes=[p.b])
                    S.add("act", lambda e, p=p, h=h: e.activation(out=kTn.t[:, h, 0:n], in_=p.t[:, 0, 0:n], func=AF.Copy),
                          reads=[p.b], writes=[kTn.bs[h]])
                    S.add("act", lambda e, p=p, h=h: e.activation(out=qTn.t[:, h, 0:n], in_=p.t[:, 1, 0:n], func=AF.Copy),
                          reads=[p.b], writes=[qTn.bs[h]])
                    yield
                def head_chain(h, gpr, gtmp, gtb):
                    kT = kTn.t[:, h, 0:n]
                    qT = qTn.t[:, h, 0:n]
                    yield
                    pkk, pkkb = gpr.get()
                    S.add("pe", lambda e, pkk=pkk, kT=kT: e.matmul(pkk[:n, :n], lhsT=kT, rhs=kT, start=True, stop=True),
                          reads=[kTn.bs[h]], writes=[pkkb])
                    pqk, pqkb = gpr.get()
                    S.add("pe", lambda e, pqk=pqk, kT=kT, qT=qT: e.matmul(pqk[:n, :n], lhsT=kT, rhs=qT, start=True, stop=True),
                          reads=[kTn.bs[h], qTn.bs[h]], writes=[pqkb])
                    dg = gtmp.get()
                    S.add("act", lambda e, dg=dg, h=h: e.activation(out=dg.t[:n, :n], in_=ident[:n, :n], func=AF.Copy, scale=gs[:n, 1, h:h + 1]),
                          reads=[cm.b, gb_], writes=[dg.b])
                    yield
                    pgb, pgbb = gpr.get()
                    S.add("pe", lambda e, pgb=pgb, dg=dg: e.matmul(pgb[:n, :n], lhsT=ones[:n, :n], rhs=dg.t[:n, :n], start=True, stop=True),
                          reads=[cm.b, dg.b], writes=[pgbb])
                    DT = gtmp.get()
                    S.add("dve", lambda e, DT=DT, pgb=pgb, h=h: e.scalar_tensor_tensor(
                        out=DT.t[:n, :n], in0=pgb[:n, :n], scalar=gs[:n, 1, h:h + 1], in1=negmT[:n, :n], op0=ALU.subtract, op1=ALU.add),
                        reads=[pgbb, gb_, cm.b], writes=[DT.b])
                    S.add("act", lambda e, DT=DT: e.activation(out=DT.t[:n, :n], in_=DT.t[:n, :n], func=AF.Exp),
                          reads=[DT.b], writes=[DT.b])
                    QKm = gtb.get()
                    S.add("dve", lambda e, QKm=QKm, pqk=pqk, DT=DT: e.tensor_tensor(out=QKm.t[:n, :n], in0=pqk[:n, :n], in1=DT.t[:n, :n], op=ALU.mult),
                          reads=[pqkb, DT.b], writes=[QKm.b])
                    DTs = gtmp.get()
                    S.add("pool", lambda e, DTs=DTs, DT=DT: e.tensor_tensor(out=DTs.t[:n, :n], in0=DT.t[:n, :n], in1=strict[:n, :n], op=ALU.mult),
                          reads=[DT.b, cm.b], writes=[DTs.b])
                    cur = gtmp.get()
                    S.add("dve", lambda e, cur=cur, pkk=pkk, DTs=DTs, h=h: e.scalar_tensor_tensor(
                        out=cur.t[:n, :n], in0=pkk[:n, :n], scalar=gs[:n, 6, h:h + 1], in1=DTs.t[:n, :n], op0=ALU.mult, op1=ALU.mult),
                        reads=[pkkb, gb_, DTs.b], writes=[cur.b])
                    yield
                    pt_, ptb = gpr.get()
                    S.add("pe", lambda e, pt_=pt_, cur=cur: e.transpose(out=pt_[:n, :n], in_=cur.t[:n, :n], identity=ident[:n, :n]),
                          reads=[cur.b, cm.b], writes=[ptb])
                    curT = gtmp.get()
                    S.add("act", lambda e, curT=curT, pt_=pt_: e.activation(out=curT.t[:n, :n], in_=pt_[:n, :n], func=AF.Copy),
                          reads=[ptb], writes=[curT.b])
                    P = gtmp.get()
                    S.add("dve", lambda e, P=P, cur=cur: e.tensor_tensor(out=P.t[:n, :n], in0=cur.t[:n, :n], in1=ident[:n, :n], op=ALU.add),
                          reads=[cur.b, cm.b], writes=[P.b])
                    nlev = 6 if n == 128 else 3
                    for k in range(1, nlev + 1):
                        nxt = None
                        if k < nlev:
                            yield
                            pn, pnb = gpr.get()
                            S.add("pe", lambda e, pn=pn, cur=cur, curT=curT: e.matmul(pn[:n, :n], lhsT=curT.t[:n, :n], rhs=cur.t[:n, :n], start=True, stop=True),
                                  reads=[cur.b, curT.b], writes=[pnb])
                            nxt = gtmp.get()
                            S.add("act", lambda e, nxt=nxt, pn=pn: e.activation(out=nxt.t[:n, :n], in_=pn[:n, :n], func=AF.Copy),
                                  reads=[pnb], writes=[nxt.b])
                        pn2, pn2b = gpr.get()
                        S.add("pe", lambda e, pn2=pn2, cur=cur, curT=curT: e.matmul(pn2[:n, :n], lhsT=cur.t[:n, :n], rhs=curT.t[:n, :n], start=True, stop=True),
                              reads=[cur.b, curT.b], writes=[pn2b])
                        nxtT = gtmp.get()
                        S.add("act", lambda e, nxtT=nxtT, pn2=pn2: e.activation(out=nxtT.t[:n, :n], in_=pn2[:n, :n], func=AF.Copy),
                              reads=[pn2b], writes=[nxtT.b])
                        yield
                        pp, ppb = gpr.get()
                        S.add("pe", lambda e, pp=pp, nxtT=nxtT, P=P: e.matmul(pp[:n, :n], lhsT=nxtT.t[:n, :n], rhs=P.t[:n, :n], start=True, stop=True),
                              reads=[nxtT.b, P.b], writes=[ppb])
                        P2 = gtmp.get()
                        S.add("dve", lambda e, P2=P2, P=P, pp=pp: e.tensor_tensor(out=P2.t[:n, :n], in0=pp[:n, :n], in1=P.t[:n, :n], op=ALU.add),
                              reads=[ppb, P.b], writes=[P2.b])
                        P = P2
                        cur, curT = nxt, nxtT
                    yield
                    pks, pksb = gpr.get()
                    S.add("pe", lambda e, pks=pks, kT=kT, h=h: e.matmul(pks[:n, :], lhsT=kT, rhs=Sbf.t[:, h, :], start=True, stop=True),
                          reads=[kTn.bs[h], Sbf.bs[h]], writes=[pksb])
                    U = gtmp.get()
                    S.add("dve", lambda e, U=U, pks=pks, h=h: e.scalar_tensor_tensor(
                        out=U.t[:n, :], in0=pks[:n, :], scalar=gs[:n, 3, h:h + 1], in1=vtm.t[:n, h, :], op0=ALU.mult, op1=ALU.add),
                        reads=[pksb, gb_, vtm.bs[h]], writes=[U.b])
                    yield
                    pw, pwb = gpr.get()
                    S.add("pe", lambda e, pw=pw, P=P, U=U: e.matmul(pw[:n, :], lhsT=P.t[:n, :n], rhs=U.t[:n, :], start=True, stop=True),
                          reads=[P.b, U.b], writes=[pwb])
                    wt = gtb.get()
                    S.add("act", lambda e, wt=wt, pw=pw, h=h: e.activation(out=wt.t[:n, :], in_=pw[:n, :], func=AF.Copy, scale=gs[:n, 5, h:h + 1]),
                          reads=[pwb, gb_], writes=[wt.b])
                    yield
                    po1, po1b = gpr.get()
                    S.add("pe", lambda e, po1=po1, qT=qT, h=h: e.matmul(po1[:n, :], lhsT=qT, rhs=Sbf.t[:, h, :], start=True, stop=True),
                          reads=[qTn.bs[h], Sbf.bs[h]], writes=[po1b])
                    po2, po2b = gpr.get()
                    S.add("pe", lambda e, po2=po2, QKm=QKm, wt=wt: e.matmul(po2[:n, :], lhsT=QKm.t[:n, :n], rhs=wt.t[:n, :], start=True, stop=True),
                          reads=[QKm.b, wt.b], writes=[po2b])
                    t1 = gtmp.get()
                    S.add("act", lambda e, t1=t1, po1=po1, h=h: e.activation(out=t1.t[:n, :], in_=po1[:n, :], func=AF.Copy, scale=gs[:n, 2, h:h + 1]),
                          reads=[po1b, gb_], writes=[t1.b])
                    S.add("dve", lambda e, t1=t1, po2=po2, h=h: e.tensor_tensor(out=otm.t[:n, h, :], in0=po2[:n, :], in1=t1.t[:n, :], op=ALU.add),
                          reads=[po2b, t1.b], writes=[otm.bs[h]])
                    psu, psub = gpr.get()
                    S.add("pe", lambda e, psu=psu, wt=wt, h=h: e.matmul(psu[:, :], lhsT=kdtm.t[:n, h, :], rhs=wt.t[:n, :], start=True, stop=True),
                          reads=[kdtm.bs[h], wt.b], writes=[psub])
                    S.add("dve", lambda e, psu=psu, h=h: e.scalar_tensor_tensor(
                        out=Sst.t[:, h, :], in0=Sst.t[:, h, :], scalar=gs[:, 7, h:h + 1], in1=psu[:, :], op0=ALU.mult, op1=ALU.add),
                        reads=[Sst.bs[h], gb_, psub], writes=[Sst.bs[h]])
                    S.add("act", lambda e, h=h: e.activation(out=Sbf.t[:, h, :], in_=Sst.t[:, h, :], func=AF.Copy),
                          reads=[Sst.bs[h]], writes=[Sbf.bs[h]])
                    j2 = gtmp.get()
                    S.add("act", lambda e, j2=j2, h=h: e.activation(out=j2.t[:n, :], in_=otm.t[:n, h, :], func=AF.Square, accum_out=nrm.t[:n, 2, h:h + 1]),
                          reads=[otm.bs[h]], writes=[j2.b, nrm.bs[2]])
                lanes = [(Rot([(gp_t[i].t[:, j, :], gp_t[i].bs[0]) for j in range(4)]),
                          Rot(gtmp.items[i * 7:(i + 1) * 7]), Rot(gtb.items[i * 2:(i + 1) * 2])) for i in range(2)]
                for h0 in range(0, H, 2):
                    alive = [head_chain(h0 + i, *lanes[i]) for i in range(2)]
                    while alive:
                        for g in list(alive):
                            try:
                                next(g)
                            except StopIteration:
                                alive.remove(g)
                        yield
                rsq(nrm.t[:n, 2, :], nrm.t[:n, 2, :], n, nrm.bs[2], 1.0 / 128.0, RMS_EPS)
                g_ = gz[bi]
                for h in range(H):
                    S.add("dve", lambda e, h=h, g_=g_: e.scalar_tensor_tensor(
                        out=ontm.t[:n, h, :], in0=otm.t[:n, h, :], scalar=nrm.t[:n, 2, h:h + 1], in1=g_.t[:n, h * 128:(h + 1) * 128],
                        op0=ALU.mult, op1=ALU.mult),
                        reads=[otm.bs[h], nrm.bs[2], g_.b], writes=[ontm.bs[h]])
                p = tpr.get()
                for h in range(H):
                    S.add("pe", lambda e, p=p, h=h: e.transpose(out=p.t[:, h, 0:n], in_=ontm.t[:n, h, :], identity=identb.t[:n, :n]),
                          reads=[ontm.bs[h], identb.b], writes=[p.b])
                S.add("act", lambda e, p=p: e.activation(out=mixT.t[:, 0:8, c0:c0 + n], in_=p.t[:, :, 0:n], func=AF.Copy),
                      reads=[p.b], writes=[mixT.b])

            pending_gdn = [(bi, n) for bi, (n, rows, orow) in enumerate(blocks)]

            def s5_gen():
                C = N // 8
                L = C + 1
                u3 = [big.t[:, 24 + ft, 0:N].rearrange("p (c s) -> p c s", s=8) for ft in range(8)]
                for j in range(32):
                    ft = j // 4
                    wv = s5ws[j % 2]
                    S.add("sp", lambda e, wv=wv, j=j: e.dma_start(out=wv.t[:].rearrange("p a c -> p (a c)"), in_=s5w2[j, :, 0:2048]),
                          reads=[swbuf], writes=[wv.b], lane="s5%d" % (j % 2))
                    for c in range(2):
                        ps = mm[2 * c + j // 16]
                        col = (j % 16) * 32
                        for s_ in range(8):
                            S.add("pe", lambda e, ps=ps, col=col, wv=wv, s_=s_, c=c, ft=ft, j=j: e.matmul(
                                ps.t[:, col:col + C], lhsT=wv.t[:, s_ * 2 + c, :], rhs=u3[ft][:, :, s_], start=(s_ == 0), stop=(s_ == 7)),
                                reads=[wv.b, big.bs[24 + ft]], writes=[ps.b])
                    yield
                xa = [xs[0], xs[1]]
                for c in range(2):
                    for hf in range(2):
                        ps = mm[2 * c + hf]
                        S.add("act", lambda e, ps=ps, c=c, hf=hf, X=xa[c]: e.activation(
                            out=X.t[:, hf * 16:hf * 16 + 16, 1:1 + C], in_=ps.t[:, :].rearrange("p (j q) -> p j q", q=32)[:, :, 0:C], func=AF.Copy),
                            reads=[ps.b], writes=[xa[c].b])
                    S.add("pool", lambda e, c=c, X=xa[c]: e.tensor_copy(out=X.t[:, :, 0], in_=xprev.t[:, c, :]),
                          reads=[xprev.b, xa[c].b], writes=[xa[c].b])
                yield
                Xr, Xi = xa
                LR = l8.t[:, 0, 0, :]
                LI = l8.t[:, 0, 1, :]

                def tt(out_ap, a_ap, b_ap, op, rd, wr):
                    S.add("pool", lambda e: e.tensor_tensor(out=out_ap, in0=a_ap, in1=b_ap, op=op), reads=rd, writes=wr)
                for cc in range(1, L):
                    tq = xq[cc % 2]
                    q0, q1, q2, q3 = tq.bs
                    tt(tq.t[:, 0, :], Xr.t[:, :, cc - 1], LR, ALU.mult, [Xr.b, l8.b], [q0])
                    tt(tq.t[:, 1, :], Xi.t[:, :, cc - 1], LI, ALU.mult, [Xi.b, l8.b], [q1])
                    tt(tq.t[:, 2, :], Xi.t[:, :, cc - 1], LR, ALU.mult, [Xi.b, l8.b], [q2])
                    tt(tq.t[:, 3, :], Xr.t[:, :, cc - 1], LI, ALU.mult, [Xr.b, l8.b], [q3])
                    tt(tq.t[:, 0, :], tq.t[:, 0, :], tq.t[:, 1, :], ALU.subtract, [q0, q1], [q0])
                    tt(tq.t[:, 2, :], tq.t[:, 2, :], tq.t[:, 3, :], ALU.add, [q2, q3], [q2])
                    tt(Xr.t[:, :, cc], Xr.t[:, :, cc], tq.t[:, 0, :], ALU.add, [Xr.b, q0], [Xr.b])
                    tt(Xi.t[:, :, cc], Xi.t[:, :, cc], tq.t[:, 2, :], ALU.add, [Xi.b, q2], [Xi.b])
                    if cc % 4 == 0:
                        yield
                for c in range(2):
                    S.add("act", lambda e, c=c, X=xa[c]: e.activation(out=xb.t[:, c, :, 0:C], in_=X.t[:, :, 0:C], func=AF.Copy),
                          reads=[xa[c].b, xb.b], writes=[xb.b])
                    S.add("pool", lambda e, c=c, X=xa[c]: e.tensor_copy(out=xprev.t[:, c, :], in_=X.t[:, :, C]),
                          reads=[xa[c].b, xprev.b], writes=[xprev.b])
                for ft in range(8):
                    kb = s5ks[ft % 2]
                    S.add("sp", lambda e, kb=kb, ft=ft: e.dma_start(out=kb.t[:].rearrange("p a c -> p (a c)"), in_=kscr[ft]),
                          reads=[swbuf], writes=[kb.b], lane="s5k%d" % (ft % 2))
                    yp = mmr.get()
                    y3 = yp.t[:, 0:N].rearrange("p (c s) -> p c s", s=8)
                    for hf in range(4):
                        vb = s5ws[hf % 2]
                        S.add("sp", lambda e, vb=vb, ft=ft, hf=hf: e.dma_start(
                            out=vb.t[:].rearrange("p (q a) c -> p q (a c)", q=4),
                            in_=s5w2[ft * 4:ft * 4 + 4, :, 2048 + hf * 512:2048 + (hf + 1) * 512].rearrange("q p f -> p q f")),
                            reads=[swbuf], writes=[vb.b], lane="s5%d" % (hf % 2))
                        for s_ in range(2 * hf, 2 * hf + 2):
                            for sp_ in range(s_ + 1):
                                S.add("pe", lambda e, y3=y3, kb=kb, s_=s_, sp_=sp_, ft=ft: e.matmul(
                                    y3[:, :, s_], lhsT=kb.t[:, s_ - sp_, :], rhs=u3[ft][:, :, sp_], start=(sp_ == 0), stop=False),
                                    reads=[kb.b, big.bs[24 + ft]], writes=[yp.b])
                            for jj in range(4):
                                for c in range(2):
                                    S.add("pe", lambda e, y3=y3, vb=vb, s_=s_, c=c, jj=jj, ft=ft, hf=hf: e.matmul(
                                        y3[:, :, s_], lhsT=vb.t[:, jj * 4 + (s_ - 2 * hf) * 2 + c, :], rhs=xb.t[:, c, ft * 4 + jj, 0:C], start=False,
                                        stop=(jj == 3 and c == 1)),
                                        reads=[vb.b, xb.b], writes=[yp.b])
                            yield
                    if True:
                        yx = sgt.get()
                        S.add("dve", lambda e, yx=yx, yp=yp, ft=ft: e.scalar_tensor_tensor(
                            out=yx.t[:, 0:N], in0=big.t[:, 24 + ft, 0:N], scalar=d5.t[:, ft:ft + 1], in1=yp.t[:, 0:N], op0=ALU.mult, op1=ALU.add),
                            reads=[big.bs[24 + ft], d5.b, yp.b], writes=[yx.b])
                        sq = sgt.get()
                        S.add("act", lambda e, sq=sq, yx=yx: e.activation(out=sq.t[:, 0:N], in_=yx.t[:, 0:N], func=AF.Square),
                              reads=[yx.b], writes=[sq.b])
                        S.add("dve", lambda e, sq=sq: e.tensor_scalar(out=sq.t[:, 0:N], in0=sq.t[:, 0:N], scalar1=0.044715, scalar2=1.0, op0=ALU.mult, op1=ALU.add),
                              reads=[sq.b], writes=[sq.b])
                        S.add("pool", lambda e, sq=sq, yx=yx: e.tensor_tensor(out=sq.t[:, 0:N], in0=sq.t[:, 0:N], in1=yx.t[:, 0:N], op=ALU.mult),
                              reads=[sq.b, yx.b], writes=[sq.b])
                        S.add("act", lambda e, sq=sq: e.activation(out=sq.t[:, 0:N], in_=sq.t[:, 0:N], func=AF.Tanh, scale=math.sqrt(2.0 / math.pi)),
                              reads=[sq.b], writes=[sq.b])
                        S.add("dve", lambda e, sq=sq: e.tensor_scalar(out=sq.t[:, 0:N], in0=sq.t[:, 0:N], scalar1=0.5, scalar2=0.5, op0=ALU.mult, op1=ALU.add),
                              reads=[sq.b], writes=[sq.b])
                        S.add("pool", lambda e, sq=sq, yx=yx, ft=ft: e.tensor_tensor(out=big.t[:, 32 + ft, 0:N], in0=sq.t[:, 0:N], in1=yx.t[:, 0:N], op=ALU.mult),
                              reads=[sq.b, yx.b], writes=[big.bs[32 + ft]])
            s5g = s5_gen()
            s5_alive = True
            for bi, n in pending_gdn:
                for step, _ in enumerate(gdn_block(bi, n, c0s[bi], abps[bi][0], abps[bi][1])):
                    if s5_alive and step % 4 == 3:
                        try:
                            next(s5g)
                        except StopIteration:
                            s5_alive = False
            if s5_alive:
                for _ in s5g:
                    pass
            for mt in range(8):
                if mt % 4 == 0:
                    w = load_w(c_glu[mt // 4], 4096)
                    wv = w.t[:].rearrange("p (m k c) -> p m k c", m=4, k=8)
                p = mmr.get()
                for kc in range(8):
                    S.add("pe", lambda e, p=p, wv=wv, mt=mt, kc=kc: e.matmul(p.t[:, 0:N], lhsT=wv[:, mt % 4, kc, :], rhs=big.t[:, 32 + kc, 0:N],
                                                                            start=(kc == 0), stop=(kc == 7)),
                          reads=[w.b, big.bs[32 + kc]], writes=[p.b])
                sg = sgt.get()
                S.add("act", lambda e, p=p, sg=sg, mt=mt: e.activation(out=sg.t[:, 0:N], in_=p.t[:, 0:N], func=AF.Tanh, scale=0.5, bias=hbg.t[:, mt:mt + 1]),
                      reads=[p.b, hbg.b], writes=[sg.b])
                S.add("dve", lambda e, sg=sg: e.tensor_scalar(out=sg.t[:, 0:N], in0=sg.t[:, 0:N], scalar1=0.5, scalar2=0.5, op0=ALU.mult, op1=ALU.add),
                      reads=[sg.b], writes=[sg.b])
                S.add("pool", lambda e, sg=sg, mt=mt: e.tensor_tensor(out=mixT.t[:, 8 + mt, 0:N], in0=sg.t[:, 0:N], in1=big.t[:, 32 + mt, 0:N], op=ALU.mult),
                      reads=[sg.b, big.bs[32 + mt]], writes=[mixT.b])
            for cg in range(8):
                w = load_w(c_out[cg], 4096)
                wv = w.t[:].rearrange("p (k c) -> p k c", k=KC)
                for bi, (n, rows, orow) in enumerate(blocks):
                    hb = hbuf[bi]
                    p = mmr.get()
                    for kc in range(KC):
                        S.add("pe", lambda e, p=p, wv=wv, kc=kc, n=n, c0=c0s[bi]: e.matmul(
                            p.t[:n, 0:256], lhsT=mixT.t[:, kc, c0:c0 + n], rhs=wv[:, kc, :], start=(kc == 0), stop=(kc == KC - 1)),
                            reads=[w.b, mixT.b], writes=[p.b])
                    S.add("dve", lambda e, p=p, hb=hb, n=n, cg=cg: e.scalar_tensor_tensor(
                        out=hb.t[:n, cg * 256:(cg + 1) * 256], in0=hb.t[:n, cg * 256:(cg + 1) * 256], scalar=ALPHA, in1=p.t[:n, 0:256],
                        op0=ALU.mult, op1=ALU.add),
                        reads=[hb.b, p.b], writes=[hb.b])
            for bi, (n, rows, orow) in enumerate(blocks):
                layer_norm(hbuf[bi], n, 2)
                to_actT(hbuf[bi], n, c0s[bi], actT)
            for bt_ in range(22):
                if prefetch is not None and bt_ >= 2:
                    try:
                        next(prefetch)
                    except StopIteration:
                        prefetch = None
                wg = load_w(c_gate[bt_], 4096)
                wu = load_w(c_up[bt_], 4096)
                wgv = wg.t[:].rearrange("p (m k c) -> p m k c", m=2, k=KC)
                wuv = wu.t[:].rearrange("p (m k c) -> p m k c", m=2, k=KC)
                for mi in range(2):
                    mt = bt_ * 2 + mi
                    pg = mmr.get()
                    pu = mmr.get()
                    for kc in range(KC):
                        S.add("pe", lambda e, pg=pg, wgv=wgv, mi=mi, kc=kc: e.matmul(
                            pg.t[:, 0:N], lhsT=wgv[:, mi, kc, :], rhs=actT.t[:, kc, 0:N], start=(kc == 0), stop=(kc == KC - 1)),
                            reads=[wg.b, actT.b], writes=[pg.b])
                    for kc in range(KC):
                        S.add("pe", lambda e, pu=pu, wuv=wuv, mi=mi, kc=kc: e.matmul(
                            pu.t[:, 0:N], lhsT=wuv[:, mi, kc, :], rhs=actT.t[:, kc, 0:N], start=(kc == 0), stop=(kc == KC - 1)),
                            reads=[wu.b, actT.b], writes=[pu.b])
                    sg = sgt.get()
                    S.add("act", lambda e, pg=pg, sg=sg: e.activation(out=sg.t[:, 0:N], in_=pg.t[:, 0:N], func=AF.Silu),
                          reads=[pg.b], writes=[sg.b])
                    S.add("dve", lambda e, pu=pu, sg=sg, mt=mt: e.tensor_tensor(out=big.t[:, mt, 0:N], in0=pu.t[:, 0:N], in1=sg.t[:, 0:N], op=ALU.mult),
                          reads=[pu.b, sg.b], writes=[big.bs[mt]])
            if prefetch is not None:
                for _ in prefetch:
                    pass
            for cg in range(8):
                ps_ = [mmr.get() for _ in range(nb)]
                for ch in range(4):
                    w = load_w(c_down[cg, ch], 11 * 256)
                    wv = w.t[:, 0:11 * 256].rearrange("p (k c) -> p k c", k=11)
                    for bi, (n, rows, orow) in enumerate(blocks):
                        for kc in range(11):
                            S.add("pe", lambda e, p=ps_[bi], wv=wv, kc=kc, ch=ch, n=n, c0=c0s[bi]: e.matmul(
                                p.t[:n, 0:256], lhsT=big.t[:, ch * 11 + kc, c0:c0 + n], rhs=wv[:, kc, :],
                                start=(ch == 0 and kc == 0), stop=(ch == 3 and kc == 10)),
                                reads=[w.b, big.bs[ch * 11 + kc]], writes=[ps_[bi].b])
                for bi, (n, rows, orow) in enumerate(blocks):
                    hb = hbuf[bi]
                    S.add("dve", lambda e, p=ps_[bi], hb=hb, n=n, cg=cg: e.scalar_tensor_tensor(
                        out=hb.t[:n, cg * 256:(cg + 1) * 256], in0=hb.t[:n, cg * 256:(cg + 1) * 256], scalar=ALPHA, in1=p.t[:n, 0:256],
                        op0=ALU.mult, op1=ALU.add),
                        reads=[hb.b, ps_[bi].b], writes=[hb.b])
            if has_next:
                for i in range(4):
                    preloaded.append(load_w(c_in_s[i], 4096))
            for bi, (n, rows, orow) in enumerate(blocks):
                hb = hbuf[bi]
                layer_norm(hb, n, 4)
                for (r0, nr, dst) in orow:
                    S.add("sp", lambda e, hb=hb, r0=r0, nr=nr, dst=dst: e.dma_start(out=dst, in_=hb.t[r0:r0 + nr, :]),
                          reads=[hb.b, outbuf], lane="y%d" % ((2 * k + bi) % 4))

        def init_state(sample):
            if sample:
                S.add("sp", lambda e: e.dma_start(out=Sst.t[:], in_=st_delta.rearrange("h k v -> k h v")),
                      writes=Sst.bs, lane="st")
                S.add("sp", lambda e: e.dma_start(out=hist.t[:], in_=st_conv[:, :, :]), writes=[hist.b], lane="st")
                S.add("sp", lambda e: e.dma_start(out=xprev.t[:, 0, :], in_=st_re[:, :]), writes=[xprev.b], lane="st")
                S.add("sp", lambda e: e.dma_start(out=xprev.t[:, 1, :], in_=st_im[:, :]), writes=[xprev.b], lane="st")
            else:
                S.add("pool", lambda e: e.memset(Sst.t[:], 0.0), writes=Sst.bs)
                S.add("pool", lambda e: e.memset(hist.t[:], 0.0), writes=[hist.b])
                S.add("pool", lambda e: e.memset(xprev.t[:], 0.0), writes=[xprev.b])
            for h in range(H):
                S.add("act", lambda e, h=h: e.activation(out=Sbf.t[:, h, :], in_=Sst.t[:, h, :], func=AF.Copy),
                      reads=[Sst.bs[h]], writes=[Sbf.bs[h]])

        def store_state(o_conv, o_delta, o_re, o_im):
            S.add("sp", lambda e: e.dma_start(out=o_delta.rearrange("h k v -> k h v"), in_=Sst.t[:]),
                  reads=Sst.bs + [outbuf], lane="so")
            S.add("sp", lambda e: e.dma_start(out=o_conv[:, :, :], in_=hist.t[:]), reads=[hist.b, outbuf], lane="so")
            S.add("sp", lambda e: e.dma_start(out=o_re[:, :], in_=xprev.t[:, 0, :]), reads=[xprev.b, outbuf], lane="so")
            S.add("sp", lambda e: e.dma_start(out=o_im[:, :], in_=xprev.t[:, 1, :]), reads=[xprev.b, outbuf], lane="so")

        init_state(False)
        bl = []
        for b in range(nblk):
            s0 = 128 * b
            if b == 0:
                rows = [(0, NMETA, meta[:, :]), (NMETA, 128 - NMETA, x_p[0:128 - NMETA, :])]
                orow = [(NMETA, 128 - NMETA, y_p[0:128 - NMETA, :])]
            else:
                rows = [(0, 128, x_p[s0 - NMETA:s0 - NMETA + 128, :])]
                orow = [(0, 128, y_p[s0 - NMETA:s0 - NMETA + 128, :])]
            bl.append((128, rows, orow))
        tiles = [bl[i:i + 2] for i in range(0, nblk, 2)]
        tiles.append([(16, [(0, 16, x_p[seq - 16:seq, :])], [(0, 16, y_p[seq - 16:seq, :])])])
        tiles.append([(16, [(0, 16, x_s[:, :])], [(0, 16, y_s[:, :])])])
        for _ in phase_A(0, tiles[0]):
            pass
        for k, tl in enumerate(tiles):
            if k == len(tiles) - 1:
                store_state(o_conv_p, o_delta_p, o_re_p, o_im_p)
                init_state(True)
            run_tile(k, tl, phase_A(k + 1, tiles[k + 1]) if k + 1 < len(tiles) else None)
        store_state(o_conv_s, o_delta_s, o_re_s, o_im_s)
        S.add("sp", lambda e: e.nop(), writes=[outbuf])
        with nc.allow_non_contiguous_dma(reason="small state layouts"):
            S.emit(nc, es)
    return nc


def _consts():
    i = np.arange(128)
    ident = np.eye(128, dtype=np.float32)
    ones = np.ones((128, 128), np.float32)
    negmT = np.where(i[:, None] <= i[None, :], 0.0, NEG).astype(np.float32)
    strict = (i[:, None] < i[None, :]).astype(np.float32)
    uincl = (i[:, None] <= i[None, :]).astype(np.float32)
    return np.ascontiguousarray(np.stack([ident, ones, negmT, strict, uincl], axis=1).reshape(128, 5 * 128))


def _stat(w, nb):
    K, M = w.shape
    kc = K // 128
    mt = M // 128
    a = w.reshape(kc, 128, mt, 128).transpose(2, 1, 0, 3)
    a = a.reshape(mt // nb, nb, 128, kc, 128).transpose(0, 2, 1, 3, 4)
    return np.ascontiguousarray(a.reshape(mt // nb, 128, nb * kc * 128))


def _mov(w, ncg):
    K, M = w.shape
    kc = K // 128
    a = w.reshape(kc, 128, ncg, 256).transpose(2, 1, 0, 3)
    return np.ascontiguousarray(a.reshape(ncg, 128, kc * 256))


def _prep_shared(inp):
    f = lambda a: np.asarray(a, np.float32)
    w_in = f(inp["w_in"])[0]
    sh = {}
    sh["w_in_s"] = _stat(np.concatenate([w_in[:, 0:3072], w_in[:, 4112:5136]], axis=1), 2)
    sh["w_in_z"] = _mov(w_in[:, 3072:4096], 4)
    sh["w_in_ab"] = np.ascontiguousarray(w_in[:, 4096:4112].reshape(KC, 128, 16).transpose(1, 0, 2).reshape(128, KC * 16))
    sh["cw"] = np.ascontiguousarray(f(inp["conv_w"])[0].reshape(4, 24, 128).transpose(2, 1, 0).reshape(128, 96))
    dnp = np.zeros((3, 128), np.float32)
    dnp[0, :8] = f(inp["dn_a_log"])[0]
    dnp[1, :8] = f(inp["dn_dt_bias"])[0]
    dnp[2, :] = f(inp["dn_norm_w"])[0]
    sh["dnp"] = dnp
    sh["s5a"] = np.ascontiguousarray(np.stack([f(inp["s5_a_re"])[0], f(inp["s5_a_im"])[0],
                                               np.broadcast_to(f(inp["s5_log_dt"])[0][:, None], (64, 64))]))
    bpad = np.zeros((2, 32, 128, 128), np.float32)
    cpad = np.zeros((2, 32, 128, 128), np.float32)
    for c, (bk, ck) in enumerate((("s5_b_re", "s5_c_re"), ("s5_b_im", "s5_c_im"))):
        B = f(inp[bk])[0]
        C = f(inp[ck])[0]
        for g in range(64):
            j, g2, gl = g // 2, g % 2, g % 8
            bpad[c, j, g2 * 64:(g2 + 1) * 64, gl * 16:(gl + 1) * 16] = B[g]
            cpad[c, j, g2 * 64:(g2 + 1) * 64, gl * 16:(gl + 1) * 16] = C[g].T
    sh["bpadT"] = bpad
    sh["cpad"] = cpad
    sh["s5d"] = np.ascontiguousarray(f(inp["s5_d"])[0].reshape(8, 128).T)
    sh["bglu"] = np.ascontiguousarray(f(inp["s5_b_glu"])[0].reshape(8, 128).T)
    wg = f(inp["s5_w_glu"])[0]
    sh["w_glu"] = _stat(wg, 4)
    sh["w_out"] = _mov(f(inp["w_out"])[0], 8)
    sh["w_gate"] = _stat(f(inp["ffn_w_gate"])[0], 2)
    sh["w_up"] = _stat(f(inp["ffn_w_up"])[0], 2)
    wd = f(inp["ffn_w_down"])[0]
    a = wd.reshape(4, 11, 128, 8, 256).transpose(3, 0, 2, 1, 4)
    sh["w_down"] = np.ascontiguousarray(a.reshape(8, 4, 128, 11 * 256))
    sh["cmat"] = _consts()
    sh["meta"] = f(inp["meta_tokens"])
    sh["lnp"] = np.ascontiguousarray(np.stack([f(inp["ln_in_g"]), f(inp["ln_in_b"]), f(inp["ln1_g"])[0], f(inp["ln1_b"])[0],
                                               f(inp["ln2_g"])[0], f(inp["ln2_b"])[0]]))
    return sh


def _s5_in(a):
    return np.ascontiguousarray(a.reshape(32, 2, 64).transpose(1, 2, 0).reshape(128, 32))


def _s5_out(a):
    return np.ascontiguousarray(a.reshape(2, 64, 32).transpose(2, 0, 1).reshape(64, 64))


def _conv_in(a):
    return np.ascontiguousarray(a.reshape(3, 24, 128).transpose(2, 1, 0))


def _conv_out(a):
    return np.ascontiguousarray(a.transpose(2, 1, 0).reshape(3, 3072))


_NC_CACHE = {}


def run(inputs, ncores, nblk):
    f = lambda a: np.asarray(a, np.float32)
    sh = _prep_shared(inputs)
    if nblk not in _NC_CACHE:
        _NC_CACHE[nblk] = build(nblk)
    nc = _NC_CACHE[nblk]
    in_maps = []
    for c in range(ncores):
        m = dict(sh)
        m["x_p"] = np.ascontiguousarray(f(inputs["x_prompt"])[c])
        m["x_s"] = np.ascontiguousarray(f(inputs["x_sample"])[c])
        m["st_conv"] = _conv_in(f(inputs["state_conv_qkv"])[0, c])
        m["st_delta"] = np.ascontiguousarray(f(inputs["state_delta"])[0, c])
        m["st_re"] = _s5_in(f(inputs["state_s5_re"])[0, c])
        m["st_im"] = _s5_in(f(inputs["state_s5_im"])[0, c])
        in_maps.append(m)
    res = run_bass_kernel_spmd(nc, in_maps, core_ids=list(range(ncores)))
    R = res.results
    st = lambda k, fn=lambda a: a: np.stack([fn(np.asarray(R[c][k], np.float32)) for c in range(ncores)])[None]
    return (np.stack([np.asarray(R[c]["y_p"], np.float32) for c in range(ncores)]),
            np.stack([np.asarray(R[c]["y_s"], np.float32) for c in range(ncores)]),
            st("o_conv_p", _conv_out), st("o_delta_p"), st("o_re_p", _s5_out), st("o_im_p", _s5_out),
            st("o_conv_s", _conv_out), st("o_delta_s"), st("o_re_s", _s5_out), st("o_im_s", _s5_out))


def kernel(**inputs):
    return run(inputs, 8, 32)
```

```python
import math
from contextlib import ExitStack

import numpy as np

import concourse.bass as bass
import concourse.mybir as mybir
from concourse.bass_utils import run_bass_kernel_spmd

F32 = mybir.dt.float32
BF16 = mybir.dt.bfloat16
ALU = mybir.AluOpType
AF = mybir.ActivationFunctionType
AX = mybir.AxisListType

D = 2048
KC = 16
H = 8
QKV = 3072
FF = 5632
FKC = 44
NMETA = 16
ALPHA = 2.0 ** 0.25
LN_EPS = 1e-5
RMS_EPS = 1e-6
PAD = 258
NEG = -30000.0
MAGIC = 12582912.0
TWO_PI = 2.0 * math.pi


class Buf:
    __slots__ = ("name", "w", "r", "excl")

    def __init__(self, name):
        self.name = name
        self.w = None
        self.r = []
        self.excl = False


class Op:
    __slots__ = ("eng", "fn", "lane", "seq", "waits", "signal", "val")

    def __init__(self, eng, fn, lane):
        self.eng = eng
        self.fn = fn
        self.lane = lane
        self.seq = 0
        self.waits = []
        self.signal = False
        self.val = 0


EPOCH = 20000


class Sched:
    ENGS = ("pe", "act", "dve", "pool", "sp")

    def __init__(self):
        self.ops = {e: [] for e in self.ENGS}
        self.cnt = {e: 0 for e in self.ENGS}
        self.waited = {e: {} for e in self.ENGS}
        self.lane_cnt = {}
        self.lane_last = {}

    def add(self, eng, fn, reads=(), writes=(), lane=None):
        op = Op(eng, fn, lane)
        deps = []
        for b in reads:
            if b.w is not None:
                deps.append(b.w)
            if b.excl:
                deps.extend(r_ for r_ in b.r if r_.eng != eng)
        for b in writes:
            if b.w is not None:
                deps.append(b.w)
            deps.extend(b.r)
        if lane is not None:
            if lane in self.lane_last:
                deps.append(self.lane_last[lane])
            self.lane_cnt[lane] = self.lane_cnt.get(lane, 0) + 1
            op.seq = self.lane_cnt[lane]
            self.lane_last[lane] = op
        else:
            self.cnt[eng] += 1
            op.seq = self.cnt[eng]
        need = {}
        for d in deps:
            if d is op:
                continue
            key = ("L", d.lane) if d.lane is not None else ("E", d.eng)
            if d.lane is None and d.eng == eng and eng == "pe":
                continue
            if key not in need or d.seq > need[key].seq:
                need[key] = d
        for key, d in need.items():
            if self.waited[eng].get(key, 0) >= d.seq:
                continue
            self.waited[eng][key] = d.seq
            d.signal = True
            op.waits.append(d)
        for b in reads:
            b.r.append(op)
        for b in writes:
            b.w = op
            b.r = []
        self.ops[eng].append(op)
        return op

    def emit(self, nc, es):
        nsem = {}
        for e in self.ENGS:
            c = 0
            for op in self.ops[e]:
                if op.lane is None and op.signal:
                    c += 1
                    op.val = c
            nsem[e] = max(1, (c + EPOCH - 1) // EPOCH)
        esem = {e: [es.enter_context(nc.semaphore("s_%s_%d" % (e, i))) for i in range(nsem[e])]
                for e in self.ENGS}
        lsem = {l: es.enter_context(nc.semaphore("l_%s" % l)) for l in self.lane_cnt}

        def semval(d):
            if d.lane is not None:
                return lsem[d.lane], 16 * d.seq
            i = (d.val - 1) // EPOCH
            return esem[d.eng][i], d.val - i * EPOCH

        def run(engname, eobj):
            for op in self.ops[engname]:
                for d in op.waits:
                    s, v = semval(d)
                    eobj.wait_ge(s, v)
                ins = op.fn(eobj)
                if op.lane is not None:
                    ins.then_inc(lsem[op.lane], 16)
                elif op.signal:
                    s, _ = semval(op)
                    ins.then_inc(s, 1)

        with nc.Block() as block:
            @block.tensor
            def _(e):
                run("pe", e)

            @block.scalar
            def _(e):
                run("act", e)

            @block.vector
            def _(e):
                run("dve", e)

            @block.gpsimd
            def _(e):
                run("pool", e)

            @block.sync
            def _(e):
                run("sp", e)


class T:
    def __init__(self, t, n=1, name="t"):
        self.t = t
        self.bs = [Buf("%s%d" % (name, i)) for i in range(n)]

    @property
    def b(self):
        return self.bs[0]


class Rot:
    def __init__(self, items):
        self.items = items
        self.i = 0

    def get(self):
        x = self.items[self.i % len(self.items)]
        self.i += 1
        return x


def build(nblk):
    seq = 128 * nblk
    nc = bass.Bass("TRN2", target_bir_lowering=False)
    S = Sched()

    def din(name, shape, dt=F32):
        return nc.dram_tensor(name, list(shape), dt, kind="ExternalInput").ap()

    def dout(name, shape):
        return nc.dram_tensor(name, list(shape), F32, kind="ExternalOutput").ap()

    x_p = din("x_p", [seq, D])
    x_s = din("x_s", [16, D])
    st_conv = din("st_conv", [128, 24, 3])
    st_delta = din("st_delta", [H, 128, 128])
    st_re = din("st_re", [128, 32])
    st_im = din("st_im", [128, 32])
    meta = din("meta", [NMETA, D])
    lnp = din("lnp", [6, D])
    w_in_s = din("w_in_s", [16, 128, 4096])
    w_in_z = din("w_in_z", [4, 128, 4096])
    w_in_ab = din("w_in_ab", [128, KC * 16])
    cw_d = din("cw", [128, 24 * 4])
    dnp = din("dnp", [3, 128])
    s5a = din("s5a", [3, 64, 64])
    bpadT = din("bpadT", [2, 32, 128, 128])
    cpad = din("cpad", [2, 32, 128, 128])
    s5d = din("s5d", [128, 8])
    bglu = din("bglu", [128, 8])
    w_glu = din("w_glu", [2, 128, 4096])
    w_out = din("w_out", [8, 128, 4096])
    w_gate = din("w_gate", [22, 128, 4096])
    w_up = din("w_up", [22, 128, 4096])
    w_down = din("w_down", [8, 4, 128, 11 * 256])
    cmat = din("cmat", [128, 5 * 128])

    y_p = dout("y_p", [seq, D])
    y_s = dout("y_s", [16, D])
    o_conv_p = dout("o_conv_p", [128, 24, 3])
    o_delta_p = dout("o_delta_p", [H, 128, 128])
    o_re_p = dout("o_re_p", [128, 32])
    o_im_p = dout("o_im_p", [128, 32])
    o_conv_s = dout("o_conv_s", [128, 24, 3])
    o_delta_s = dout("o_delta_s", [H, 128, 128])
    o_re_s = dout("o_re_s", [128, 32])
    o_im_s = dout("o_im_s", [128, 32])
    s5scr = nc.dram_tensor("s5scr", [4, 64, 64], F32, kind="Internal").ap()
    s5w2 = nc.dram_tensor("s5w2", [32, 128, 4096], BF16, kind="Internal").ap()
    kscr = nc.dram_tensor("kscr", [8, 128, 1024], BF16, kind="Internal").ap()
    def dscr(name, shape):
        return nc.dram_tensor(name, list(shape), BF16, kind="Internal").ap()
    c_in_s = dscr("c_in_s", [16, 128, 4096])
    c_in_z = dscr("c_in_z", [4, 128, 4096])
    c_glu = dscr("c_glu", [2, 128, 4096])
    c_out = dscr("c_out", [8, 128, 4096])
    c_gate = dscr("c_gate", [22, 128, 4096])
    c_up = dscr("c_up", [22, 128, 4096])
    c_down = dscr("c_down", [8, 4, 128, 11 * 256])
    wsbuf = Buf("wscratch")
    outbuf = Buf("dram_out")
    scrbuf = Buf("s5scr")
    swbuf = Buf("s5w")

    with ExitStack() as es:
        def sb(name, shape, dt=F32, n=1):
            return T(es.enter_context(nc.sbuf_tensor("sb_" + name, list(shape), dt)), n, name)

        def psum(name, shape, dt=F32, n=1):
            t_ = T(es.enter_context(nc.psum_tensor("ps_" + name, list(shape), dt)), n, name)
            for b_ in t_.bs:
                b_.excl = True
            return t_

        cm = sb("cm", [128, 5, 128])
        identb = sb("identb", [128, 128], BF16)
        ident = cm.t[:, 0, :]
        ones = cm.t[:, 1, :]
        negmT = cm.t[:, 2, :]
        strict = cm.t[:, 3, :]
        uincl = cm.t[:, 4, :]
        S.add("sp", lambda e: e.dma_start(out=cm.t[:].rearrange("p a b -> p (a b)"), in_=cmat[:, :]),
              writes=[cm.b], lane="c0")
        S.add("act", lambda e: e.activation(out=identb.t[:], in_=ident, func=AF.Copy),
              reads=[cm.b], writes=[identb.b])
        dn = sb("dn", [128, 3, 128])
        S.add("sp", lambda e: e.dma_start(out=dn.t[:], in_=dnp.partition_broadcast(128)),
              writes=[dn.b], lane="c0")
        negA = sb("negA", [128, 8])
        S.add("act", lambda e: e.activation(out=negA.t[:], in_=dn.t[:, 0, 0:8], func=AF.Exp),
              reads=[dn.b], writes=[negA.b])
        S.add("dve", lambda e: e.tensor_scalar_mul(out=negA.t[:], in0=negA.t[:], scalar1=-1.0),
              reads=[negA.b], writes=[negA.b])
        dtb = dn.t[:, 1, 0:8]
        normw = dn.t[:, 2, :]
        cw = sb("cw", [128, 24, 4])
        S.add("sp", lambda e: e.dma_start(out=cw.t[:].rearrange("p a b -> p (a b)"), in_=cw_d[:, :]),
              writes=[cw.b], lane="c0")
        d5 = sb("d5", [128, 8])
        hbg = sb("hbg", [128, 8])
        S.add("sp", lambda e: e.dma_start(out=d5.t[:], in_=s5d[:, :]), writes=[d5.b], lane="c0")
        S.add("sp", lambda e: e.dma_start(out=hbg.t[:], in_=bglu[:, :]), writes=[hbg.b], lane="c0")
        S.add("dve", lambda e: e.tensor_scalar_mul(out=hbg.t[:], in0=hbg.t[:], scalar1=0.5),
              reads=[hbg.b], writes=[hbg.b])
        mhalf = sb("mhalf", [128, 8])
        S.add("pool", lambda e: e.memset(mhalf.t[:], -0.5), writes=[mhalf.b])

        mm = [psum("mm%d" % i, [128, 512]) for i in range(4)]
        tp = [psum("tp%d" % i, [128, 8, 128], BF16) for i in range(2)]
        gp_t = [psum("gp%d" % i, [128, 4, 128], F32, 4) for i in range(2)]
        mmr = Rot(mm)
        tpr = Rot(tp)
        gpr = Rot([(gp_t[i].t[:, j, :], gp_t[i].bs[0]) for j in range(4) for i in range(2)])

        a3 = sb("a3", [64, 3, 64])
        S.add("sp", lambda e: e.dma_start(out=a3.t[:], in_=s5a.rearrange("a g p -> g a p")),
              writes=[a3.b], lane="c0")
        s5t = sb("s5t", [64, 12, 64])
        tb = s5t.b

        def s5op(eng, fn):
            S.add(eng, fn, reads=[a3.b, tb], writes=[tb])
        st = s5t.t
        s5op("act", lambda e: e.activation(out=st[:, 0, :], in_=a3.t[:, 2, :], func=AF.Exp))
        s5op("dve", lambda e: e.tensor_tensor(out=st[:, 1, :], in0=a3.t[:, 0, :], in1=st[:, 0, :], op=ALU.mult))
        s5op("dve", lambda e: e.tensor_tensor(out=st[:, 2, :], in0=a3.t[:, 1, :], in1=st[:, 0, :], op=ALU.mult))
        s5op("act", lambda e: e.activation(out=st[:, 3, :], in_=st[:, 1, :], func=AF.Exp))
        for (dst, shift) in ((5, 0.0), (6, math.pi / 2)):
            s5op("dve", lambda e, shift=shift: e.tensor_scalar(out=st[:, 11, :], in0=st[:, 2, :], scalar1=shift,
                                                              scalar2=None, op0=ALU.add))
            s5op("dve", lambda e: e.tensor_scalar(out=st[:, 4, :], in0=st[:, 11, :], scalar1=1.0 / TWO_PI,
                                                  scalar2=MAGIC, op0=ALU.mult, op1=ALU.add))
            s5op("dve", lambda e: e.tensor_scalar(out=st[:, 4, :], in0=st[:, 4, :], scalar1=-MAGIC,
                                                  scalar2=-TWO_PI, op0=ALU.add, op1=ALU.mult))
            s5op("dve", lambda e: e.tensor_tensor(out=st[:, 4, :], in0=st[:, 4, :], in1=st[:, 11, :], op=ALU.add))
            s5op("act", lambda e, dst=dst: e.activation(out=st[:, dst, :], in_=st[:, 4, :], func=AF.Sin))
        s5op("dve", lambda e: e.tensor_tensor(out=st[:, 7, :], in0=st[:, 3, :], in1=st[:, 6, :], op=ALU.mult))
        s5op("dve", lambda e: e.tensor_tensor(out=st[:, 8, :], in0=st[:, 3, :], in1=st[:, 5, :], op=ALU.mult))
        s5op("dve", lambda e: e.tensor_tensor(out=st[:, 11, :], in0=a3.t[:, 0, :], in1=a3.t[:, 0, :], op=ALU.mult))
        s5op("dve", lambda e: e.tensor_tensor(out=st[:, 0, :], in0=a3.t[:, 1, :], in1=a3.t[:, 1, :], op=ALU.mult))
        s5op("dve", lambda e: e.tensor_tensor(out=st[:, 11, :], in0=st[:, 11, :], in1=st[:, 0, :], op=ALU.add))
        s5op("dve", lambda e: e.reciprocal(out=st[:, 11, :], in_=st[:, 11, :]))
        s5op("dve", lambda e: e.tensor_scalar(out=st[:, 4, :], in0=st[:, 7, :], scalar1=-1.0, scalar2=None, op0=ALU.add))
        s5op("dve", lambda e: e.tensor_tensor(out=st[:, 9, :], in0=st[:, 4, :], in1=a3.t[:, 0, :], op=ALU.mult))
        s5op("dve", lambda e: e.tensor_tensor(out=st[:, 0, :], in0=st[:, 8, :], in1=a3.t[:, 1, :], op=ALU.mult))
        s5op("dve", lambda e: e.tensor_tensor(out=st[:, 9, :], in0=st[:, 9, :], in1=st[:, 0, :], op=ALU.add))
        s5op("dve", lambda e: e.tensor_tensor(out=st[:, 9, :], in0=st[:, 9, :], in1=st[:, 11, :], op=ALU.mult))
        s5op("dve", lambda e: e.tensor_tensor(out=st[:, 10, :], in0=st[:, 8, :], in1=a3.t[:, 0, :], op=ALU.mult))
        s5op("dve", lambda e: e.tensor_tensor(out=st[:, 0, :], in0=st[:, 4, :], in1=a3.t[:, 1, :], op=ALU.mult))
        s5op("dve", lambda e: e.tensor_tensor(out=st[:, 10, :], in0=st[:, 10, :], in1=st[:, 0, :], op=ALU.subtract))
        s5op("dve", lambda e: e.tensor_tensor(out=st[:, 10, :], in0=st[:, 10, :], in1=st[:, 11, :], op=ALU.mult))
        S.add("sp", lambda e: e.dma_start(out=s5scr.rearrange("a g p -> g a p"), in_=st[:, 7:11, :]),
              reads=[tb], writes=[scrbuf], lane="c0")
        Sst = sb("Sst", [128, H, 128], F32, H)
        Sbf = sb("Sbf", [128, H, 128], BF16, H)
        hist = sb("hist", [128, 24, 3])
        xprev = sb("xprev", [128, 2, 32])

        NT = 256
        hb4 = [sb("h%d" % i, [128, D]) for i in range(4)]
        hb16 = sb("hb16", [128, D], BF16)
        lnbc = sb("lnbc", [128, 2, D])
        actTs = [sb("actT%d" % i, [128, KC, NT], BF16) for i in range(2)]
        mixT = sb("mixT", [128, KC, NT], BF16)
        big = sb("big", [128, FKC, NT], BF16, FKC)
        Wsl = [sb("W%d" % i, [128, 4096], BF16) for i in range(4)]
        wab = sb("wab", [128, KC, 16], BF16)
        wr = Rot(Wsl)
        wlane = Rot(["w0", "w1", "w2", "w3"])
        stats_l = [sb("stats%d" % i, [128, 4, 6]) for i in range(2)]
        mv_l = [sb("mv%d" % i, [128, 4]) for i in range(2)]
        ln_i = [0]
        cbufs = Rot([sb("cb%d" % i, [128, NT + 3]) for i in range(2)])
        caccs = Rot([sb("ca%d" % i, [128, NT]) for i in range(2)])
        absb = sb("absb", [128, 2, 16], F32, 2)
        gsm = sb("gsm", [128, 12, 8])
        gtmp = Rot([sb("gt%d" % i, [128, 128]) for i in range(14)])
        gtb = Rot([sb("gb%d" % i, [128, 128], BF16) for i in range(4)])
        ktm = sb("ktm", [128, H, 128], BF16, H)
        kdtm = sb("kdtm", [128, H, 128], BF16, H)
        vtm = sb("vtm", [128, H, 128], BF16, H)
        kTn = sb("kTn", [128, H, 128], BF16, H)
        qTn = sb("qTn", [128, H, 128], BF16, H)
        otm = sb("otm", [128, H, 128], F32, H)
        ontm = sb("ontm", [128, H, 128], BF16, H)
        nrm = sb("nrm", [128, 3, 8], F32, 3)
        gz = [sb("gz%d" % i, [128, 1024], BF16) for i in range(2)]
        xs = [sb("xs%d" % i, [128, 32, 33]) for i in range(2)]
        xq = [sb("xq%d" % i, [128, 4, 32], F32, 4) for i in range(2)]
        xb = sb("xb", [128, 2, 32, 32], BF16)
        s5ws = [sb("s5ws%d" % i, [128, 16, 128], BF16) for i in range(2)]
        s5ks = [sb("s5ks%d" % i, [128, 8, 128], BF16) for i in range(2)]
        sgt = Rot([sb("sg%d" % i, [128, NT]) for i in range(3)])

        pwv = lnbc.t[:].rearrange("p a d -> p (a d)")[:, 0:4 * 17 * 32].rearrange("p (c k j) -> p c k j", c=4, k=17)
        pwb = lnbc.b
        ct = sb("ct", [128, 2, 32])
        l8 = sb("l8", [128, 1, 2, 32])
        with nc.allow_non_contiguous_dma(reason="tiny s5 param reshuffle"):
            for c in range(2):
                S.add("sp", lambda e, c=c: e.dma_start(
                    out=pwv[:, c, 1, :], in_=s5scr[c].rearrange("(j g) p -> (g p) j", g=2)),
                    reads=[scrbuf], writes=[pwb], lane="c0")
                S.add("sp", lambda e, c=c: e.dma_start(
                    out=pwv[:, c, 9, :], in_=s5scr[2 + c].rearrange("(j g) p -> (g p) j", g=2)),
                    reads=[scrbuf], writes=[pwb], lane="c0")
        S.add("dve", lambda e: e.memset(pwv[:, 0, 0, :], 1.0), writes=[pwb])
        S.add("dve", lambda e: e.memset(pwv[:, 1, 0, :], 0.0), writes=[pwb])

        def cmul(dr, di, ar, ai, br, bi, db, sbs):
            rd = sbs + [ct.b]
            S.add("dve", lambda e: e.tensor_tensor(out=ct.t[:, 0, :], in0=ar, in1=br, op=ALU.mult), reads=rd, writes=[ct.b])
            S.add("dve", lambda e: e.tensor_tensor(out=ct.t[:, 1, :], in0=ai, in1=bi, op=ALU.mult), reads=rd, writes=[ct.b])
            S.add("dve", lambda e: e.tensor_tensor(out=dr, in0=ct.t[:, 0, :], in1=ct.t[:, 1, :], op=ALU.subtract), reads=rd + [db], writes=[db])
            S.add("dve", lambda e: e.tensor_tensor(out=ct.t[:, 0, :], in0=ar, in1=bi, op=ALU.mult), reads=rd + [db], writes=[ct.b])
            S.add("dve", lambda e: e.tensor_tensor(out=ct.t[:, 1, :], in0=ai, in1=br, op=ALU.mult), reads=rd + [db], writes=[ct.b])
            S.add("dve", lambda e: e.tensor_tensor(out=di, in0=ct.t[:, 0, :], in1=ct.t[:, 1, :], op=ALU.add), reads=rd + [db], writes=[db])
        for k in range(1, 8):
            cmul(pwv[:, 0, k + 1, :], pwv[:, 1, k + 1, :], pwv[:, 0, k, :], pwv[:, 1, k, :], pwv[:, 0, 1, :], pwv[:, 1, 1, :], pwb, [pwb])
            cmul(pwv[:, 0, 9 + k, :], pwv[:, 1, 9 + k, :], pwv[:, 0, k, :], pwv[:, 1, k, :], pwv[:, 0, 9, :], pwv[:, 1, 9, :], pwb, [pwb])
        S.add("dve", lambda e: e.tensor_scalar_mul(out=pwv[:, 2:4, :, :], in0=pwv[:, 0:2, :, :], scalar1=-1.0), reads=[pwb], writes=[pwb])
        S.add("dve", lambda e: e.tensor_copy(out=l8.t[:, 0, :, :], in_=pwv[:, 0:2, 8, :]), reads=[pwb], writes=[l8.b])
        hv0 = hb4[0].t[:, :]
        hv1 = hb4[1].t[:, :]
        ldt = [T(hv1[:, i * 512:(i + 1) * 512].rearrange("p (a c) -> p a c", a=4), 1, "ld%d" % i) for i in range(2)]
        ncit = [T(hv1[:, 1024 + i * 128:1024 + (i + 1) * 128], 1, "nci%d" % i) for i in range(2)]
        wtt = Rot([T(hv1[:, 1280 + i * 128:1280 + (i + 1) * 128], 1, "wt%d" % i) for i in range(6)] + [T(hv0[:, 1024 + i * 128:1024 + (i + 1) * 128], 1, "wu%d" % i) for i in range(8)])
        kacc = T(hv0[:, 0:1024].rearrange("p (a c) -> p a c", a=8), 1, "kacc")
        kbf = T(hb16.t[:, 0:1024].rearrange("p (a c) -> p a c", a=8), 1, "kbf")
        spr = Rot([(gp_t[0].t[:, 0, :], gp_t[0].bs[0]), (mm[2].t[:, 0:128], mm[2].b), (gp_t[1].t[:, 0, :], gp_t[1].bs[0]), (mm[3].t[:, 0:128], mm[3].b)])
        for j in range(32):
            ld = ldt[j % 2]
            nci = ncit[j % 2]
            wst = s5ws[0]
            vst = s5ws[1]
            S.add("sp", lambda e, ld=ld, j=j: e.dma_start(out=ld.t[:, 0:2, :], in_=bpadT[:, j].rearrange("a p c -> p a c")),
                  writes=[ld.b], lane="b%d" % (j % 2))
            S.add("sp", lambda e, ld=ld, j=j: e.dma_start(out=ld.t[:, 2:4, :], in_=cpad[:, j].rearrange("a p c -> p a c")),
                  writes=[ld.b], lane="b%d" % (j % 2))
            S.add("act", lambda e, ld=ld, nci=nci: e.activation(out=nci.t[:, :], in_=ld.t[:, 3, :], func=AF.Copy, scale=-1.0),
                  reads=[ld.b], writes=[nci.b])
            Br, Bi, Cr, Ci = ld.t[:, 0, :], ld.t[:, 1, :], ld.t[:, 2, :], ld.t[:, 3, :]
            for k in range(8):
                gr, gi, ngi = pwv[:, 0, 9 + k, j:j + 1], pwv[:, 1, 9 + k, j:j + 1], pwv[:, 3, 9 + k, j:j + 1]
                wtr = wtt.get()
                wti = wtt.get()
                S.add("act", lambda e, wtr=wtr, Br=Br, gr=gr: e.activation(out=wtr.t[:, :], in_=Br, func=AF.Copy, scale=gr),
                      reads=[ld.b, pwb], writes=[wtr.b])
                S.add("dve", lambda e, wtr=wtr, Bi=Bi, ngi=ngi: e.scalar_tensor_tensor(out=wtr.t[:, :], in0=Bi, scalar=ngi, in1=wtr.t[:, :], op0=ALU.mult, op1=ALU.add),
                      reads=[ld.b, pwb, wtr.b], writes=[wtr.b])
                S.add("act", lambda e, wti=wti, Br=Br, gi=gi: e.activation(out=wti.t[:, :], in_=Br, func=AF.Copy, scale=gi),
                      reads=[ld.b, pwb], writes=[wti.b])
                S.add("dve", lambda e, wti=wti, Bi=Bi, gr=gr: e.scalar_tensor_tensor(out=wti.t[:, :], in0=Bi, scalar=gr, in1=wti.t[:, :], op0=ALU.mult, op1=ALU.add),
                      reads=[ld.b, pwb, wti.b], writes=[wti.b])
                kp = mm[k // 4]
                S.add("pe", lambda e, kp=kp, k=k, wtr=wtr, Cr=Cr: e.matmul(kp.t[:, (k % 4) * 128:(k % 4 + 1) * 128], lhsT=wtr.t[:, :], rhs=Cr, start=True, stop=False),
                      reads=[wtr.b, ld.b], writes=[kp.b])
                S.add("pe", lambda e, kp=kp, k=k, wti=wti, nci=nci: e.matmul(kp.t[:, (k % 4) * 128:(k % 4 + 1) * 128], lhsT=wti.t[:, :], rhs=nci.t[:, :], start=False, stop=True),
                      reads=[wti.b, nci.b], writes=[kp.b])
                sp_ = 7 - k
                for c, wsrc in ((0, wtr), (1, wti)):
                    pt_, ptb = spr.get()
                    S.add("pe", lambda e, pt_=pt_, wsrc=wsrc: e.transpose(out=pt_[:, :], in_=wsrc.t[:, :], identity=ident),
                          reads=[wsrc.b, cm.b], writes=[ptb])
                    S.add("act", lambda e, pt_=pt_, sp_=sp_, c=c, wst=wst: e.activation(out=wst.t[:, sp_ * 2 + c, :], in_=pt_[:, :], func=AF.Copy),
                          reads=[ptb], writes=[wst.b])
                ar, ai, nar, nai = pwv[:, 0, k + 1, j:j + 1], pwv[:, 1, k + 1, j:j + 1], pwv[:, 2, k + 1, j:j + 1], pwv[:, 3, k + 1, j:j + 1]
                v1 = wtt.get()
                S.add("act", lambda e, v1=v1, Cr=Cr, ar=ar: e.activation(out=v1.t[:, :], in_=Cr, func=AF.Copy, scale=ar),
                      reads=[ld.b, pwb], writes=[v1.b])
                S.add("dve", lambda e, v1=v1, Ci=Ci, nai=nai, k=k, vst=vst: e.scalar_tensor_tensor(out=vst.t[:, k * 2, :], in0=Ci, scalar=nai, in1=v1.t[:, :], op0=ALU.mult, op1=ALU.add),
                      reads=[ld.b, pwb, v1.b], writes=[vst.b])
                v2 = wtt.get()
                S.add("act", lambda e, v2=v2, Cr=Cr, nai=nai: e.activation(out=v2.t[:, :], in_=Cr, func=AF.Copy, scale=nai),
                      reads=[ld.b, pwb], writes=[v2.b])
                S.add("dve", lambda e, v2=v2, Ci=Ci, nar=nar, k=k, vst=vst: e.scalar_tensor_tensor(out=vst.t[:, k * 2 + 1, :], in0=Ci, scalar=nar, in1=v2.t[:, :], op0=ALU.mult, op1=ALU.add),
                      reads=[ld.b, pwb, v2.b], writes=[vst.b])
            for half in range(2):
                kp = mm[half]
                if j % 4 == 0:
                    S.add("act", lambda e, kp=kp, half=half: e.activation(out=kacc.t[:, half * 4:half * 4 + 4, :], in_=kp.t[:, :].rearrange("p (a c) -> p a c", a=4), func=AF.Copy),
                          reads=[kp.b], writes=[kacc.b])
                else:
                    S.add("dve", lambda e, kp=kp, half=half: e.tensor_tensor(out=kacc.t[:, half * 4:half * 4 + 4, :], in0=kp.t[:, :].rearrange("p (a c) -> p a c", a=4),
                                                                          in1=kacc.t[:, half * 4:half * 4 + 4, :], op=ALU.add),
                          reads=[kp.b, kacc.b], writes=[kacc.b])
            if j % 4 == 3:
                S.add("act", lambda e: e.activation(out=kbf.t[:, :, :], in_=kacc.t[:, :, :], func=AF.Copy), reads=[kacc.b], writes=[kbf.b])
                S.add("sp", lambda e, j=j: e.dma_start(out=kscr[j // 4], in_=kbf.t[:, :, :].rearrange("p a c -> p (a c)")),
                      reads=[kbf.b], writes=[swbuf], lane="b%d" % (j % 2))
            S.add("sp", lambda e, j=j, wst=wst: e.dma_start(out=s5w2[j, :, 0:2048], in_=wst.t[:].rearrange("p a c -> p (a c)")),
                  reads=[wst.b], writes=[swbuf], lane="b%d" % (j % 2))
            S.add("sp", lambda e, j=j, vst=vst: e.dma_start(out=s5w2[j, :, 2048:4096], in_=vst.t[:].rearrange("p a c -> p (a c)")),
                  reads=[vst.b], writes=[swbuf], lane="b%d" % (j % 2))
        S.add("pool", lambda e: e.memset(ct.t[:], 0.0),
              reads=[b_.b for b_ in ldt + ncit + wtt.items + [kacc, kbf]] + [pwb],
              writes=[hb4[0].b, hb4[1].b, hb16.b, lnbc.b, ct.b])
        def load_w(src_ap, nel):
            w = wr.get()
            S.add("sp", lambda e, w=w, src_ap=src_ap, nel=nel: e.dma_start(out=w.t[:, 0:nel], in_=src_ap),
                  reads=[wsbuf], writes=[w.b], lane=wlane.get())
            return w

        def convert_w(src_ap, dst_ap, nel):
            w = wr.get()
            S.add("pool", lambda e, w=w: e.dma_start(out=w.t[:, 0:nel], in_=src_ap), writes=[w.b], lane=cwl.get())
            S.add("sp", lambda e, w=w: e.dma_start(out=dst_ap, in_=w.t[:, 0:nel]), reads=[w.b, wsbuf], lane=cvl.get())

        cvl = Rot(["cv0", "cv1", "cv2", "cv3"])
        cwl = Rot(["cw0", "cw1", "cw2", "cw3"])
        S.add("pool", lambda e: e.dma_start(out=wab.t[:].rearrange("p a b -> p (a b)"), in_=w_in_ab[:, :]),
              writes=[wab.b], lane="wab")
        for i in range(16):
            convert_w(w_in_s[i], c_in_s[i], 4096)
        for i in range(4):
            convert_w(w_in_z[i], c_in_z[i], 4096)
        for i in range(2):
            convert_w(w_glu[i], c_glu[i], 4096)
        for i in range(8):
            convert_w(w_out[i], c_out[i], 4096)
        for i in range(22):
            convert_w(w_gate[i], c_gate[i], 4096)
            convert_w(w_up[i], c_up[i], 4096)
        for i in range(8):
            for k in range(4):
                convert_w(w_down[i, k], c_down[i, k], 11 * 256)
        S.add("sp", lambda e: e.nop(), writes=[wsbuf])

        ln_cur = [-1]

        def layer_norm(hb, n, gi):
            stats = stats_l[ln_i[0] % 2]
            mv = mv_l[ln_i[0] % 2]
            ln_i[0] += 1
            if ln_cur[0] != gi:
                ln_cur[0] = gi
                S.add("sp", lambda e: e.dma_start(out=lnbc.t[:].rearrange("p a d -> p (a d)"),
                                                  in_=lnp[gi:gi + 2, :].rearrange("a d -> (a d)").partition_broadcast(128)),
                      writes=[lnbc.b], lane="ln")
            for c in range(4):
                S.add("dve", lambda e, c=c: e.bn_stats(out=stats.t[:n, c, :], in_=hb.t[:n, c * 512:(c + 1) * 512]),
                      reads=[hb.b], writes=[stats.b])
            S.add("dve", lambda e: e.bn_aggr(out=mv.t[:n, 0:2], in_=stats.t[:n].rearrange("p a b -> p (a b)")),
                  reads=[stats.b], writes=[mv.b])
            S.add("dve", lambda e: e.tensor_scalar(out=mv.t[:n, 2:3], in0=mv.t[:n, 1:2], scalar1=LN_EPS, scalar2=None, op0=ALU.add),
                  reads=[mv.b], writes=[mv.b])
            S.add("pool", lambda e: e.tensor_tensor(out=mv.t[:n, 2:3], in0=mv.t[:n, 2:3], in1=mhalf.t[:n, 0:1], op=ALU.pow),
                  reads=[mv.b, mhalf.b], writes=[mv.b])
            S.add("dve", lambda e: e.scalar_tensor_tensor(out=mv.t[:n, 3:4], in0=mv.t[:n, 0:1], scalar=-1.0, in1=mv.t[:n, 2:3],
                                                          op0=ALU.mult, op1=ALU.mult),
                  reads=[mv.b], writes=[mv.b])
            S.add("act", lambda e: e.activation(out=hb.t[:n, :], in_=hb.t[:n, :], func=AF.Identity,
                                                scale=mv.t[:n, 2:3], bias=mv.t[:n, 3:4]),
                  reads=[hb.b, mv.b], writes=[hb.b])
            S.add("dve", lambda e: e.tensor_tensor(out=hb.t[:n, :], in0=hb.t[:n, :], in1=lnbc.t[:n, 0, :], op=ALU.mult),
                  reads=[hb.b, lnbc.b], writes=[hb.b])
            S.add("dve", lambda e: e.tensor_tensor(out=hb.t[:n, :], in0=hb.t[:n, :], in1=lnbc.t[:n, 1, :], op=ALU.add),
                  reads=[hb.b, lnbc.b], writes=[hb.b])

        def to_actT(hb, n, c0, actT):
            S.add("act", lambda e: e.activation(out=hb16.t[:n, :], in_=hb.t[:n, :], func=AF.Copy),
                  reads=[hb.b], writes=[hb16.b])
            for half in range(2):
                p = tpr.get()
                for k in range(8):
                    kc = half * 8 + k
                    S.add("pe", lambda e, p=p, k=k, kc=kc: e.transpose(out=p.t[:, k, 0:n], in_=hb16.t[:n, kc * 128:(kc + 1) * 128],
                                                                       identity=identb.t[:n, :n]),
                          reads=[hb16.b, identb.b], writes=[p.b])
                S.add("act", lambda e, p=p, half=half: e.activation(out=actT.t[:, half * 8:half * 8 + 8, c0:c0 + n],
                                                                    in_=p.t[:, :, 0:n], func=AF.Copy),
                      reads=[p.b], writes=[actT.b])

        def pe_T(dst_ap, dst_b, src_ap, src_b, np_, nf, evac="act", scale=None, scale_b=None):
            p = tpr.get()
            S.add("pe", lambda e: e.transpose(out=p.t[:nf, 0, 0:np_], in_=src_ap, identity=identb.t[:np_, :np_]),
                  reads=[src_b, identb.b], writes=[p.b])
            S.add("act", lambda e: e.activation(out=dst_ap, in_=p.t[:nf, 0, 0:np_], func=AF.Copy),
                  reads=[p.b], writes=[dst_b])

        def rsq(dst, src, n, b, mul, add):
            S.add("dve", lambda e: e.tensor_scalar(out=dst, in0=src, scalar1=mul, scalar2=add, op0=ALU.mult, op1=ALU.add),
                  reads=[b], writes=[b])
            S.add("pool", lambda e: e.tensor_tensor(out=dst, in0=dst, in1=mhalf.t[:n, :], op=ALU.pow),
                  reads=[b, mhalf.b], writes=[b])

        def phase_A(k, blocks):
            hbuf = [hb4[(2 * k) % 4], hb4[(2 * k + 1) % 4]]
            actT = actTs[k % 2]
            c = 0
            for bi, (n, rows, orow) in enumerate(blocks):
                hb = hbuf[bi]
                for (r0, nr, src) in rows:
                    S.add("sp", lambda e, hb=hb, r0=r0, nr=nr, src=src: e.dma_start(out=hb.t[r0:r0 + nr, :], in_=src),
                          writes=[hb.b], lane="x%d" % ((2 * k + bi) % 4))
                yield
                layer_norm(hb, n, 0)
                for _ in range(6):
                    yield
                to_actT(hb, n, c, actT)
                yield
                c += n

        preloaded = []

        def run_tile(k, blocks, prefetch):
            hbuf = [hb4[(2 * k) % 4], hb4[(2 * k + 1) % 4]]
            actT = actTs[k % 2]
            has_next = prefetch is not None
            N = sum(b_[0] for b_ in blocks)
            c0s = []
            c = 0
            for b_ in blocks:
                c0s.append(c)
                c += b_[0]
            nb = len(blocks)
            abps = []
            for bi, (n, rows, orow) in enumerate(blocks):
                pa, pb = gpr.get()
                for kc in range(KC):
                    S.add("pe", lambda e, pa=pa, kc=kc, n=n, c0=c0s[bi]: e.matmul(
                        pa[:n, 0:16], lhsT=actT.t[:, kc, c0:c0 + n], rhs=wab.t[:, kc, :], start=(kc == 0), stop=(kc == KC - 1)),
                        reads=[actT.b, wab.b], writes=[pb])
                S.add("act", lambda e, pa=pa, n=n, bi=bi: e.activation(out=absb.t[:n, bi, :], in_=pa[:n, 0:16], func=AF.Copy),
                      reads=[pb], writes=[absb.bs[bi]])
                abps.append((absb.t[:, bi, :], absb.bs[bi]))
            for bt_ in range(16):
                w = preloaded.pop(0) if preloaded else load_w(c_in_s[bt_], 4096)
                wv = w.t[:].rearrange("p (m k c) -> p m k c", m=2, k=KC)
                for mi in range(2):
                    mt = bt_ * 2 + mi
                    p = mmr.get()
                    for kc in range(KC):
                        S.add("pe", lambda e, p=p, wv=wv, mi=mi, kc=kc: e.matmul(
                            p.t[:, 0:N], lhsT=wv[:, mi, kc, :], rhs=actT.t[:, kc, 0:N], start=(kc == 0), stop=(kc == KC - 1)),
                            reads=[w.b, actT.b], writes=[p.b])
                    if mt < 24:
                        cb = cbufs.get()
                        ca = caccs.get()
                        hsrc = hist
                        S.add("act", lambda e, cb=cb, p=p: e.activation(out=cb.t[:, 3:3 + N], in_=p.t[:, 0:N], func=AF.Copy),
                              reads=[p.b], writes=[cb.b])
                        S.add("act", lambda e, cb=cb, mt=mt: e.activation(out=cb.t[:, 0:3], in_=hist.t[:, mt, :], func=AF.Copy),
                              reads=[hist.b, cb.b], writes=[cb.b])
                        S.add("act", lambda e, cb=cb, mt=mt: e.activation(out=hist.t[:, mt, :], in_=cb.t[:, N:N + 3], func=AF.Copy),
                              reads=[cb.b], writes=[hist.b])
                        S.add("dve", lambda e, cb=cb, ca=ca, mt=mt: e.tensor_scalar(
                            out=ca.t[:, 0:N], in0=cb.t[:, 0:N], scalar1=cw.t[:, mt, 0:1], scalar2=None, op0=ALU.mult),
                            reads=[cb.b, cw.b], writes=[ca.b])
                        for i in range(1, 4):
                            S.add("dve", lambda e, cb=cb, ca=ca, mt=mt, i=i: e.scalar_tensor_tensor(
                                out=ca.t[:, 0:N], in0=cb.t[:, i:i + N], scalar=cw.t[:, mt, i:i + 1], in1=ca.t[:, 0:N],
                                op0=ALU.mult, op1=ALU.add),
                                reads=[cb.b, cw.b, ca.b], writes=[ca.b])
                        S.add("act", lambda e, ca=ca, mt=mt: e.activation(out=big.t[:, mt, 0:N], in_=ca.t[:, 0:N], func=AF.Silu),
                              reads=[ca.b], writes=[big.bs[mt]])
                    else:
                        S.add("act", lambda e, p=p, mt=mt: e.activation(out=big.t[:, mt, 0:N], in_=p.t[:, 0:N], func=AF.Copy),
                              reads=[p.b], writes=[big.bs[mt]])
            for cg in range(4):
                w = load_w(c_in_z[cg], 4096)
                wv = w.t[:].rearrange("p (k c) -> p k c", k=KC)
                for bi, (n, rows, orow) in enumerate(blocks):
                    p = mmr.get()
                    for kc in range(KC):
                        S.add("pe", lambda e, p=p, wv=wv, kc=kc, n=n, c0=c0s[bi]: e.matmul(
                            p.t[:n, 0:256], lhsT=actT.t[:, kc, c0:c0 + n], rhs=wv[:, kc, :], start=(kc == 0), stop=(kc == KC - 1)),
                            reads=[w.b, actT.b], writes=[p.b])
                    g_ = gz[bi]
                    S.add("act", lambda e, p=p, g_=g_, n=n, cg=cg: e.activation(out=g_.t[:n, cg * 256:(cg + 1) * 256], in_=p.t[:n, 0:256], func=AF.Silu),
                          reads=[p.b], writes=[g_.b])
                    for hh in range(2):
                        S.add("pool", lambda e, g_=g_, n=n, cg=cg, hh=hh: e.tensor_tensor(
                            out=g_.t[:n, cg * 256 + hh * 128:cg * 256 + hh * 128 + 128],
                            in0=g_.t[:n, cg * 256 + hh * 128:cg * 256 + hh * 128 + 128], in1=normw[:n, :], op=ALU.mult),
                            reads=[g_.b, dn.b], writes=[g_.b])

            def gdn_block(bi, n, c0, pa, pb):
                gs = gsm.t
                gb_ = gsm.b
                S.add("dve", lambda e: e.tensor_tensor(out=gs[:n, 0, :], in0=pa[:n, 0:8], in1=dtb[:n, :], op=ALU.add),
                      reads=[pb, dn.b], writes=[gb_])
                S.add("act", lambda e: e.activation(out=gs[:n, 5, :], in_=pa[:n, 8:16], func=AF.Tanh, scale=0.5),
                      reads=[pb], writes=[gb_])
                S.add("act", lambda e: e.activation(out=gs[:n, 0, :], in_=gs[:n, 0, :], func=AF.Exp),
                      reads=[gb_], writes=[gb_])
                S.add("act", lambda e: e.activation(out=gs[:n, 0, :], in_=gs[:n, 0, :], func=AF.Ln, bias=1.0),
                      reads=[gb_], writes=[gb_])
                S.add("dve", lambda e: e.tensor_tensor(out=gs[:n, 0, :], in0=gs[:n, 0, :], in1=negA.t[:n, :], op=ALU.mult),
                      reads=[gb_, negA.b], writes=[gb_])
                S.add("dve", lambda e: e.tensor_scalar(out=gs[:n, 5, :], in0=gs[:n, 5, :], scalar1=0.5, scalar2=0.5, op0=ALU.mult, op1=ALU.add),
                      reads=[gb_], writes=[gb_])
                S.add("dve", lambda e: e.tensor_scalar_mul(out=gs[:n, 6, :], in0=gs[:n, 5, :], scalar1=-1.0),
                      reads=[gb_], writes=[gb_])
                pc, pcb = gpr.get()
                S.add("pe", lambda e, pc=pc: e.matmul(pc[:n, 0:8], lhsT=uincl[:n, :n], rhs=gs[:n, 0, :], start=True, stop=True),
                      reads=[cm.b, gb_], writes=[pcb])
                S.add("pe", lambda e, pc=pc: e.matmul(pc[:, 8:16], lhsT=ones[:n, :], rhs=gs[:n, 0, :], start=True, stop=True),
                      reads=[cm.b, gb_], writes=[pcb])
                S.add("act", lambda e, pc=pc: e.activation(out=gs[:n, 1, :], in_=pc[:n, 0:8], func=AF.Copy),
                      reads=[pcb], writes=[gb_])
                S.add("act", lambda e, pc=pc: e.activation(out=gs[:n, 2, :], in_=pc[:n, 0:8], func=AF.Exp),
                      reads=[pcb], writes=[gb_])
                S.add("act", lambda e, pc=pc: e.activation(out=gs[:, 7, :], in_=pc[:, 8:16], func=AF.Exp),
                      reads=[pcb], writes=[gb_])
                S.add("dve", lambda e, pc=pc: e.tensor_tensor(out=gs[:n, 4, :], in0=pc[:n, 8:16], in1=gs[:n, 1, :], op=ALU.subtract),
                      reads=[pcb, gb_], writes=[gb_])
                S.add("act", lambda e: e.activation(out=gs[:n, 4, :], in_=gs[:n, 4, :], func=AF.Exp),
                      reads=[gb_], writes=[gb_])
                S.add("dve", lambda e: e.tensor_scalar_mul(out=gs[:n, 3, :], in0=gs[:n, 2, :], scalar1=-1.0),
                      reads=[gb_], writes=[gb_])
                for h in range(H):
                    p = tpr.get()
                    S.add("pe", lambda e, p=p, h=h: e.transpose(out=p.t[:n, 0, :], in_=big.t[:, 8 + h, c0:c0 + n], identity=identb.t[:, :]),
                          reads=[big.bs[8 + h], identb.b], writes=[p.b])
                    S.add("pe", lambda e, p=p, h=h: e.transpose(out=p.t[:n, 1, :], in_=big.t[:, h, c0:c0 + n], identity=identb.t[:, :]),
                          reads=[big.bs[h], identb.b], writes=[p.b])
                    S.add("pe", lambda e, p=p, h=h: e.transpose(out=p.t[:n, 2, :], in_=big.t[:, 16 + h, c0:c0 + n], identity=identb.t[:, :]),
                          reads=[big.bs[16 + h], identb.b], writes=[p.b])
                    jk = gtmp.get()
                    S.add("act", lambda e, p=p, h=h, jk=jk: e.activation(out=jk.t[:n, :], in_=p.t[:n, 0, :], func=AF.Square,
                                                                        accum_out=nrm.t[:n, 0, h:h + 1]),
                          reads=[p.b], writes=[jk.b, nrm.bs[0]])
                    jq = gtmp.get()
                    S.add("act", lambda e, p=p, h=h, jq=jq: e.activation(out=jq.t[:n, :], in_=p.t[:n, 1, :], func=AF.Square,
                                                                        accum_out=nrm.t[:n, 1, h:h + 1]),
                          reads=[p.b], writes=[jq.b, nrm.bs[1]])
                    S.add("act", lambda e, p=p, h=h: e.activation(out=vtm.t[:n, h, :], in_=p.t[:n, 2, :], func=AF.Copy),
                          reads=[p.b], writes=[vtm.bs[h]])
                    S.add("act", lambda e, p=p, h=h: e.activation(out=kdtm.t[:n, h, :], in_=p.t[:n, 0, :], func=AF.Copy),
                          reads=[p.b], writes=[kdtm.bs[h]])
                    S.add("act", lambda e, p=p, h=h: e.activation(out=ontm.t[:n, h, :], in_=p.t[:n, 1, :], func=AF.Copy),
                          reads=[p.b], writes=[ontm.bs[h]])
                    yield
                rsq(nrm.t[:n, 0, :], nrm.t[:n, 0, :], n, nrm.bs[0], 1.0, RMS_EPS)
                rsq(nrm.t[:n, 1, :], nrm.t[:n, 1, :], n, nrm.bs[1], 128.0, 128.0 * RMS_EPS)
                for h in range(H):
                    S.add("act", lambda e, h=h: e.activation(out=ktm.t[:n, h, :], in_=kdtm.t[:n, h, :], func=AF.Copy, scale=nrm.t[:n, 0, h:h + 1]),
                          reads=[kdtm.bs[h], nrm.bs[0]], writes=[ktm.bs[h]])
                    S.add("act", lambda e, h=h: e.activation(out=ontm.t[:n, h, :], in_=ontm.t[:n, h, :], func=AF.Copy, scale=nrm.t[:n, 1, h:h + 1]),
                          reads=[ontm.bs[h], nrm.bs[1]], writes=[ontm.bs[h]])
                    S.add("pool", lambda e, h=h: e.tensor_scalar(out=kdtm.t[:n, h, :], in0=ktm.t[:n, h, :], scalar1=gs[:n, 4, h:h + 1],
                                                                 scalar2=None, op0=ALU.mult),
                          reads=[ktm.bs[h], gb_], writes=[kdtm.bs[h]])
                    p = tpr.get()
                    S.add("pe", lambda e, p=p, h=h: e.transpose(out=p.t[:, 0, 0:n], in_=ktm.t[:n, h, :], identity=identb.t[:n, :n]),
                          reads=[ktm.bs[h], identb.b], writes=[p.b])
                    S.add("pe", lambda e, p=p, h=h: e.transpose(out=p.t[:, 1, 0:n], in_=ontm.t[:n, h, :], identity=identb.t[:n, :n]),
                          reads=[ontm.bs[h], identb.b], writes=[p.b])
                    S.add("act", lambda e, p=p, h=h: e.activation(out=kTn.t[:, h, 0:n], in_=p.t[:, 0, 0:n], func=AF.Copy),
                          reads=[p.b], writes=[kTn.bs[h]])
                    S.add("act", lambda e, p=p, h=h: e.activation(out=qTn.t[:, h, 0:n], in_=p.t[:, 1, 0:n], func=AF.Copy),
                          reads=[p.b], writes=[qTn.bs[h]])
                    yield
                def head_chain(h, gpr, gtmp, gtb):
                    kT = kTn.t[:, h, 0:n]
                    qT = qTn.t[:, h, 0:n]
                    yield
                    pkk, pkkb = gpr.get()
                    S.add("pe", lambda e, pkk=pkk, kT=kT: e.matmul(pkk[:n, :n], lhsT=kT, rhs=kT, start=True, stop=True),
                          reads=[kTn.bs[h]], writes=[pkkb])
                    pqk, pqkb = gpr.get()
                    S.add("pe", lambda e, pqk=pqk, kT=kT, qT=qT: e.matmul(pqk[:n, :n], lhsT=kT, rhs=qT, start=True, stop=True),
                          reads=[kTn.bs[h], qTn.bs[h]], writes=[pqkb])
                    dg = gtmp.get()
                    S.add("act", lambda e, dg=dg, h=h: e.activation(out=dg.t[:n, :n], in_=ident[:n, :n], func=AF.Copy, scale=gs[:n, 1, h:h + 1]),
                          reads=[cm.b, gb_], writes=[dg.b])
                    yield
                    pgb, pgbb = gpr.get()
                    S.add("pe", lambda e, pgb=pgb, dg=dg: e.matmul(pgb[:n, :n], lhsT=ones[:n, :n], rhs=dg.t[:n, :n], start=True, stop=True),
                          reads=[cm.b, dg.b], writes=[pgbb])
                    DT = gtmp.get()
                    S.add("dve", lambda e, DT=DT, pgb=pgb, h=h: e.scalar_tensor_tensor(
                        out=DT.t[:n, :n], in0=pgb[:n, :n], scalar=gs[:n, 1, h:h + 1], in1=negmT[:n, :n], op0=ALU.subtract, op1=ALU.add),
                        reads=[pgbb, gb_, cm.b], writes=[DT.b])
                    S.add("act", lambda e, DT=DT: e.activation(out=DT.t[:n, :n], in_=DT.t[:n, :n], func=AF.Exp),
                          reads=[DT.b], writes=[DT.b])
                    QKm = gtb.get()
                    S.add("dve", lambda e, QKm=QKm, pqk=pqk, DT=DT: e.tensor_tensor(out=QKm.t[:n, :n], in0=pqk[:n, :n], in1=DT.t[:n, :n], op=ALU.mult),
                          reads=[pqkb, DT.b], writes=[QKm.b])
                    DTs = gtmp.get()
                    S.add("pool", lambda e, DTs=DTs, DT=DT: e.tensor_tensor(out=DTs.t[:n, :n], in0=DT.t[:n, :n], in1=strict[:n, :n], op=ALU.mult),
                          reads=[DT.b, cm.b], writes=[DTs.b])
                    cur = gtmp.get()
                    S.add("dve", lambda e, cur=cur, pkk=pkk, DTs=DTs, h=h: e.scalar_tensor_tensor(
                        out=cur.t[:n, :n], in0=pkk[:n, :n], scalar=gs[:n, 6, h:h + 1], in1=DTs.t[:n, :n], op0=ALU.mult, op1=ALU.mult),
                        reads=[pkkb, gb_, DTs.b], writes=[cur.b])
                    yield
                    pt_, ptb = gpr.get()
                    S.add("pe", lambda e, pt_=pt_, cur=cur: e.transpose(out=pt_[:n, :n], in_=cur.t[:n, :n], identity=ident[:n, :n]),
                          reads=[cur.b, cm.b], writes=[ptb])
                    curT = gtmp.get()
                    S.add("act", lambda e, curT=curT, pt_=pt_: e.activation(out=curT.t[:n, :n], in_=pt_[:n, :n], func=AF.Copy),
                          reads=[ptb], writes=[curT.b])
                    P = gtmp.get()
                    S.add("dve", lambda e, P=P, cur=cur: e.tensor_tensor(out=P.t[:n, :n], in0=cur.t[:n, :n], in1=ident[:n, :n], op=ALU.add),
                          reads=[cur.b, cm.b], writes=[P.b])
                    nlev = 6 if n == 128 else 3
                    for k in range(1, nlev + 1):
                        nxt = None
                        if k < nlev:
                            yield
                            pn, pnb = gpr.get()
                            S.add("pe", lambda e, pn=pn, cur=cur, curT=curT: e.matmul(pn[:n, :n], lhsT=curT.t[:n, :n], rhs=cur.t[:n, :n], start=True, stop=True),
                                  reads=[cur.b, curT.b], writes=[pnb])
                            nxt = gtmp.get()
                            S.add("act", lambda e, nxt=nxt, pn=pn: e.activation(out=nxt.t[:n, :n], in_=pn[:n, :n], func=AF.Copy),
                                  reads=[pnb], writes=[nxt.b])
                        pn2, pn2b = gpr.get()
                        S.add("pe", lambda e, pn2=pn2, cur=cur, curT=curT: e.matmul(pn2[:n, :n], lhsT=cur.t[:n, :n], rhs=curT.t[:n, :n], start=True, stop=True),
                              reads=[cur.b, curT.b], writes=[pn2b])
                        nxtT = gtmp.get()
                        S.add("act", lambda e, nxtT=nxtT, pn2=pn2: e.activation(out=nxtT.t[:n, :n], in_=pn2[:n, :n], func=AF.Copy),
                              reads=[pn2b], writes=[nxtT.b])
                        yield
                        pp, ppb = gpr.get()
                        S.add("pe", lambda e, pp=pp, nxtT=nxtT, P=P: e.matmul(pp[:n, :n], lhsT=nxtT.t[:n, :n], rhs=P.t[:n, :n], start=True, stop=True),
                              reads=[nxtT.b, P.b], writes=[ppb])
                        P2 = gtmp.get()
                        S.add("dve", lambda e, P2=P2, P=P, pp=pp: e.tensor_tensor(out=P2.t[:n, :n], in0=pp[:n, :n], in1=P.t[:n, :n], op=ALU.add),
                              reads=[ppb, P.b], writes=[P2.b])
                        P = P2
                        cur, curT = nxt, nxtT
                    yield
                    pks, pksb = gpr.get()
                    S.add("pe", lambda e, pks=pks, kT=kT, h=h: e.matmul(pks[:n, :], lhsT=kT, rhs=Sbf.t[:, h, :], start=True, stop=True),
                          reads=[kTn.bs[h], Sbf.bs[h]], writes=[pksb])
                    U = gtmp.get()
                    S.add("dve", lambda e, U=U, pks=pks, h=h: e.scalar_tensor_tensor(
                        out=U.t[:n, :], in0=pks[:n, :], scalar=gs[:n, 3, h:h + 1], in1=vtm.t[:n, h, :], op0=ALU.mult, op1=ALU.add),
                        reads=[pksb, gb_, vtm.bs[h]], writes=[U.b])
                    yield
                    pw, pwb = gpr.get()
                    S.add("pe", lambda e, pw=pw, P=P, U=U: e.matmul(pw[:n, :], lhsT=P.t[:n, :n], rhs=U.t[:n, :], start=True, stop=True),
                          reads=[P.b, U.b], writes=[pwb])
                    wt = gtb.get()
                    S.add("act", lambda e, wt=wt, pw=pw, h=h: e.activation(out=wt.t[:n, :], in_=pw[:n, :], func=AF.Copy, scale=gs[:n, 5, h:h + 1]),
                          reads=[pwb, gb_], writes=[wt.b])
                    yield
                    po1, po1b = gpr.get()
                    S.add("pe", lambda e, po1=po1, qT=qT, h=h: e.matmul(po1[:n, :], lhsT=qT, rhs=Sbf.t[:, h, :], start=True, stop=True),
                          reads=[qTn.bs[h], Sbf.bs[h]], writes=[po1b])
                    po2, po2b = gpr.get()
                    S.add("pe", lambda e, po2=po2, QKm=QKm, wt=wt: e.matmul(po2[:n, :], lhsT=QKm.t[:n, :n], rhs=wt.t[:n, :], start=True, stop=True),
                          reads=[QKm.b, wt.b], writes=[po2b])
                    t1 = gtmp.get()
                    S.add("act", lambda e, t1=t1, po1=po1, h=h: e.activation(out=t1.t[:n, :], in_=po1[:n, :], func=AF.Copy, scale=gs[:n, 2, h:h + 1]),
                          reads=[po1b, gb_], writes=[t1.b])
                    S.add("dve", lambda e, t1=t1, po2=po2, h=h: e.tensor_tensor(out=otm.t[:n, h, :], in0=po2[:n, :], in1=t1.t[:n, :], op=ALU.add),
                          reads=[po2b, t1.b], writes=[otm.bs[h]])
                    psu, psub = gpr.get()
                    S.add("pe", lambda e, psu=psu, wt=wt, h=h: e.matmul(psu[:, :], lhsT=kdtm.t[:n, h, :], rhs=wt.t[:n, :], start=True, stop=True),
                          reads=[kdtm.bs[h], wt.b], writes=[psub])
                    S.add("dve", lambda e, psu=psu, h=h: e.scalar_tensor_tensor(
                        out=Sst.t[:, h, :], in0=Sst.t[:, h, :], scalar=gs[:, 7, h:h + 1], in1=psu[:, :], op0=ALU.mult, op1=ALU.add),
                        reads=[Sst.bs[h], gb_, psub], writes=[Sst.bs[h]])
                    S.add("act", lambda e, h=h: e.activation(out=Sbf.t[:, h, :], in_=Sst.t[:, h, :], func=AF.Copy),
                          reads=[Sst.bs[h]], writes=[Sbf.bs[h]])
                    j2 = gtmp.get()
                    S.add("act", lambda e, j2=j2, h=h: e.activation(out=j2.t[:n, :], in_=otm.t[:n, h, :], func=AF.Square, accum_out=nrm.t[:n, 2, h:h + 1]),
                          reads=[otm.bs[h]], writes=[j2.b, nrm.bs[2]])
                lanes = [(Rot([(gp_t[i].t[:, j, :], gp_t[i].bs[0]) for j in range(4)]),
                          Rot(gtmp.items[i * 7:(i + 1) * 7]), Rot(gtb.items[i * 2:(i + 1) * 2])) for i in range(2)]
                for h0 in range(0, H, 2):
                    alive = [head_chain(h0 + i, *lanes[i]) for i in range(2)]
                    while alive:
                        for g in list(alive):
                            try:
                                next(g)
                            except StopIteration:
                                alive.remove(g)
                        yield
                rsq(nrm.t[:n, 2, :], nrm.t[:n, 2, :], n, nrm.bs[2], 1.0 / 128.0, RMS_EPS)
                g_ = gz[bi]
                for h in range(H):
                    S.add("dve", lambda e, h=h, g_=g_: e.scalar_tensor_tensor(
                        out=ontm.t[:n, h, :], in0=otm.t[:n, h, :], scalar=nrm.t[:n, 2, h:h + 1], in1=g_.t[:n, h * 128:(h + 1) * 128],
                        op0=ALU.mult, op1=ALU.mult),
                        reads=[otm.bs[h], nrm.bs[2], g_.b], writes=[ontm.bs[h]])
                p = tpr.get()
                for h in range(H):
                    S.add("pe", lambda e, p=p, h=h: e.transpose(out=p.t[:, h, 0:n], in_=ontm.t[:n, h, :], identity=identb.t[:n, :n]),
                          reads=[ontm.bs[h], identb.b], writes=[p.b])
                S.add("act", lambda e, p=p: e.activation(out=mixT.t[:, 0:8, c0:c0 + n], in_=p.t[:, :, 0:n], func=AF.Copy),
                      reads=[p.b], writes=[mixT.b])

            pending_gdn = [(bi, n) for bi, (n, rows, orow) in enumerate(blocks)]

            def s5_gen():
                C = N // 8
                L = C + 1
                u3 = [big.t[:, 24 + ft, 0:N].rearrange("p (c s) -> p c s", s=8) for ft in range(8)]
                for j in range(32):
                    ft = j // 4
                    wv = s5ws[j % 2]
                    S.add("sp", lambda e, wv=wv, j=j: e.dma_start(out=wv.t[:].rearrange("p a c -> p (a c)"), in_=s5w2[j, :, 0:2048]),
                          reads=[swbuf], writes=[wv.b], lane="s5%d" % (j % 2))
                    for c in range(2):
                        ps = mm[2 * c + j // 16]
                        col = (j % 16) * 32
                        for s_ in range(8):
                            S.add("pe", lambda e, ps=ps, col=col, wv=wv, s_=s_, c=c, ft=ft, j=j: e.matmul(
                                ps.t[:, col:col + C], lhsT=wv.t[:, s_ * 2 + c, :], rhs=u3[ft][:, :, s_], start=(s_ == 0), stop=(s_ == 7)),
                                reads=[wv.b, big.bs[24 + ft]], writes=[ps.b])
                    yield
                xa = [xs[0], xs[1]]
                for c in range(2):
                    for hf in range(2):
                        ps = mm[2 * c + hf]
                        S.add("act", lambda e, ps=ps, c=c, hf=hf, X=xa[c]: e.activation(
                            out=X.t[:, hf * 16:hf * 16 + 16, 1:1 + C], in_=ps.t[:, :].rearrange("p (j q) -> p j q", q=32)[:, :, 0:C], func=AF.Copy),
                            reads=[ps.b], writes=[xa[c].b])
                    S.add("pool", lambda e, c=c, X=xa[c]: e.tensor_copy(out=X.t[:, :, 0], in_=xprev.t[:, c, :]),
                          reads=[xprev.b, xa[c].b], writes=[xa[c].b])
                yield
                Xr, Xi = xa
                LR = l8.t[:, 0, 0, :]
                LI = l8.t[:, 0, 1, :]

                def tt(out_ap, a_ap, b_ap, op, rd, wr):
                    S.add("pool", lambda e: e.tensor_tensor(out=out_ap, in0=a_ap, in1=b_ap, op=op), reads=rd, writes=wr)
                for cc in range(1, L):
                    tq = xq[cc % 2]
                    q0, q1, q2, q3 = tq.bs
                    tt(tq.t[:, 0, :], Xr.t[:, :, cc - 1], LR, ALU.mult, [Xr.b, l8.b], [q0])
                    tt(tq.t[:, 1, :], Xi.t[:, :, cc - 1], LI, ALU.mult, [Xi.b, l8.b], [q1])
                    tt(tq.t[:, 2, :], Xi.t[:, :, cc - 1], LR, ALU.mult, [Xi.b, l8.b], [q2])
                    tt(tq.t[:, 3, :], Xr.t[:, :, cc - 1], LI, ALU.mult, [Xr.b, l8.b], [q3])
                    tt(tq.t[:, 0, :], tq.t[:, 0, :], tq.t[:, 1, :], ALU.subtract, [q0, q1], [q0])
                    tt(tq.t[:, 2, :], tq.t[:, 2, :], tq.t[:, 3, :], ALU.add, [q2, q3], [q2])
                    tt(Xr.t[:, :, cc], Xr.t[:, :, cc], tq.t[:, 0, :], ALU.add, [Xr.b, q0], [Xr.b])
                    tt(Xi.t[:, :, cc], Xi.t[:, :, cc], tq.t[:, 2, :], ALU.add, [Xi.b, q2], [Xi.b])
                    if cc % 4 == 0:
                        yield
                for c in range(2):
                    S.add("act", lambda e, c=c, X=xa[c]: e.activation(out=xb.t[:, c, :, 0:C], in_=X.t[:, :, 0:C], func=AF.Copy),
                          reads=[xa[c].b, xb.b], writes=[xb.b])
                    S.add("pool", lambda e, c=c, X=xa[c]: e.tensor_copy(out=xprev.t[:, c, :], in_=X.t[:, :, C]),
                          reads=[xa[c].b, xprev.b], writes=[xprev.b])
                for ft in range(8):
                    kb = s5ks[ft % 2]
                    S.add("sp", lambda e, kb=kb, ft=ft: e.dma_start(out=kb.t[:].rearrange("p a c -> p (a c)"), in_=kscr[ft]),
                          reads=[swbuf], writes=[kb.b], lane="s5k%d" % (ft % 2))
                    yp = mmr.get()
                    y3 = yp.t[:, 0:N].rearrange("p (c s) -> p c s", s=8)
                    for hf in range(4):
                        vb = s5ws[hf % 2]
                        S.add("sp", lambda e, vb=vb, ft=ft, hf=hf: e.dma_start(
                            out=vb.t[:].rearrange("p (q a) c -> p q (a c)", q=4),
                            in_=s5w2[ft * 4:ft * 4 + 4, :, 2048 + hf * 512:2048 + (hf + 1) * 512].rearrange("q p f -> p q f")),
                            reads=[swbuf], writes=[vb.b], lane="s5%d" % (hf % 2))
                        for s_ in range(2 * hf, 2 * hf + 2):
                            for sp_ in range(s_ + 1):
                                S.add("pe", lambda e, y3=y3, kb=kb, s_=s_, sp_=sp_, ft=ft: e.matmul(
                                    y3[:, :, s_], lhsT=kb.t[:, s_ - sp_, :], rhs=u3[ft][:, :, sp_], start=(sp_ == 0), stop=False),
                                    reads=[kb.b, big.bs[24 + ft]], writes=[yp.b])
                            for jj in range(4):
                                for c in range(2):
                                    S.add("pe", lambda e, y3=y3, vb=vb, s_=s_, c=c, jj=jj, ft=ft, hf=hf: e.matmul(
                                        y3[:, :, s_], lhsT=vb.t[:, jj * 4 + (s_ - 2 * hf) * 2 + c, :], rhs=xb.t[:, c, ft * 4 + jj, 0:C], start=False,
                                        stop=(jj == 3 and c == 1)),
                                        reads=[vb.b, xb.b], writes=[yp.b])
                            yield
                    if True:
                        yx = sgt.get()
                        S.add("dve", lambda e, yx=yx, yp=yp, ft=ft: e.scalar_tensor_tensor(
                            out=yx.t[:, 0:N], in0=big.t[:, 24 + ft, 0:N], scalar=d5.t[:, ft:ft + 1], in1=yp.t[:, 0:N], op0=ALU.mult, op1=ALU.add),
                            reads=[big.bs[24 + ft], d5.b, yp.b], writes=[yx.b])
                        sq = sgt.get()
                        S.add("act", lambda e, sq=sq, yx=yx: e.activation(out=sq.t[:, 0:N], in_=yx.t[:, 0:N], func=AF.Square),
                              reads=[yx.b], writes=[sq.b])
                        S.add("dve", lambda e, sq=sq: e.tensor_scalar(out=sq.t[:, 0:N], in0=sq.t[:, 0:N], scalar1=0.044715, scalar2=1.0, op0=ALU.mult, op1=ALU.add),
                              reads=[sq.b], writes=[sq.b])
                        S.add("dve", lambda e, sq=sq, yx=yx: e.tensor_tensor(out=sq.t[:, 0:N], in0=sq.t[:, 0:N], in1=yx.t[:, 0:N], op=ALU.mult),
                              reads=[sq.b, yx.b], writes=[sq.b])
                        S.add("act", lambda e, sq=sq: e.activation(out=sq.t[:, 0:N], in_=sq.t[:, 0:N], func=AF.Tanh, scale=math.sqrt(2.0 / math.pi)),
                              reads=[sq.b], writes=[sq.b])
                        S.add("dve", lambda e, sq=sq: e.tensor_scalar(out=sq.t[:, 0:N], in0=sq.t[:, 0:N], scalar1=0.5, scalar2=0.5, op0=ALU.mult, op1=ALU.add),
                              reads=[sq.b], writes=[sq.b])
                        S.add("dve", lambda e, sq=sq, yx=yx, ft=ft: e.tensor_tensor(out=big.t[:, 32 + ft, 0:N], in0=sq.t[:, 0:N], in1=yx.t[:, 0:N], op=ALU.mult),
                              reads=[sq.b, yx.b], writes=[big.bs[32 + ft]])
            s5g = s5_gen()
            s5_alive = True
            for bi, n in pending_gdn:
                for step, _ in enumerate(gdn_block(bi, n, c0s[bi], abps[bi][0], abps[bi][1])):
                    if s5_alive and step % 4 == 3:
                        try:
                            next(s5g)
                        except StopIteration:
                            s5_alive = False
            if s5_alive:
                for _ in s5g:
                    pass
            for mt in range(8):
                if mt % 4 == 0:
                    w = load_w(c_glu[mt // 4], 4096)
                    wv = w.t[:].rearrange("p (m k c) -> p m k c", m=4, k=8)
                p = mmr.get()
                for kc in range(8):
                    S.add("pe", lambda e, p=p, wv=wv, mt=mt, kc=kc: e.matmul(p.t[:, 0:N], lhsT=wv[:, mt % 4, kc, :], rhs=big.t[:, 32 + kc, 0:N],
                                                                            start=(kc == 0), stop=(kc == 7)),
                          reads=[w.b, big.bs[32 + kc]], writes=[p.b])
                sg = sgt.get()
                S.add("act", lambda e, p=p, sg=sg, mt=mt: e.activation(out=sg.t[:, 0:N], in_=p.t[:, 0:N], func=AF.Tanh, scale=0.5, bias=hbg.t[:, mt:mt + 1]),
                      reads=[p.b, hbg.b], writes=[sg.b])
                S.add("dve", lambda e, sg=sg: e.tensor_scalar(out=sg.t[:, 0:N], in0=sg.t[:, 0:N], scalar1=0.5, scalar2=0.5, op0=ALU.mult, op1=ALU.add),
                      reads=[sg.b], writes=[sg.b])
                S.add("dve", lambda e, sg=sg, mt=mt: e.tensor_tensor(out=mixT.t[:, 8 + mt, 0:N], in0=sg.t[:, 0:N], in1=big.t[:, 32 + mt, 0:N], op=ALU.mult),
                      reads=[sg.b, big.bs[32 + mt]], writes=[mixT.b])
            for cg in range(8):
                w = load_w(c_out[cg], 4096)
                wv = w.t[:].rearrange("p (k c) -> p k c", k=KC)
                for bi, (n, rows, orow) in enumerate(blocks):
                    hb = hbuf[bi]
                    p = mmr.get()
                    for kc in range(KC):
                        S.add("pe", lambda e, p=p, wv=wv, kc=kc, n=n, c0=c0s[bi]: e.matmul(
                            p.t[:n, 0:256], lhsT=mixT.t[:, kc, c0:c0 + n], rhs=wv[:, kc, :], start=(kc == 0), stop=(kc == KC - 1)),
                            reads=[w.b, mixT.b], writes=[p.b])
                    S.add("dve", lambda e, p=p, hb=hb, n=n, cg=cg: e.scalar_tensor_tensor(
                        out=hb.t[:n, cg * 256:(cg + 1) * 256], in0=hb.t[:n, cg * 256:(cg + 1) * 256], scalar=ALPHA, in1=p.t[:n, 0:256],
                        op0=ALU.mult, op1=ALU.add),
                        reads=[hb.b, p.b], writes=[hb.b])
            for bi, (n, rows, orow) in enumerate(blocks):
                layer_norm(hbuf[bi], n, 2)
                to_actT(hbuf[bi], n, c0s[bi], actT)
            for bt_ in range(22):
                if prefetch is not None and bt_ >= 2:
                    try:
                        next(prefetch)
                    except StopIteration:
                        prefetch = None
                wg = load_w(c_gate[bt_], 4096)
                wu = load_w(c_up[bt_], 4096)
                wgv = wg.t[:].rearrange("p (m k c) -> p m k c", m=2, k=KC)
                wuv = wu.t[:].rearrange("p (m k c) -> p m k c", m=2, k=KC)
                for mi in range(2):
                    mt = bt_ * 2 + mi
                    pg = mmr.get()
                    pu = mmr.get()
                    for kc in range(KC):
                        S.add("pe", lambda e, pg=pg, wgv=wgv, mi=mi, kc=kc: e.matmul(
                            pg.t[:, 0:N], lhsT=wgv[:, mi, kc, :], rhs=actT.t[:, kc, 0:N], start=(kc == 0), stop=(kc == KC - 1)),
                            reads=[wg.b, actT.b], writes=[pg.b])
                    for kc in range(KC):
                        S.add("pe", lambda e, pu=pu, wuv=wuv, mi=mi, kc=kc: e.matmul(
                            pu.t[:, 0:N], lhsT=wuv[:, mi, kc, :], rhs=actT.t[:, kc, 0:N], start=(kc == 0), stop=(kc == KC - 1)),
                            reads=[wu.b, actT.b], writes=[pu.b])
                    sg = sgt.get()
                    S.add("act", lambda e, pg=pg, sg=sg: e.activation(out=sg.t[:, 0:N], in_=pg.t[:, 0:N], func=AF.Silu),
                          reads=[pg.b], writes=[sg.b])
                    S.add("dve", lambda e, pu=pu, sg=sg, mt=mt: e.tensor_tensor(out=big.t[:, mt, 0:N], in0=pu.t[:, 0:N], in1=sg.t[:, 0:N], op=ALU.mult),
                          reads=[pu.b, sg.b], writes=[big.bs[mt]])
            if prefetch is not None:
                for _ in prefetch:
                    pass
            for cg in range(8):
                ps_ = [mmr.get() for _ in range(nb)]
                for ch in range(4):
                    w = load_w(c_down[cg, ch], 11 * 256)
                    wv = w.t[:, 0:11 * 256].rearrange("p (k c) -> p k c", k=11)
                    for bi, (n, rows, orow) in enumerate(blocks):
                        for kc in range(11):
                            S.add("pe", lambda e, p=ps_[bi], wv=wv, kc=kc, ch=ch, n=n, c0=c0s[bi]: e.matmul(
                                p.t[:n, 0:256], lhsT=big.t[:, ch * 11 + kc, c0:c0 + n], rhs=wv[:, kc, :],
                                start=(ch == 0 and kc == 0), stop=(ch == 3 and kc == 10)),
                                reads=[w.b, big.bs[ch * 11 + kc]], writes=[ps_[bi].b])
                for bi, (n, rows, orow) in enumerate(blocks):
                    hb = hbuf[bi]
                    S.add("dve", lambda e, p=ps_[bi], hb=hb, n=n, cg=cg: e.scalar_tensor_tensor(
                        out=hb.t[:n, cg * 256:(cg + 1) * 256], in0=hb.t[:n, cg * 256:(cg + 1) * 256], scalar=ALPHA, in1=p.t[:n, 0:256],
                        op0=ALU.mult, op1=ALU.add),
                        reads=[hb.b, ps_[bi].b], writes=[hb.b])
            if has_next:
                for i in range(4):
                    preloaded.append(load_w(c_in_s[i], 4096))
            for bi, (n, rows, orow) in enumerate(blocks):
                hb = hbuf[bi]
                layer_norm(hb, n, 4)
                for (r0, nr, dst) in orow:
                    S.add("sp", lambda e, hb=hb, r0=r0, nr=nr, dst=dst: e.dma_start(out=dst, in_=hb.t[r0:r0 + nr, :]),
                          reads=[hb.b, outbuf], lane="y%d" % ((2 * k + bi) % 4))

        def init_state(sample):
            if sample:
                S.add("sp", lambda e: e.dma_start(out=Sst.t[:], in_=st_delta.rearrange("h k v -> k h v")),
                      writes=Sst.bs, lane="st")
                S.add("sp", lambda e: e.dma_start(out=hist.t[:], in_=st_conv[:, :, :]), writes=[hist.b], lane="st")
                S.add("sp", lambda e: e.dma_start(out=xprev.t[:, 0, :], in_=st_re[:, :]), writes=[xprev.b], lane="st")
                S.add("sp", lambda e: e.dma_start(out=xprev.t[:, 1, :], in_=st_im[:, :]), writes=[xprev.b], lane="st")
            else:
                S.add("pool", lambda e: e.memset(Sst.t[:], 0.0), writes=Sst.bs)
                S.add("pool", lambda e: e.memset(hist.t[:], 0.0), writes=[hist.b])
                S.add("pool", lambda e: e.memset(xprev.t[:], 0.0), writes=[xprev.b])
            for h in range(H):
                S.add("act", lambda e, h=h: e.activation(out=Sbf.t[:, h, :], in_=Sst.t[:, h, :], func=AF.Copy),
                      reads=[Sst.bs[h]], writes=[Sbf.bs[h]])

        def store_state(o_conv, o_delta, o_re, o_im):
            S.add("sp", lambda e: e.dma_start(out=o_delta.rearrange("h k v -> k h v"), in_=Sst.t[:]),
                  reads=Sst.bs + [outbuf], lane="so")
            S.add("sp", lambda e: e.dma_start(out=o_conv[:, :, :], in_=hist.t[:]), reads=[hist.b, outbuf], lane="so")
            S.add("sp", lambda e: e.dma_start(out=o_re[:, :], in_=xprev.t[:, 0, :]), reads=[xprev.b, outbuf], lane="so")
            S.add("sp", lambda e: e.dma_start(out=o_im[:, :], in_=xprev.t[:, 1, :]), reads=[xprev.b, outbuf], lane="so")

        init_state(False)
        bl = []
        for b in range(nblk):
            s0 = 128 * b
            if b == 0:
                rows = [(0, NMETA, meta[:, :]), (NMETA, 128 - NMETA, x_p[0:128 - NMETA, :])]
                orow = [(NMETA, 128 - NMETA, y_p[0:128 - NMETA, :])]
            else:
                rows = [(0, 128, x_p[s0 - NMETA:s0 - NMETA + 128, :])]
                orow = [(0, 128, y_p[s0 - NMETA:s0 - NMETA + 128, :])]
            bl.append((128, rows, orow))
        tiles = [bl[i:i + 2] for i in range(0, nblk, 2)]
        tiles.append([(16, [(0, 16, x_p[seq - 16:seq, :])], [(0, 16, y_p[seq - 16:seq, :])])])
        tiles.append([(16, [(0, 16, x_s[:, :])], [(0, 16, y_s[:, :])])])
        for _ in phase_A(0, tiles[0]):
            pass
        for k, tl in enumerate(tiles):
            if k == len(tiles) - 1:
                store_state(o_conv_p, o_delta_p, o_re_p, o_im_p)
                init_state(True)
            run_tile(k, tl, phase_A(k + 1, tiles[k + 1]) if k + 1 < len(tiles) else None)
        store_state(o_conv_s, o_delta_s, o_re_s, o_im_s)
        S.add("sp", lambda e: e.nop(), writes=[outbuf])
        with nc.allow_non_contiguous_dma(reason="small state layouts"):
            S.emit(nc, es)
    return nc


def _consts():
    i = np.arange(128)
    ident = np.eye(128, dtype=np.float32)
    ones = np.ones((128, 128), np.float32)
    negmT = np.where(i[:, None] <= i[None, :], 0.0, NEG).astype(np.float32)
    strict = (i[:, None] < i[None, :]).astype(np.float32)
    uincl = (i[:, None] <= i[None, :]).astype(np.float32)
    return np.ascontiguousarray(np.stack([ident, ones, negmT, strict, uincl], axis=1).reshape(128, 5 * 128))


def _stat(w, nb):
    K, M = w.shape
    kc = K // 128
    mt = M // 128
    a = w.reshape(kc, 128, mt, 128).transpose(2, 1, 0, 3)
    a = a.reshape(mt // nb, nb, 128, kc, 128).transpose(0, 2, 1, 3, 4)
    return np.ascontiguousarray(a.reshape(mt // nb, 128, nb * kc * 128))


def _mov(w, ncg):
    K, M = w.shape
    kc = K // 128
    a = w.reshape(kc, 128, ncg, 256).transpose(2, 1, 0, 3)
    return np.ascontiguousarray(a.reshape(ncg, 128, kc * 256))


def _prep_shared(inp):
    f = lambda a: np.asarray(a, np.float32)
    w_in = f(inp["w_in"])[0]
    sh = {}
    sh["w_in_s"] = _stat(np.concatenate([w_in[:, 0:3072], w_in[:, 4112:5136]], axis=1), 2)
    sh["w_in_z"] = _mov(w_in[:, 3072:4096], 4)
    sh["w_in_ab"] = np.ascontiguousarray(w_in[:, 4096:4112].reshape(KC, 128, 16).transpose(1, 0, 2).reshape(128, KC * 16))
    sh["cw"] = np.ascontiguousarray(f(inp["conv_w"])[0].reshape(4, 24, 128).transpose(2, 1, 0).reshape(128, 96))
    dnp = np.zeros((3, 128), np.float32)
    dnp[0, :8] = f(inp["dn_a_log"])[0]
    dnp[1, :8] = f(inp["dn_dt_bias"])[0]
    dnp[2, :] = f(inp["dn_norm_w"])[0]
    sh["dnp"] = dnp
    sh["s5a"] = np.ascontiguousarray(np.stack([f(inp["s5_a_re"])[0], f(inp["s5_a_im"])[0],
                                               np.broadcast_to(f(inp["s5_log_dt"])[0][:, None], (64, 64))]))
    bpad = np.zeros((2, 32, 128, 128), np.float32)
    cpad = np.zeros((2, 32, 128, 128), np.float32)
    for c, (bk, ck) in enumerate((("s5_b_re", "s5_c_re"), ("s5_b_im", "s5_c_im"))):
        B = f(inp[bk])[0]
        C = f(inp[ck])[0]
        for g in range(64):
            j, g2, gl = g // 2, g % 2, g % 8
            bpad[c, j, g2 * 64:(g2 + 1) * 64, gl * 16:(gl + 1) * 16] = B[g]
            cpad[c, j, g2 * 64:(g2 + 1) * 64, gl * 16:(gl + 1) * 16] = C[g].T
    sh["bpadT"] = bpad
    sh["cpad"] = cpad
    sh["s5d"] = np.ascontiguousarray(f(inp["s5_d"])[0].reshape(8, 128).T)
    sh["bglu"] = np.ascontiguousarray(f(inp["s5_b_glu"])[0].reshape(8, 128).T)
    wg = f(inp["s5_w_glu"])[0]
    sh["w_glu"] = _stat(wg, 4)
    sh["w_out"] = _mov(f(inp["w_out"])[0], 8)
    sh["w_gate"] = _stat(f(inp["ffn_w_gate"])[0], 2)
    sh["w_up"] = _stat(f(inp["ffn_w_up"])[0], 2)
    wd = f(inp["ffn_w_down"])[0]
    a = wd.reshape(4, 11, 128, 8, 256).transpose(3, 0, 2, 1, 4)
    sh["w_down"] = np.ascontiguousarray(a.reshape(8, 4, 128, 11 * 256))
    sh["cmat"] = _consts()
    sh["meta"] = f(inp["meta_tokens"])
    sh["lnp"] = np.ascontiguousarray(np.stack([f(inp["ln_in_g"]), f(inp["ln_in_b"]), f(inp["ln1_g"])[0], f(inp["ln1_b"])[0],
                                               f(inp["ln2_g"])[0], f(inp["ln2_b"])[0]]))
    return sh


def _s5_in(a):
    return np.ascontiguousarray(a.reshape(32, 2, 64).transpose(1, 2, 0).reshape(128, 32))


def _s5_out(a):
    return np.ascontiguousarray(a.reshape(2, 64, 32).transpose(2, 0, 1).reshape(64, 64))


def _conv_in(a):
    return np.ascontiguousarray(a.reshape(3, 24, 128).transpose(2, 1, 0))


def _conv_out(a):
    return np.ascontiguousarray(a.transpose(2, 1, 0).reshape(3, 3072))


_NC_CACHE = {}


def run(inputs, ncores, nblk):
    f = lambda a: np.asarray(a, np.float32)
    sh = _prep_shared(inputs)
    if nblk not in _NC_CACHE:
        _NC_CACHE[nblk] = build(nblk)
    nc = _NC_CACHE[nblk]
    in_maps = []
    for c in range(ncores):
        m = dict(sh)
        m["x_p"] = np.ascontiguousarray(f(inputs["x_prompt"])[c])
        m["x_s"] = np.ascontiguousarray(f(inputs["x_sample"])[c])
        m["st_conv"] = _conv_in(f(inputs["state_conv_qkv"])[0, c])
        m["st_delta"] = np.ascontiguousarray(f(inputs["state_delta"])[0, c])
        m["st_re"] = _s5_in(f(inputs["state_s5_re"])[0, c])
        m["st_im"] = _s5_in(f(inputs["state_s5_im"])[0, c])
        in_maps.append(m)
    res = run_bass_kernel_spmd(nc, in_maps, core_ids=list(range(ncores)))
    R = res.results
    st = lambda k, fn=lambda a: a: np.stack([fn(np.asarray(R[c][k], np.float32)) for c in range(ncores)])[None]
    return (np.stack([np.asarray(R[c]["y_p"], np.float32) for c in range(ncores)]),
            np.stack([np.asarray(R[c]["y_s"], np.float32) for c in range(ncores)]),
            st("o_conv_p", _conv_out), st("o_delta_p"), st("o_re_p", _s5_out), st("o_im_p", _s5_out),
            st("o_conv_s", _conv_out), st("o_delta_s"), st("o_re_s", _s5_out), st("o_im_s", _s5_out))


def kernel(**inputs):
    return run(inputs, 8, 32)
```
